# Optimizing a Trainium2 kernel written in Bass

```python
import math
import jax
import jax.numpy as jnp
from jax import lax
import numpy as np

D_MODEL = 1024
BATCH = 2
SEQ = 16384
DEPTH = 4

GRID_W = 64
CTX_LEN = 256
BLK = 128
WINDOW = 128
N_BRANCH = 4
BRANCH_W = D_MODEL // 4
D_FF = 4 * D_MODEL
ROPE_BASE = 10000.0
NEG_INF = -1e30
A_HEADS = 4
A_DQK = BRANCH_W // (2 * A_HEADS)
A_DV = BRANCH_W // A_HEADS
B_HEADS = 4
B_D = BRANCH_W // B_HEADS
B_LORA_W = 64
B_LORA_A = 64
B_LORA_G = 128
B_GN_EPS = 64e-5
C_HEADS = 4
C_DK = BRANCH_W // (2 * C_HEADS)
C_DV = BRANCH_W // C_HEADS
D_QHEADS = 4
D_KVHEADS = 2
D_D = BRANCH_W // D_QHEADS

A_COLS = (A_HEADS * 2 * A_DQK, A_HEADS * 2 * A_DQK, A_HEADS * A_DV)
B_COLS = (BRANCH_W, BRANCH_W, BRANCH_W, B_LORA_W, B_LORA_W, B_LORA_A, B_LORA_A, B_LORA_G)
C_COLS = (C_HEADS * C_DK, C_HEADS * C_DK, C_HEADS * C_DV, BRANCH_W)
D_COLS = (D_QHEADS * D_D, D_KVHEADS * D_D, D_KVHEADS * D_D)
GROUP_COLS = (sum(A_COLS), sum(B_COLS), sum(C_COLS), sum(D_COLS), N_BRANCH * D_MODEL)
W_IN_COLS = sum(GROUP_COLS)

kernel_name = 'hybrid_diffusion_parallel_mixer'


def split_cols(x, sizes):
    return jnp.split(x, np.cumsum(sizes)[:-1].tolist(), axis=-1)


def rms_norm(x, g, eps=1e-6):
    xf = x.astype(jnp.float32)
    y = xf * lax.rsqrt(jnp.mean(xf * xf, axis=-1, keepdims=True) + eps)
    return (y * g.astype(jnp.float32)).astype(x.dtype)


def head_norm(x, g, b=None, eps=1e-5, center=True):
    xf = x.astype(jnp.float32)
    if center:
        xf = xf - jnp.mean(xf, axis=-1, keepdims=True)
    y = xf * lax.rsqrt(jnp.mean(xf * xf, axis=-1, keepdims=True) + eps)
    y = y.reshape(x.shape[:-2] + (-1,)) * g.astype(jnp.float32)
    if b is not None:
        y = y + b.astype(jnp.float32)
    return y.astype(x.dtype)


def rope_angles(pos, dim):
    inv_freq = ROPE_BASE ** (-jnp.arange(dim // 2, dtype=jnp.float32) / (dim // 2))
    return pos[:, None] * inv_freq[None, :]


def axial_rope_angles(rows, head_dim):
    row = jnp.repeat(jnp.arange(rows, dtype=jnp.float32), GRID_W)
    col = jnp.tile(jnp.arange(GRID_W, dtype=jnp.float32), rows)
    return rope_angles(row, head_dim // 2), rope_angles(col, head_dim // 2)


def rotate_half(x):
    x1, x2 = jnp.split(x, 2, axis=-1)
    return jnp.concatenate([-x2, x1], axis=-1)


def apply_rope(x, ang):
    full = jnp.concatenate([ang, ang], axis=-1)
    shp = (ang.shape[0],) + (1,) * (x.ndim - 3) + (full.shape[-1],)
    cos = jnp.cos(full).reshape(shp).astype(x.dtype)
    sin = jnp.sin(full).reshape(shp).astype(x.dtype)
    return x * cos + rotate_half(x) * sin


def apply_axial_rope(x, ang_r, ang_c):
    xr, xc = jnp.split(x, 2, axis=-1)
    return jnp.concatenate([apply_rope(xr, ang_r), apply_rope(xc, ang_c)], axis=-1)


def centred_shift(x):
    z = jnp.zeros_like(x[:, :1])
    prev = jnp.concatenate([z, x[:, :-1]], axis=1)
    nxt = jnp.concatenate([x[:, 1:], z], axis=1)
    return 0.5 * (prev + nxt)


def diff_attention(cols_l, cols_x, p, ang_r, ang_c, layer_idx, ctx_out):
    def heads(cols):
        q, k, v = split_cols(cols, A_COLS)
        sh = cols.shape[:2]
        return (q.reshape(sh + (A_HEADS, 2, A_DQK)), k.reshape(sh + (A_HEADS, 2, A_DQK)),
                v.reshape(sh + (A_HEADS, A_DV)))
    ql, kl, vl = heads(cols_l)
    qx, kx, vx = heads(cols_x)
    ql = apply_axial_rope(ql, ang_r, ang_c)
    kl = apply_axial_rope(kl, ang_r, ang_c)
    lam_init = 0.8 - 0.6 * math.exp(-0.3 * layer_idx)
    lq = p['diff_lam_q'].astype(jnp.float32)
    lk = p['diff_lam_k'].astype(jnp.float32)
    lam = jnp.exp(jnp.sum(lq[0] * lk[0])) - jnp.exp(jnp.sum(lq[1] * lk[1])) + lam_init
    scale = A_DQK ** -0.5

    def attend(q, k, v):
        s = jnp.einsum('bqhcd,bkhcd->bhcqk', q, k).astype(jnp.float32) * scale
        pr = jax.nn.softmax(s, axis=-1)
        wgt = (pr[:, :, 0] - lam * pr[:, :, 1]).astype(v.dtype)
        return jnp.einsum('bhqk,bkhd->bqhd', wgt, v)

    B, S = cols_l.shape[:2]
    nb = S // BLK
    k_all = jnp.concatenate([kx, kl], axis=1)
    v_all = jnp.concatenate([vx, vl], axis=1)
    q_blocks = jnp.swapaxes(ql.reshape(B, nb, BLK, A_HEADS, 2, A_DQK), 0, 1)
    o_blocks = lax.map(lambda qb: attend(qb, k_all, v_all), q_blocks)
    o_l = jnp.swapaxes(o_blocks, 0, 1).reshape(B, S, A_HEADS, A_DV)

    def finish(o):
        return head_norm(o, p['diff_subln'], center=False) * (1.0 - lam_init)
    y_l = finish(o_l)
    y_x = finish(attend(qx, kx, vx)) if ctx_out else None
    return y_l, y_x


def rwkv7_scan(r, w, k, v, kk, b, s0, reverse):
    tm = lambda t: jnp.moveaxis(t.astype(jnp.float32), 1, 0)

    def step(S, inp):
        rt, wt, kt, vt, kkt, bt = inp
        sa = jnp.einsum('bhvk,bhk->bhv', S, -kkt)
        S = S * wt[:, :, None, :] + sa[..., None] * bt[:, :, None, :] + vt[..., None] * kt[:, :, None, :]
        return S, jnp.einsum('bhvk,bhk->bhv', S, rt)

    S, y = lax.scan(step, s0, (tm(r), tm(w), tm(k), tm(v), tm(kk), tm(b)), reverse=reverse)
    return S, jnp.moveaxis(y, 0, 1)


def rwkv7_time_mix(cols_l, cols_x, p, ctx_out):
    hd = lambda t: t.reshape(t.shape[:-1] + (B_HEADS, B_D))

    def prep(cols):
        xs = cols + (centred_shift(cols) - cols) * p['rwkv_mu']
        r, k, v, wl_f, wl_b, al_f, al_b, gl = split_cols(xs, B_COLS)
        kk = hd(k * p['rwkv_kk'])
        kk = kk * lax.rsqrt(jnp.maximum(jnp.sum(kk * kk, axis=-1, keepdims=True), 1e-12))
        dirs = []
        for d, (wl, al) in enumerate(((wl_f, al_f), (wl_b, al_b))):
            w = -jax.nn.softplus(-(p['rwkv_w0'][d] + jnp.tanh(wl) @ p['rwkv_w2'][d])) - 0.5
            a = jax.nn.sigmoid(p['rwkv_a0'][d] + al @ p['rwkv_a2'][d])
            kd = k * (1.0 + (a - 1.0) * p['rwkv_ka'])
            dirs.append((hd(jnp.exp(-jnp.exp(w))), hd(kd), kk * hd(a)))
        return hd(r), hd(v), kk, gl, dirs

    rl, vl, kkl, gll, dirs_l = prep(cols_l)
    rx, vx, kkx, glx, dirs_x = prep(cols_x)
    s0 = jnp.zeros((cols_l.shape[0], B_HEADS, B_D, B_D), jnp.float32)
    outs_l, outs_x = [], []
    for d, rev in enumerate((False, True)):
        dec_x, kd_x, b_x = dirs_x[d]
        dec_l, kd_l, b_l = dirs_l[d]
        s_x, y_x = rwkv7_scan(rx, dec_x, kd_x, vx, kkx, b_x, s0, rev)
        _, y_l = rwkv7_scan(rl, dec_l, kd_l, vl, kkl, b_l, s_x, rev)
        outs_l.append((y_l, kd_l))
        outs_x.append((y_x, kd_x))

    def finish(outs, r, v, gl):
        y = head_norm(outs[0][0] + outs[1][0], p['rwkv_lnx_g'], p['rwkv_lnx_b'], eps=B_GN_EPS).astype(v.dtype)
        bonus = sum(jnp.sum(r * kd * p['rwkv_rk'], axis=-1, keepdims=True) * v for _, kd in outs)
        gate = jax.nn.sigmoid(gl) @ p['rwkv_g2']
        return (y + bonus.reshape(y.shape)) * gate

    y_l = finish(outs_l, rl, vl, gll)
    y_x = finish(outs_x, rx, vx, glx) if ctx_out else None
    return y_l, y_x


def retention_chunkwise(q, k, v, log_g, s0):
    B, T, H, _ = q.shape
    n = T // BLK
    chunks = lambda t: jnp.swapaxes(t.astype(jnp.float32).reshape(B, n, BLK, H, t.shape[-1]), 0, 1)
    idx = jnp.arange(BLK, dtype=jnp.float32)
    rel = idx[:, None] - idx[None, :]
    dmat = jnp.where(rel[None] >= 0, jnp.exp(jnp.maximum(rel, 0.0)[None] * log_g[:, None, None]), 0.0)
    q_dec = jnp.exp((idx[:, None] + 1.0) * log_g[None, :])
    k_dec = jnp.exp((BLK - 1.0 - idx[:, None]) * log_g[None, :])
    c_dec = jnp.exp(BLK * log_g)

    def step(S, blk):
        qc, kc, vc = blk
        sc = jnp.einsum('bihd,bjhd->bhij', qc, kc) * dmat[None]
        o = jnp.einsum('bhij,bjhe->bihe', sc, vc) + jnp.einsum('bihd,bhde->bihe', qc * q_dec[None, :, :, None], S)
        S = S * c_dec[None, :, None, None] + jnp.einsum('bjhd,bjhe->bhde', kc * k_dec[None, :, :, None], vc)
        return S, o

    S, o = lax.scan(step, s0, (chunks(q), chunks(k), chunks(v)))
    return S, jnp.swapaxes(o, 0, 1).reshape(B, T, H, v.shape[-1])


def retention(cols_l, cols_x, p, ang, ctx_out):
    def heads(cols):
        q, k, v, g = split_cols(cols, C_COLS)
        sh = cols.shape[:2]
        return (q.reshape(sh + (C_HEADS, C_DK)), k.reshape(sh + (C_HEADS, C_DK)) * (C_DK ** -0.5),
                v.reshape(sh + (C_HEADS, C_DV)), g)
    ql, kl, vl, gl = heads(cols_l)
    qx, kx, vx, gx = heads(cols_x)
    ql = apply_rope(ql, ang)
    kl = apply_rope(kl, ang)
    log_g = jax.nn.log_sigmoid(p['ret_decay'].astype(jnp.float32))
    s0 = jnp.zeros((cols_l.shape[0], C_HEADS, C_DK, C_DV), jnp.float32)
    flip = lambda t: t[:, ::-1]
    s_xf, o_xf = retention_chunkwise(qx, kx, vx, log_g[0], s0)
    _, o_lf = retention_chunkwise(ql, kl, vl, log_g[0], s_xf)
    s_xb, o_xb = retention_chunkwise(flip(qx), flip(kx), flip(vx), log_g[1], s0)
    _, o_lb = retention_chunkwise(flip(ql), flip(kl), flip(vl), log_g[1], s_xb)

    def finish(o, g):
        return head_norm(o, p['ret_gn']).astype(g.dtype) * jax.nn.silu(g)
    y_l = finish(o_lf + flip(o_lb), gl)
    y_x = finish(o_xf + flip(o_xb), gx) if ctx_out else None
    return y_l, y_x


def window_gqa(cols_l, cols_x, p, ang_r, ang_c, ctx_out):
    G = D_QHEADS // D_KVHEADS

    def heads(cols):
        q, k, v = split_cols(cols, D_COLS)
        sh = cols.shape[:2]
        return (q.reshape(sh + (D_QHEADS, D_D)), k.reshape(sh + (D_KVHEADS, D_D)),
                v.reshape(sh + (D_KVHEADS, D_D)))
    ql, kl, vl = heads(cols_l)
    qx, kx, vx = heads(cols_x)
    ql = apply_axial_rope(ql, ang_r, ang_c)
    kl = apply_axial_rope(kl, ang_r, ang_c)
    scale = D_D ** -0.5
    sink = p['win_sink'].astype(jnp.float32).reshape(D_KVHEADS, G)
    B, S = cols_l.shape[:2]
    C = cols_x.shape[1]
    nb = S // BLK
    qb = ql.reshape(B, nb, BLK, D_KVHEADS, G, D_D)

    def band(t):
        tp = jnp.pad(t, ((0, 0), (BLK, BLK), (0, 0), (0, 0))).reshape(B, nb + 2, BLK, D_KVHEADS, D_D)
        return jnp.concatenate([tp[:, :-2], tp[:, 1:-1], tp[:, 2:]], axis=2)
    kb, vb = band(kl), band(vl)
    start = jnp.arange(nb)[:, None] * BLK
    qpos = start + jnp.arange(BLK)[None, :]
    kpos = start - BLK + jnp.arange(3 * BLK)[None, :]
    valid = ((jnp.abs(qpos[:, :, None] - kpos[:, None, :]) <= WINDOW)
             & (kpos[:, None, :] >= 0) & (kpos[:, None, :] < S))
    s_band = jnp.einsum('bnqhgd,bnkhd->bnhgqk', qb, kb).astype(jnp.float32) * scale
    s_band = jnp.where(valid[None, :, None, None], s_band, NEG_INF)
    s_ctx = jnp.einsum('bnqhgd,bchd->bnhgqc', qb, kx).astype(jnp.float32) * scale
    s_sink = jnp.broadcast_to(sink[None, None, :, :, None, None], s_ctx.shape[:-1] + (1,))
    pr = jax.nn.softmax(jnp.concatenate([s_ctx, s_band, s_sink], axis=-1), axis=-1)
    p_ctx = pr[..., :C].astype(vl.dtype)
    p_band = pr[..., C:C + 3 * BLK].astype(vl.dtype)
    o = (jnp.einsum('bnhgqc,bchd->bnqhgd', p_ctx, vx)
         + jnp.einsum('bnhgqk,bnkhd->bnqhgd', p_band, vb))
    y_l = o.reshape(B, S, D_QHEADS * D_D)
    if ctx_out:
        s = jnp.einsum('bqhgd,bkhd->bhgqk', qx.reshape(B, C, D_KVHEADS, G, D_D), kx).astype(jnp.float32) * scale
        s_sink_x = jnp.broadcast_to(sink[None, :, :, None, None], s.shape[:-1] + (1,))
        px = jax.nn.softmax(jnp.concatenate([s, s_sink_x], axis=-1), axis=-1)[..., :C].astype(vx.dtype)
        y_x = jnp.einsum('bhgqk,bkhd->bqhgd', px, vx).reshape(B, C, D_QHEADS * D_D)
    else:
        y_x = None
    return y_l, y_x


def merge_branches(ys, gates, p):
    m = sum(jax.nn.sigmoid(gates[..., n * D_MODEL:(n + 1) * D_MODEL]) * (y @ p['w_branch'][n])
            for n, y in enumerate(ys))
    return m @ p['w_out']


def token_mixer(h_lat, h_ctx, p, rope, layer_idx, ctx_out):
    a_r, a_c, d_r, d_c, ret_ang = rope
    a_l, b_l, c_l, d_l, gate_l = split_cols(h_lat @ p['w_in'], GROUP_COLS)
    a_x, b_x, c_x, d_x, gate_x = split_cols(h_ctx @ p['w_in'], GROUP_COLS)
    ya_l, ya_x = diff_attention(a_l, a_x, p, a_r, a_c, layer_idx, ctx_out)
    yb_l, yb_x = rwkv7_time_mix(b_l, b_x, p, ctx_out)
    yc_l, yc_x = retention(c_l, c_x, p, ret_ang, ctx_out)
    yd_l, yd_x = window_gqa(d_l, d_x, p, d_r, d_c, ctx_out)
    out_l = merge_branches((ya_l, yb_l, yc_l, yd_l), gate_l, p)
    out_x = merge_branches((ya_x, yb_x, yc_x, yd_x), gate_x, p) if ctx_out else None
    return out_l, out_x


def sq_relu_mlp(h, w_up, w_down):
    return jnp.square(jax.nn.relu(h @ w_up)) @ w_down


def setup_inputs(seed: int = 0) -> dict:
    key = jax.random.key(seed)
    keys = iter(jax.random.split(key, 48))
    nrm = lambda shape, s: s * jax.random.normal(next(keys), shape, jnp.float32)
    gain = lambda shape: 1.0 + nrm(shape, 0.02)
    L, D = DEPTH, D_MODEL
    decay_speed = jnp.asarray(-6.0 + 5.0 * np.linspace(0.0, 1.0, BRANCH_W) ** 0.85, jnp.float32)
    ret_base = jnp.asarray(np.log(2.0 ** (5.0 + np.arange(C_HEADS)) - 1.0), jnp.float32)
    return {
        'x': nrm((BATCH, SEQ, D), 1.0),
        'c': nrm((BATCH, D), 1.0),
        'ctx': nrm((BATCH, CTX_LEN, D), 1.0),
        'c_ctx': nrm((D,), 1.0),
        'ada_w': nrm((L, D, 6 * D), 0.5 * D ** -0.5),
        'ada_b': nrm((L, 6 * D), 0.02),
        'norm_pre_mix': gain((L, D)),
        'norm_post_mix': gain((L, D)),
        'norm_pre_mlp': gain((L, D)),
        'norm_post_mlp': gain((L, D)),
        'w_in': nrm((L, D, W_IN_COLS), D ** -0.5),
        'diff_lam_q': nrm((L, 2, A_DQK), 0.1),
        'diff_lam_k': nrm((L, 2, A_DQK), 0.1),
        'diff_subln': gain((L, A_HEADS * A_DV)),
        'rwkv_mu': jax.random.uniform(next(keys), (L, sum(B_COLS)), jnp.float32),
        'rwkv_w0': decay_speed[None, None, :] + nrm((L, 2, BRANCH_W), 0.1),
        'rwkv_w2': nrm((L, 2, B_LORA_W, BRANCH_W), 0.1),
        'rwkv_a0': nrm((L, 2, BRANCH_W), 0.1),
        'rwkv_a2': nrm((L, 2, B_LORA_A, BRANCH_W), 0.1),
        'rwkv_g2': nrm((L, B_LORA_G, BRANCH_W), B_LORA_G ** -0.5),
        'rwkv_kk': 0.85 + nrm((L, BRANCH_W), 0.02),
        'rwkv_ka': 1.0 + nrm((L, BRANCH_W), 0.02),
        'rwkv_rk': nrm((L, B_HEADS, B_D), 0.1),
        'rwkv_lnx_g': gain((L, BRANCH_W)),
        'rwkv_lnx_b': nrm((L, BRANCH_W), 0.02),
        'ret_decay': ret_base[None, None, :] + nrm((L, 2, C_HEADS), 0.05),
        'ret_gn': gain((L, C_HEADS * C_DV)),
        'win_sink': nrm((L, D_QHEADS), 0.5),
        'w_branch': nrm((L, N_BRANCH, BRANCH_W, D), BRANCH_W ** -0.5),
        'w_out': nrm((L, D, D), D ** -0.5),
        'w_up': nrm((L, D, D_FF), D ** -0.5),
        'w_down': nrm((L, D_FF, D), D_FF ** -0.5),
    }


def reference(x, c, ctx, c_ctx, ada_w, ada_b, norm_pre_mix, norm_post_mix, norm_pre_mlp, norm_post_mlp,
              w_in, diff_lam_q, diff_lam_k, diff_subln, rwkv_mu, rwkv_w0, rwkv_w2, rwkv_a0, rwkv_a2, rwkv_g2,
              rwkv_kk, rwkv_ka, rwkv_rk, rwkv_lnx_g, rwkv_lnx_b, ret_decay, ret_gn, win_sink,
              w_branch, w_out, w_up, w_down):
    n_lat = x.shape[1]
    rows = n_lat // GRID_W
    a_r, a_c = axial_rope_angles(rows, A_DQK)
    d_r, d_c = axial_rope_angles(rows, D_D)
    ret_ang = rope_angles(jnp.arange(n_lat, dtype=jnp.float32), C_DK)
    rope = (a_r, a_c, d_r, d_c, ret_ang)
    params = dict(w_in=w_in, diff_lam_q=diff_lam_q, diff_lam_k=diff_lam_k, diff_subln=diff_subln,
                  rwkv_mu=rwkv_mu, rwkv_w0=rwkv_w0, rwkv_w2=rwkv_w2, rwkv_a0=rwkv_a0, rwkv_a2=rwkv_a2,
                  rwkv_g2=rwkv_g2, rwkv_kk=rwkv_kk, rwkv_ka=rwkv_ka, rwkv_rk=rwkv_rk,
                  rwkv_lnx_g=rwkv_lnx_g, rwkv_lnx_b=rwkv_lnx_b, ret_decay=ret_decay, ret_gn=ret_gn,
                  win_sink=win_sink, w_branch=w_branch, w_out=w_out)
    cx = ctx
    for l in range(DEPTH):
        p = {name: arr[l] for name, arr in params.items()}
        ctx_out = l < DEPTH - 1
        mod_l = (jax.nn.silu(c) @ ada_w[l] + ada_b[l])[:, None, :]
        mod_x = (jax.nn.silu(c_ctx) @ ada_w[l] + ada_b[l])[None, None, :]
        sh1, sc1, gt1, sh2, sc2, gt2 = jnp.split(mod_l, 6, axis=-1)
        xsh1, xsc1, xgt1, xsh2, xsc2, xgt2 = jnp.split(mod_x, 6, axis=-1)
        h_l = rms_norm(x, norm_pre_mix[l]) * (1.0 + sc1) + sh1
        h_x = rms_norm(cx, norm_pre_mix[l]) * (1.0 + xsc1) + xsh1
        m_l, m_x = token_mixer(h_l, h_x, p, rope, l, ctx_out)
        x = x + gt1 * rms_norm(m_l, norm_post_mix[l])
        h_l = rms_norm(x, norm_pre_mlp[l]) * (1.0 + sc2) + sh2
        x = x + gt2 * rms_norm(sq_relu_mlp(h_l, w_up[l], w_down[l]), norm_post_mlp[l])
        if ctx_out:
            cx = cx + xgt1 * rms_norm(m_x, norm_post_mix[l])
            h_x = rms_norm(cx, norm_pre_mlp[l]) * (1.0 + xsc2) + xsh2
            cx = cx + xgt2 * rms_norm(sq_relu_mlp(h_x, w_up[l], w_down[l]), norm_post_mlp[l])
    return x
```

```python
import numpy as np
from contextlib import ExitStack
import concourse.bass as bass
import concourse.mybir as mybir
from concourse.bass_utils import run_bass_kernel_spmd

F32 = mybir.dt.float32
BF16 = mybir.dt.bfloat16
AF = mybir.ActivationFunctionType
ALU = mybir.AluOpType
AX = mybir.AxisListType


class Prog:
    ENGS = ['tensor', 'vector', 'scalar', 'gpsimd', 'sync']
    NDMA = 6

    def __init__(self, same_engine_sync=True):
        self.nc = bass.Bass("TRN2", target_bir_lowering=False)
        self.stack = ExitStack()
        self.ops = []
        self.last_w = {}
        self.readers = {}
        self.same_engine_sync = same_engine_sync
        self.n_un = 0

    def dram(self, name, shape, dt, kind):
        return self.nc.dram_tensor(name, list(shape), dt, kind=kind).ap()

    def sb(self, name, shape, dt=F32):
        return self.stack.enter_context(self.nc.sbuf_tensor('sb_' + name, list(shape), dt))

    def ps(self, name, shape, dt=F32):
        return self.stack.enter_context(self.nc.psum_tensor('ps_' + name, list(shape), dt))

    @staticmethod
    def _key(x):
        if isinstance(x, str):
            return (x, None)
        if isinstance(x, tuple):
            return (Prog._key(x[0])[0], str(x[1]))
        return (x.tensor.name if hasattr(x, 'tensor') else x.name, None)

    @staticmethod
    def _conf(table, base, sub):
        d = table.get(base)
        if not d:
            return []
        if sub is None:
            return list(d.values())
        return [v for s_, v in d.items() if s_ is None or s_ == sub]

    def op(self, eng, fn, r=(), w=(), dma=False, out_dram=False):
        idx = len(self.ops)
        deps = set()
        rk = [self._key(k) for k in r]
        wk = [self._key(k) for k in w]
        for (b, s_) in rk:
            deps.update(self._conf(self.last_w, b, s_))
        for (b, s_) in wk:
            deps.update(self._conf(self.last_w, b, s_))
            for lst in self._conf(self.readers, b, s_):
                deps.update(lst)
        for (b, s_) in rk:
            self.readers.setdefault(b, {}).setdefault(s_, []).append(idx)
        for (b, s_) in wk:
            if s_ is None:
                self.last_w[b] = {None: idx}
                self.readers[b] = {}
            else:
                self.last_w.setdefault(b, {})[s_] = idx
                rd = self.readers.setdefault(b, {})
                rd[s_] = []
        deps.discard(idx)
        self.ops.append(dict(eng=eng, fn=fn, deps=deps, dma=dma, signal=False, out_dram=out_dram))
        return idx

    def mm(self, out, lhsT, rhs, start=True, stop=True, r=None, w=None):
        self.op('tensor', lambda e: e.matmul(out, lhsT, rhs, start=start, stop=stop),
                r if r is not None else [lhsT, rhs], w if w is not None else [out])

    def act(self, out, in_, func, scale=1.0, bias=0.0, accum=None, r=None, w=None, extra_r=()):
        rr = list(r) if r is not None else [in_]
        for x in (scale, bias):
            if not isinstance(x, (int, float)):
                rr.append(x)
        rr += list(extra_r)
        ww = list(w) if w is not None else [out]
        if accum is not None:
            ww.append(accum)
        self.op('scalar', lambda e: e.activation(out, in_, func, bias=bias, scale=scale, accum_out=accum), rr, ww)

    def tt(self, out, in0, in1, op, eng='vector', r=None, w=None):
        self.op(eng, lambda e: e.tensor_tensor(out, in0, in1, op),
                r if r is not None else [in0, in1], w if w is not None else [out])

    def ts(self, out, in0, s1, s2, op0, op1=None, eng='vector', accum=None, r=None, w=None):
        rr = list(r) if r is not None else [in0]
        for x in (s1, s2):
            if x is not None and not isinstance(x, (int, float)):
                rr.append(x)
        ww = list(w) if w is not None else [out]
        if accum is not None:
            ww.append(accum)
        if op1 is None:
            self.op(eng, lambda e: e.tensor_scalar(out, in0, s1, None, op0), rr, ww)
        else:
            self.op(eng, lambda e: e.tensor_scalar(out, in0, s1, s2, op0, op1, accum_out=accum), rr, ww)

    def stt(self, out, in0, scalar, in1, op0, op1, accum=None, r=None, w=None):
        rr = list(r) if r is not None else [in0, in1]
        if not isinstance(scalar, (int, float)):
            rr.append(scalar)
        ww = list(w) if w is not None else [out]
        if accum is not None:
            ww.append(accum)
        self.op('vector', lambda e: e.scalar_tensor_tensor(out, in0, scalar, in1, op0, op1, accum_out=accum), rr, ww)

    def ttr(self, out, in0, in1, op0, op1, accum, r=None, w=None):
        rr = r if r is not None else [in0, in1]
        ww = w if w is not None else [out, accum]
        self.op('vector', lambda e: e.tensor_tensor(out, in0, in1, op0), rr, ww)
        self.op('vector', lambda e: e.tensor_reduce(accum, out, AX.X, op1), list(ww), list(ww))

    def copy(self, out, in_, eng='vector', r=None, w=None):
        if eng == 'scalar':
            self.op('scalar', lambda e: e.copy(out, in_), r if r is not None else [in_], w if w is not None else [out])
        else:
            self.op(eng, lambda e: e.tensor_copy(out, in_), r if r is not None else [in_], w if w is not None else [out])

    def memset(self, ap, val, eng='vector', w=None):
        self.op(eng, lambda e: e.memset(ap, val), [], w if w is not None else [ap])

    def recip(self, out, in_, r=None, w=None):
        self.op('vector', lambda e: e.reciprocal(out, in_), r if r is not None else [in_], w if w is not None else [out])

    def reduce(self, out, in_, op=None, r=None, w=None):
        op = op or ALU.add
        self.op('vector', lambda e: e.tensor_reduce(out, in_, AX.X, op), r if r is not None else [in_],
                w if w is not None else [out])

    def dma(self, out, in_, q='sync', r=None, w=None, out_dram=False, **kw):
        self.op(q, lambda e: e.dma_start(out=out, in_=in_, **kw), r if r is not None else [in_],
                w if w is not None else [out], dma=True, out_dram=out_dram)

    def barrier(self, scratch):
        idx = len(self.ops)
        deps = set(range(idx))
        self.ops.append(dict(eng='vector', fn=lambda e: e.memset(scratch, 0.0), deps=deps, dma=False,
                             signal=False, out_dram=False, barrier=True))
        for b in set(list(self.last_w.keys()) + list(self.readers.keys())):
            self.last_w[b] = {None: idx}
            self.readers[b] = {}

    def emit(self):
        nc = self.nc
        ops = self.ops
        for i, o in enumerate(ops):
            for j in o['deps']:
                pj = ops[j]
                if pj['dma']:
                    continue
                if pj['eng'] == o['eng'] and (o['eng'] == 'tensor' or not self.same_engine_sync):
                    continue
                pj['signal'] = True
        cnt = {e: 0 for e in self.ENGS}
        for o in ops:
            if o['dma']:
                continue
            if o['signal']:
                cnt[o['eng']] += 1
            o['cnt'] = cnt[o['eng']]
        dma_seq = {e: 0 for e in self.ENGS}
        dma_tot = {}
        prev_on_sem = {}
        for i, o in enumerate(ops):
            if not o['dma']:
                continue
            q = o['eng']
            s = (q, dma_seq[q] % self.NDMA)
            dma_seq[q] += 1
            dma_tot[s] = dma_tot.get(s, 0) + 16
            o['dsem'] = s
            o['dval'] = dma_tot[s]
            o['dprev'] = prev_on_sem.get(s)
            prev_on_sem[s] = i
        with ExitStack() as st:
            esem = {e: st.enter_context(nc.semaphore('s_' + e)) for e in self.ENGS}
            dsem = {}
            for q in self.ENGS:
                if dma_seq[q] > 0:
                    for k in range(self.NDMA):
                        dsem[(q, k)] = st.enter_context(nc.semaphore('d_%s_%d' % (q, k)))
            block = st.enter_context(nc.Block())

            def run_engine(ename, e):
                waited = {}
                for i, o in enumerate(ops):
                    if o['eng'] != ename:
                        continue
                    need = {}
                    for j in o['deps']:
                        pj = ops[j]
                        if pj['dma']:
                            key = ('d', pj['dsem'])
                            val = pj['dval']
                        else:
                            if pj['eng'] == ename and (ename == 'tensor' or not self.same_engine_sync):
                                continue
                            key = ('e', pj['eng'])
                            val = pj['cnt']
                        if val > need.get(key, 0):
                            need[key] = val
                    if o['dma'] and o['dprev'] is not None:
                        key = ('d', o['dsem'])
                        val = ops[o['dprev']]['dval']
                        if val > need.get(key, 0):
                            need[key] = val
                    for key, val in need.items():
                        if waited.get(key, 0) >= val:
                            continue
                        waited[key] = val
                        sem = dsem[key[1]] if key[0] == 'd' else esem[key[1]]
                        e.wait_ge(sem, val)
                    ins = o['fn'](e)
                    if o['dma']:
                        ins.then_inc(dsem[o['dsem']], 16)
                    elif o['signal']:
                        ins.then_inc(esem[ename], 1)
                for s, tot in dma_tot.items():
                    if s[0] == ename and waited.get(('d', s), 0) < tot:
                        e.wait_ge(dsem[s], tot)

            @block.tensor
            def _(e):
                run_engine('tensor', e)

            @block.vector
            def _(e):
                run_engine('vector', e)

            @block.scalar
            def _(e):
                run_engine('scalar', e)

            @block.gpsimd
            def _(e):
                run_engine('gpsimd', e)

            @block.sync
            def _(e):
                run_engine('sync', e)
        self.stack.close()
        return nc


D = 1024
KC = 8


def load_w_bf16(P, name, dram_ap, K, N, q='gpsimd'):
    kc = K // 128
    t = P.sb(name, [128, kc, N], BF16)
    src = dram_ap.rearrange('(k p) n -> p k n', p=128)
    for k in range(kc):
        for n0 in range(0, N, 2048):
            n1 = min(N, n0 + 2048)
            P.dma(t[:, k, n0:n1], src[:, k, n0:n1], q='gpsimd', w=[(t, k)])
    return t


def mod_setup(P, adaw_d, adab_d, cT_d, nch, pfx):
    aw = P.sb(pfx + 'aw', [128, KC, nch * D], F32)
    src = adaw_d.rearrange('(k p) n -> p k n', p=128)
    for k in range(KC):
        P.dma(aw[:, k, :], src[:, k, :], q='sync', w=[(aw, k)])
    ab = P.sb(pfx + 'ab', [128, nch * 8], F32)
    P.dma(ab[:], adab_d, q='sync')
    cT = P.sb(pfx + 'cT', [128, KC, 2], F32)
    P.dma(cT[:], cT_d, q='sync')
    sc = P.sb(pfx + 'sc', [128, KC, 2], F32)
    P.act(sc[:], cT[:], AF.Silu)
    modT = P.sb(pfx + 'modT', [128, nch * 8, 2], F32)
    pm = P.ps(pfx + 'pm', [128, 2], F32)
    for j in range(nch * 8):
        for k in range(KC):
            P.mm(pm[:], aw[:, k, j * 128:(j + 1) * 128], sc[:, k, :], start=(k == 0), stop=(k == KC - 1),
                 r=[(aw, k), sc], w=[pm])
        P.ts(modT[:, j, :], pm[:], ab[:, j:j + 1], None, ALU.add, r=[pm, ab], w=[(modT, j)])
    return modT


def norm_to_hT(P, xt, hT_out, scaleT, shiftT, col, ident, bufs, i):
    junk, ss, xn, pT = bufs
    P.act(junk[:], xt, AF.Square, accum=ss[:, 0:1], w=[junk])
    P.act(ss[:, 1:2], ss[:, 0:1], AF.Sqrt, scale=1.0 / D, bias=1e-6, r=[ss], w=[ss])
    P.recip(ss[:, 2:3], ss[:, 1:2], r=[ss], w=[ss])
    P.ts(xn[:], xt, ss[:, 2:3], None, ALU.mult, r=[xt, ss], w=[xn])
    for k in range(KC):
        pt = pT[k % 2]
        P.op('tensor', lambda e, pt=pt, k=k: e.transpose(pt[:], xn[:, k * 128:(k + 1) * 128], ident[:]),
             [xn, ident], [pt])
        P.ts(hT_out[:, k, :], pt[:], scaleT[:, k, col:col + 1], shiftT[:, k, col:col + 1], ALU.mult, ALU.add,
             r=[pt, scaleT, shiftT], w=[hT_out])


def build_T0(NT):
    P = Prog()
    x_d = P.dram('x', [NT * 128, D], F32, 'ExternalInput')
    adaw_d = P.dram('adaw', [D, 2 * D], F32, 'ExternalInput')
    adab_d = P.dram('adab', [128, 16], F32, 'ExternalInput')
    cT_d = P.dram('cT', [128, KC, 2], F32, 'ExternalInput')
    g_d = P.dram('gT', [128, KC], F32, 'ExternalInput')
    id_d = P.dram('ident', [128, 128], BF16, 'ExternalInput')
    hT_d = P.dram('hT', [D, NT * 128], BF16, 'ExternalOutput')
    emit_T0(P, x_d, adaw_d, adab_d, cT_d, g_d, id_d, hT_d, NT)
    return P.emit()


def emit_T0(P, x_d, adaw_d, adab_d, cT_d, g_d, id_d, hT_d, NT, pfx='t0', x_sb_tiles=None):
    ident = P.sb(pfx + 'ident', [128, 128], BF16)
    P.dma(ident[:], id_d, q='sync')
    gT = P.sb(pfx + 'gT', [128, KC], F32)
    P.dma(gT[:], g_d, q='sync')
    modT = mod_setup(P, adaw_d, adab_d, cT_d, 2, pfx)
    scaleT = P.sb(pfx + 'scaleT', [128, KC, 2], F32)
    shiftT = P.sb(pfx + 'shiftT', [128, KC, 2], F32)
    for c in range(2):
        P.stt(scaleT[:, :, c], modT[:, 8:16, c], 1.0, gT[:], ALU.add, ALU.mult, r=[modT, gT], w=[scaleT])
        P.copy(shiftT[:, :, c], modT[:, 0:8, c], r=[modT], w=[shiftT])
    bufs = (P.sb(pfx + 'junk', [128, D], F32), P.sb(pfx + 'ss', [128, 4], F32), P.sb(pfx + 'xn', [128, D], BF16),
            [P.ps(pfx + 'pT%d' % i, [128, 128], BF16) for i in range(2)])
    xts = [P.sb(pfx + 'x%d' % i, [128, D], F32) for i in range(2)]
    hts = [P.sb(pfx + 'h%d' % i, [128, KC, 128], BF16) for i in range(2)]
    hT_v = hT_d.rearrange('(k p) t -> p k t', p=128)
    for i in range(NT):
        if x_sb_tiles is None:
            xt = xts[i % 2]
            P.dma(xt[:], x_d[i * 128:(i + 1) * 128, :], q='sync')
            xa = xt[:]
        else:
            xa = x_sb_tiles(i)
        ht = hts[i % 2]
        norm_to_hT(P, xa, ht, scaleT, shiftT, 1 if i == NT - 1 else 0, ident, bufs, i)
        P.dma(hT_v[:, :, i * 128:(i + 1) * 128], ht[:], q='sync', out_dram=True, w=['hT_d'])


def mod_stream(P, adaw_d, adab_d, cT_d, nch, pfx):
    awb = [P.sb(pfx + 'awb%d' % i, [128, KC, 512], F32) for i in range(1)]
    src = adaw_d.rearrange('(k p) n -> p k n', p=128)
    ab = P.sb(pfx + 'ab', [128, nch * 8], F32)
    P.dma(ab[:], adab_d, q='sync')
    cT = P.sb(pfx + 'cT', [128, KC, 2], F32)
    P.dma(cT[:], cT_d, q='sync')
    sc = P.sb(pfx + 'sc', [128, KC, 2], F32)
    P.act(sc[:], cT[:], AF.Silu)
    modT = P.sb(pfx + 'modT', [128, nch * 8, 2], F32)
    pm = P.ps(pfx + 'pm', [128, 2], F32)
    for hc in range(nch * 2):
        aw = awb[0]
        P.dma(aw[:], src[:, :, hc * 512:(hc + 1) * 512], q='sync', w=[aw])
        for jj in range(4):
            j = hc * 4 + jj
            for k in range(KC):
                P.mm(pm[:], aw[:, k, jj * 128:(jj + 1) * 128], sc[:, k, :], start=(k == 0), stop=(k == KC - 1),
                     r=[aw, sc], w=[pm])
            P.ts(modT[:, j, :], pm[:], ab[:, j:j + 1], None, ALU.add, r=[pm, ab], w=[(modT, j)])
    return modT


def bcast_rows(P, modT, j0, gbc_d, identf, pfx):
    gb = P.sb(pfx + 'gb', [128, D], F32)
    P.dma(gb[:], gbc_d, q='sync')
    ones = P.sb(pfx + 'ones', [128, 128], F32)
    P.memset(ones[:], 1.0)
    mk = P.sb(pfx + 'mk', [128, 128], F32)
    pb = P.ps(pfx + 'pb', [128, 128], F32)
    G = [P.sb(pfx + 'G%d' % c, [128, D], F32) for c in range(2)]
    for c in range(2):
        for k in range(KC):
            P.ts(mk[:], ones[:], modT[:, j0 + k, c:c + 1], None, ALU.mult, r=[ones, modT], w=[mk])
            P.mm(pb[:], mk[:], identf[:], r=[mk, identf], w=[pb])
            P.tt(G[c][:, k * 128:(k + 1) * 128], pb[:], gb[:, k * 128:(k + 1) * 128], ALU.mult, r=[pb, gb], w=[G[c]])
    return G


def resid_epilogue(P, ps_halves, xt, G, out_d_rows, bufs, q='sync'):
    mo, junk, ss, tmp, xo = bufs
    for hf in range(2):
        P.copy(mo[:, hf * 512:(hf + 1) * 512], ps_halves[hf][:], eng='vector' if hf == 0 else 'scalar', r=[ps_halves[hf]], w=[mo])
    P.act(junk[:], mo[:], AF.Square, accum=ss[:, 0:1], r=[mo], w=[junk, ss])
    P.act(ss[:, 1:2], ss[:, 0:1], AF.Sqrt, scale=1.0 / D, bias=1e-6, r=[ss], w=[ss])
    P.recip(ss[:, 2:3], ss[:, 1:2], r=[ss], w=[ss])
    P.stt(tmp[:], mo[:], ss[:, 2:3], G[:], ALU.mult, ALU.mult, r=[mo, ss, G], w=[tmp])
    P.tt(xo[:], xt[:], tmp[:], ALU.add, r=[xt, tmp], w=[xo])
    P.dma(out_d_rows, xo[:], q=q, out_dram=True, w=['xout'])


def groups_of(NT, gs):
    g = []
    t = 0
    while t < NT - 1:
        n = min(gs, NT - 1 - t)
        g.append((t, n, 0))
        t += n
    g.append((NT - 1, 1, 1))
    return g


def build_T1(NT):
    P = Prog()
    TT = NT * 128
    x_d = P.dram('x', [TT, D], F32, 'ExternalInput')
    hT_d = P.dram('hT', [D, TT], BF16, 'ExternalInput')
    yT_d = P.dram('yT', [D, TT], BF16, 'ExternalInput')
    wg_d = P.dram('wg', [D, 4096], F32, 'ExternalInput')
    wb_d = P.dram('wb', [D, D], F32, 'ExternalInput')
    wo_d = P.dram('wo', [D, D], F32, 'ExternalInput')
    adaw_d = P.dram('adaw', [D, D], F32, 'ExternalInput')
    adab_d = P.dram('adab', [128, 8], F32, 'ExternalInput')
    cT_d = P.dram('cT', [128, KC, 2], F32, 'ExternalInput')
    gbc_d = P.dram('gbc', [128, D], F32, 'ExternalInput')
    idf_d = P.dram('identf', [128, 128], F32, 'ExternalInput')
    xo_d = P.dram('xo', [TT, D], F32, 'ExternalOutput')
    identf = P.sb('identf', [128, 128], F32)
    P.dma(identf[:], idf_d)
    Wg = load_w_bf16(P, 'Wg', wg_d, D, 4096)
    Wb = load_w_bf16(P, 'Wb', wb_d, D, D)
    Wo = load_w_bf16(P, 'Wo', wo_d, D, D)
    modT = mod_stream(P, adaw_d, adab_d, cT_d, 1, 't1')
    G = bcast_rows(P, modT, 0, gbc_d, identf, 't1')
    hv = hT_d.rearrange('(k p) t -> p k t', p=128)
    yv = yT_d.rearrange('(k p) t -> p k t', p=128)
    hbs = [P.sb('hb%d' % i, [128, KC, 512], BF16) for i in range(1)]
    ybs = [P.sb('yb%d' % i, [128, KC, 512], BF16) for i in range(1)]
    mT = P.sb('mT', [128, KC, 512], BF16)
    macc = P.sb('macc', [128, 512], F32)
    sig = P.sb('sig', [128, 512], F32)
    tmpm = P.sb('tmpm', [128, 512], F32)
    pg = [P.ps('pg%d' % i, [128, 512], F32) for i in range(2)]
    pl = [P.ps('pl%d' % i, [128, 512], F32) for i in range(2)]
    po = [P.ps('po%d' % i, [128, 512], F32) for i in range(2)]
    xts = [P.sb('xt%d' % i, [128, D], F32) for i in range(2)]
    bufs = (P.sb('mo', [128, D], F32), P.sb('junk', [128, D], F32), P.sb('ss', [128, 4], F32), P.sb('tmp', [128, D], F32),
            P.sb('xo', [128, D], F32))
    it = 0
    for gi, (t0, ntile, c) in enumerate(groups_of(NT, 4)):
        n = ntile * 128
        hb = hbs[0]; yb = ybs[0]
        P.dma(hb[:, :, :n], hv[:, :, t0 * 128:t0 * 128 + n], q='sync', w=[hb])
        P.dma(yb[:, :, :n], yv[:, :, t0 * 128:t0 * 128 + n], q='sync', w=[yb])
        for dc in range(KC):
            for nb in range(4):
                g_ = pg[it % 2]; l_ = pl[it % 2]
                it += 1
                col = nb * 1024 + dc * 128
                for k in range(KC):
                    P.mm(g_[:, :n], Wg[:, k, col:col + 128], hb[:, k, :n], start=(k == 0), stop=(k == KC - 1), r=[Wg, hb], w=[g_])
                for cc in range(2):
                    P.mm(l_[:, :n], Wb[:, nb * 2 + cc, dc * 128:(dc + 1) * 128], yb[:, nb * 2 + cc, :n], start=(cc == 0), stop=(cc == 1),
                         r=[Wb, yb], w=[l_])
                P.copy(sig[:, :n], g_[:, :n], eng='vector', r=[g_], w=[sig])
                P.act(sig[:, :n], sig[:, :n], AF.Sigmoid, r=[sig], w=[sig])
                if nb == 0:
                    P.tt(macc[:, :n], sig[:, :n], l_[:, :n], ALU.mult, r=[sig, l_], w=[macc])
                else:
                    P.tt(tmpm[:, :n], sig[:, :n], l_[:, :n], ALU.mult, r=[sig, l_], w=[tmpm])
                    if nb < 3:
                        P.tt(macc[:, :n], macc[:, :n], tmpm[:, :n], ALU.add, r=[macc, tmpm], w=[macc])
                    else:
                        P.tt(mT[:, dc, :n], macc[:, :n], tmpm[:, :n], ALU.add, r=[macc, tmpm], w=[mT])
        for j in range(ntile):
            t = t0 + j
            xt = xts[t % 2]
            P.dma(xt[:], x_d[t * 128:(t + 1) * 128, :], q='sync')
            for hf in range(2):
                for dc in range(KC):
                    P.mm(po[hf][:], mT[:, dc, j * 128:(j + 1) * 128], Wo[:, dc, hf * 512:(hf + 1) * 512], start=(dc == 0), stop=(dc == KC - 1),
                         r=[mT, Wo], w=[po[hf]])
            resid_epilogue(P, po, xt, G[c], xo_d[t * 128:(t + 1) * 128, :], bufs)
    return P.emit()


def build_T2(NT):
    P = Prog()
    TT = NT * 128
    x_d = P.dram('x', [TT, D], F32, 'ExternalInput')
    wu_d = P.dram('wu', [D, 4096], F32, 'ExternalInput')
    wd_d = P.dram('wd', [4096, D], F32, 'ExternalInput')
    adaw_d = P.dram('adaw', [D, 3 * D], F32, 'ExternalInput')
    adab_d = P.dram('adab', [128, 24], F32, 'ExternalInput')
    cT_d = P.dram('cT', [128, KC, 2], F32, 'ExternalInput')
    g_d = P.dram('gT', [128, KC], F32, 'ExternalInput')
    gbc_d = P.dram('gbc', [128, D], F32, 'ExternalInput')
    idf_d = P.dram('identf', [128, 128], F32, 'ExternalInput')
    id_d = P.dram('ident', [128, 128], BF16, 'ExternalInput')
    xo_d = P.dram('xo', [TT, D], F32, 'ExternalOutput')
    identf = P.sb('identf', [128, 128], F32)
    P.dma(identf[:], idf_d)
    ident = P.sb('identb', [128, 128], BF16)
    P.dma(ident[:], id_d)
    gT = P.sb('gT', [128, KC], F32)
    P.dma(gT[:], g_d)
    Wu = load_w_bf16(P, 'Wu', wu_d, D, 4096)
    Wd = load_w_bf16(P, 'Wd', wd_d, 4096, D)
    modT = mod_stream(P, adaw_d, adab_d, cT_d, 3, 't2')
    G = bcast_rows(P, modT, 16, gbc_d, identf, 't2')
    scaleT = P.sb('scaleT', [128, KC, 2], F32)
    shiftT = P.sb('shiftT', [128, KC, 2], F32)
    for c in range(2):
        P.stt(scaleT[:, :, c], modT[:, 8:16, c], 1.0, gT[:], ALU.add, ALU.mult, r=[modT, gT], w=[scaleT])
        P.copy(shiftT[:, :, c], modT[:, 0:8, c], r=[modT], w=[shiftT])
    nbufs = (P.sb('njunk', [128, D], F32), P.sb('nss', [128, 4], F32), P.sb('nxn', [128, D], BF16),
             [P.ps('npT%d' % i, [128, 128], BF16) for i in range(2)])
    h2 = P.sb('h2', [128, KC, 256], BF16)
    aT = P.sb('aT', [128, 32, 256], BF16)
    rl = P.sb('rl', [128, 256], F32)
    pu = [P.ps('pu%d' % i, [128, 256], F32) for i in range(2)]
    po = [P.ps('po%d' % i, [128, 512], F32) for i in range(2)]
    xts = [P.sb('xt%d' % i, [128, D], F32) for i in range(2)]
    tmp_ = P.sb('tmp', [128, D], F32)
    bufs = (P.sb('mo', [128, D], F32), nbufs[0], P.sb('ss', [128, 4], F32), tmp_, tmp_)
    for gi, (t0, ntile, c) in enumerate(groups_of(NT, 2)):
        n = ntile * 128
        for j in range(ntile):
            t = t0 + j
            xt = xts[j]
            P.dma(xt[:], x_d[t * 128:(t + 1) * 128, :], q='sync')
            norm_to_hT(P, xt[:], h2[:, :, j * 128:(j + 1) * 128], scaleT, shiftT, c, ident, nbufs, t)
        for ff in range(32):
            p_ = pu[ff % 2]
            for k in range(KC):
                P.mm(p_[:, :n], Wu[:, k, ff * 128:(ff + 1) * 128], h2[:, k, :n], start=(k == 0), stop=(k == KC - 1), r=[Wu, h2], w=[p_])
            P.ts(rl[:, :n], p_[:, :n], 0.0, None, ALU.max, r=[p_], w=[rl])
            P.tt(aT[:, ff, :n], rl[:, :n], rl[:, :n], ALU.mult, r=[rl], w=[aT])
        for j in range(ntile):
            t = t0 + j
            for hf in range(2):
                for ff in range(32):
                    P.mm(po[hf][:], aT[:, ff, j * 128:(j + 1) * 128], Wd[:, ff, hf * 512:(hf + 1) * 512], start=(ff == 0), stop=(ff == 31),
                         r=[aT, Wd], w=[po[hf]])
            resid_epilogue(P, po, xts[j], G[c], xo_d[t * 128:(t + 1) * 128, :], bufs)
    return P.emit()


D = 1024
KC = 8


def chunks_of(L):
    ch = [(0, 256)]
    t = 256
    while t < L:
        ch.append((t, 512))
        t += 512
    return ch


class HLoader:
    def __init__(self, P, hT_d, pfx='hl'):
        self.P = P
        self.v = hT_d.rearrange('(k p) t -> p k t', p=128)
        self.bufs = [P.sb(pfx + 'hb%d' % i, [128, KC, 512], BF16) for i in range(2)]
        self.i = 0

    def load(self, t0, n):
        b = self.bufs[self.i % 2]
        self.i += 1
        self.P.dma(b[:, :, :n], self.v[:, :, t0:t0 + n], q='sync', w=[b])
        return b


def build_attn(L, kind):
    P = Prog()
    NTL = L // 128
    hT_d = P.dram('hT', [D, L], BF16, 'ExternalInput')
    w_d = P.dram('w', [D, 320], F32, 'ExternalInput')
    cos_d = P.dram('cosT', [64, L], F32, 'ExternalInput')
    sin_d = P.dram('sinT', [64, L], F32, 'ExternalInput')
    id_d = P.dram('ident', [128, 128], F32, 'ExternalInput')
    aux_d = P.dram('aux', [128, 256], F32, 'ExternalInput')
    y_d = P.dram('y', [L, 64], BF16, 'ExternalOutput')
    if kind == 'D':
        mk_d = P.dram('masks', [128, 2, 128], F32, 'ExternalInput')

    W = load_w_bf16(P, 'W', w_d, D, 320)
    ident = P.sb('ident', [128, 128], F32)
    P.dma(ident[:], id_d)
    aux = P.sb('aux', [128, 256], F32)
    P.dma(aux[:], aux_d)
    QT = P.sb('QT', [64, L], BF16)
    KT = P.sb('KT', [64, L], BF16)
    Va = P.sb('Va', [128, NTL, 65], BF16)
    P.memset(Va[:, :, 64:65], 1.0)
    pa = [P.ps('pa%d' % i, [128, 512], F32) for i in range(2)]
    acc = [P.ps('acc%d' % i, [128, 512], F32) for i in range(2)]
    pv = [P.ps('pv%d' % i, [128, 64], F32) for i in range(2)]
    ptr = [P.ps('ptr%d' % i, [128, 65], F32) for i in range(2)]
    cs = [P.sb('cs%d' % i, [64, 512], F32) for i in range(2)]
    sn = [P.sb('sn%d' % i, [64, 512], F32) for i in range(2)]
    t1 = P.sb('t1', [64, 512], F32)
    t2 = P.sb('t2', [64, 512], F32)
    hl = HLoader(P, hT_d)
    for ci, (t0, n) in enumerate(chunks_of(L)):
        hb = hl.load(t0, n)
        c_ = cs[ci % 2]
        s_ = sn[ci % 2]
        P.dma(c_[:, :n], cos_d[:, t0:t0 + n], q='sync')
        P.dma(s_[:, :n], sin_d[:, t0:t0 + n], q='sync')
        for which, dst in ((0, QT), (2, KT)):
            for k in range(KC):
                P.mm(pa[0][0:64, :n], W[:, k, which * 64:(which + 1) * 64], hb[:, k, :n], start=(k == 0), stop=(k == KC - 1),
                     r=[W, hb], w=[pa[0]])
            for k in range(KC):
                P.mm(pa[1][0:64, :n], W[:, k, (which + 1) * 64:(which + 2) * 64], hb[:, k, :n], start=(k == 0), stop=(k == KC - 1),
                     r=[W, hb], w=[pa[1]])
            P.tt(t1[:, :n], pa[0][0:64, :n], c_[:, :n], ALU.mult)
            P.tt(t2[:, :n], pa[1][0:64, :n], s_[:, :n], ALU.mult)
            P.tt(dst[:, t0:t0 + n], t1[:, :n], t2[:, :n], ALU.add, w=[(dst, ci)])
        for j in range(n // 128):
            p_ = pv[j % 2]
            for k in range(KC):
                P.mm(p_[:], hb[:, k, j * 128:(j + 1) * 128], W[:, k, 256:320], start=(k == 0), stop=(k == KC - 1),
                     r=[W, hb], w=[p_])
            P.copy(Va[:, t0 // 128 + j, 0:64], p_[:], eng='scalar', w=[(Va, t0 // 128 + j)])

    sm = P.sb('sm', [128, 16], F32)
    junk = P.sb('junk', [128, 64], F32)
    subA = P.sb('subA', [128, 64], F32)
    if kind == 'A':
        P.ttr(junk[:, 0:32], aux[:, 0:32], aux[:, 64:96], ALU.mult, ALU.add, sm[:, 0:1], r=[aux], w=[junk, (sm, 0)])
        P.ttr(junk[:, 32:64], aux[:, 32:64], aux[:, 96:128], ALU.mult, ALU.add, sm[:, 1:2], r=[aux, (sm, 0)], w=[junk, (sm, 1)])
        P.act(sm[:, 2:4], sm[:, 0:2], AF.Exp, r=[(sm, 1)], w=[(sm, 2)])
        P.tt(sm[:, 4:5], sm[:, 2:3], sm[:, 3:4], ALU.subtract, r=[(sm, 2)], w=[(sm, 4)])
        P.ts(sm[:, 5:6], sm[:, 4:5], aux[:, 192:193], -1.0, ALU.add, ALU.mult, r=[(sm, 4), aux], w=[(sm, 5)])
        P.ts(subA[:], aux[:, 128:192], aux[:, 193:194], None, ALU.mult, r=[aux], w=[subA])
        comps = [(0, 32), (32, 64)]
        scale = 32 ** -0.5
        QW = 512
    else:
        P.act(sm[:, 6:7], aux[:, 194:195], AF.Exp, r=[aux], w=[(sm, 6)])
        mk = P.sb('mk', [128, 2, 128], F32)
        P.dma(mk[:], mk_d)
        comps = [(0, 64)]
        scale = 64 ** -0.5
        QW = 128
    neglam = sm[:, 5:6]
    esink = sm[:, 6:7]

    pT = [P.sb('pT%d' % i, [128, 512], BF16) for i in range(3)]
    accS = [P.sb('accS%d' % i, [65, 512], F32) for i in range(2)]
    o1 = P.sb('o1', [128, 64], F32)
    dd = P.sb('dd', [128, 64], F32)
    fs = P.sb('fs', [128, 8], F32)
    yt = [P.sb('yt%d' % i, [128, 64], BF16) for i in range(2)]
    if kind == 'A':
        qch = [(0, 256, [0, 1])] + [(t0, 512, list(range(NTL))) for (t0, n) in chunks_of(L)[1:]]
    else:
        qch = []
        for g in range(NTL):
            if g < 2:
                qch.append((g * 128, 128, [(0, None), (1, None)]))
            else:
                ks = [(0, None), (1, None)]
                if g > 2:
                    ks.append((g - 1, 0))
                ks.append((g, None))
                if g < NTL - 1:
                    ks.append((g + 1, 1))
                qch.append((g * 128, 128, ks))
    it = 0
    yi = 0
    for (q0, nq, ks) in qch:
        for c, (c0, c1) in enumerate(comps):
            for ti, kk in enumerate(ks):
                kt, mi = kk if kind == 'D' else (kk, None)
                s_ = pa[it % 2]
                p_ = pT[it % 3]
                it += 1
                P.mm(s_[:, :nq], KT[c0:c1, kt * 128:(kt + 1) * 128], QT[c0:c1, q0:q0 + nq], r=[KT, QT], w=[s_])
                P.act(p_[:, :nq], s_[:, :nq], AF.Exp, scale=scale)
                if mi is not None:
                    P.tt(p_[:, :nq], p_[:, :nq], mk[:, mi, :], ALU.mult, r=[p_, mk], w=[p_])
                P.mm(acc[c][0:65, :nq], Va[:, kt, :], p_[:, :nq], start=(ti == 0), stop=(ti == len(ks) - 1),
                     r=[Va, p_], w=[acc[c]])
            P.copy(accS[c][:, :nq], acc[c][0:65, :nq], eng='vector')
        for j in range(nq // 128):
            for c in range(len(comps)):
                P.mm(ptr[c][:], accS[c][:, j * 128:(j + 1) * 128], ident[0:65, 0:65], r=[accS[c], ident], w=[ptr[c]])
            y_ = yt[yi % 2]
            yi += 1
            if kind == 'A':
                P.recip(fs[:, 0:1], ptr[0][:, 64:65], r=[ptr[0]], w=[(fs, 0)])
                P.ts(o1[:], ptr[0][:, 0:64], fs[:, 0:1], None, ALU.mult, r=[ptr[0], (fs, 0)], w=[o1])
                P.recip(fs[:, 1:2], ptr[1][:, 64:65], r=[ptr[1]], w=[(fs, 1)])
                P.tt(fs[:, 2:3], fs[:, 1:2], neglam, ALU.mult, r=[(fs, 1), (sm, 5)], w=[(fs, 2)])
                P.stt(dd[:], ptr[1][:, 0:64], fs[:, 2:3], o1[:], ALU.mult, ALU.add, r=[ptr[1], (fs, 2), o1], w=[dd])
                P.ttr(junk[:], dd[:], dd[:], ALU.mult, ALU.add, fs[:, 3:4], r=[dd], w=[junk, (fs, 3)])
                P.act(fs[:, 4:5], fs[:, 3:4], AF.Sqrt, scale=1.0 / 64, bias=1e-5, r=[(fs, 3)], w=[(fs, 4)])
                P.recip(fs[:, 5:6], fs[:, 4:5], r=[(fs, 4)], w=[(fs, 5)])
                P.stt(y_[:], dd[:], fs[:, 5:6], subA[:], ALU.mult, ALU.mult, r=[dd, (fs, 5), subA], w=[y_])
            else:
                P.tt(fs[:, 0:1], ptr[0][:, 64:65], esink, ALU.add, r=[ptr[0], (sm, 6)], w=[(fs, 0)])
                P.recip(fs[:, 1:2], fs[:, 0:1], r=[(fs, 0)], w=[(fs, 1)])
                P.ts(y_[:], ptr[0][:, 0:64], fs[:, 1:2], None, ALU.mult, r=[ptr[0], (fs, 1)], w=[y_])
            P.dma(y_d[q0 + j * 128:q0 + (j + 1) * 128, :], y_[:], q='sync', out_dram=True, w=['y_d'])
    return P.emit()


def build_ret(L, stop=99):
    P = Prog()
    NTL = L // 128
    hT_d = P.dram('hT', [D, L], BF16, 'ExternalInput')
    w_d = P.dram('w', [D, 256], F32, 'ExternalInput')
    tab_d = P.dram('tabs', [4, 32, L], F32, 'ExternalInput')
    idb_d = P.dram('identb', [128, 128], BF16, 'ExternalInput')
    cst_d = P.dram('cst', [128, 770], F32, 'ExternalInput')
    aux_d = P.dram('aux', [128, 128], F32, 'ExternalInput')
    y_d = P.dram('y', [L, 64], BF16, 'ExternalOutput')
    W = load_w_bf16(P, 'W', w_d, D, 256)
    identb = P.sb('identb', [128, 128], BF16)
    P.dma(identb[:], idb_d)
    cst = P.sb('cst', [128, 770], F32)
    P.dma(cst[:], cst_d)
    aux = P.sb('aux', [128, 128], F32)
    P.dma(aux[:], aux_d)
    qT = P.sb('qT', [32, L], BF16)
    kT = P.sb('kT', [32, L], BF16)
    V = P.sb('V', [128, NTL, 64], BF16)
    G = P.sb('G', [128, NTL, 64], F32)
    of = P.sb('of', [128, NTL, 64], F32)
    pa = [P.ps('pa%d' % i, [128, 512], F32) for i in range(2)]
    pv = [P.ps('pv%d' % i, [128, 128], F32) for i in range(2)]
    tb = [P.sb('tb%d' % i, [32, 4, 512], F32) for i in range(2)]
    t1 = P.sb('t1', [32, 512], F32)
    t2 = P.sb('t2', [32, 512], F32)
    hl = HLoader(P, hT_d)
    for ci, (t0, n) in enumerate(chunks_of(L)):
        hb = hl.load(t0, n)
        tb_ = tb[ci % 2]
        for q4 in range(4):
            P.dma(tb_[:, q4, :n], tab_d[q4, :, t0:t0 + n], q='sync', w=[tb_])
        for which, dst, tc in ((0, qT, 0), (2, kT, 2)):
            for k in range(KC):
                P.mm(pa[0][0:32, :n], W[:, k, which * 32:(which + 1) * 32], hb[:, k, :n], start=(k == 0), stop=(k == KC - 1),
                     r=[W, hb], w=[pa[0]])
            for k in range(KC):
                P.mm(pa[1][0:32, :n], W[:, k, (which + 1) * 32:(which + 2) * 32], hb[:, k, :n], start=(k == 0), stop=(k == KC - 1),
                     r=[W, hb], w=[pa[1]])
            P.tt(t1[:, :n], pa[0][0:32, :n], tb_[:, tc, :n], ALU.mult)
            P.tt(t2[:, :n], pa[1][0:32, :n], tb_[:, tc + 1, :n], ALU.mult)
            P.tt(dst[:, t0:t0 + n], t1[:, :n], t2[:, :n], ALU.add, w=[(dst, ci)])
        for j in range(n // 128):
            p_ = pv[j % 2]
            g = t0 // 128 + j
            for k in range(KC):
                P.mm(p_[:], hb[:, k, j * 128:(j + 1) * 128], W[:, k, 128:256], start=(k == 0), stop=(k == KC - 1),
                     r=[W, hb], w=[p_])
            P.copy(V[:, g, :], p_[:, 0:64], eng='vector', r=[p_], w=[(V, g)])
            P.copy(G[:, g, :], p_[:, 64:128], eng='vector', r=[p_], w=[(G, g)])
            P.act(G[:, g, :], G[:, g, :], AF.Silu, r=[(G, g)], w=[(G, g)])
    if stop == 1:
        return P.emit()
    sm = P.sb('sm', [128, 8], F32)
    P.act(sm[:, 0:2], aux[:, 0:2], AF.Sigmoid, r=[aux], w=[sm])
    P.act(sm[:, 2:4], sm[:, 0:2], AF.Ln, r=[sm], w=[sm])
    P.act(sm[:, 4:6], sm[:, 2:4], AF.Exp, scale=128.0, r=[sm], w=[sm])
    P.act(sm[:, 6:7], cst[:, 768:769], AF.Exp, scale=sm[:, 2:3], r=[cst, sm], w=[sm])
    P.act(sm[:, 7:8], cst[:, 769:770], AF.Exp, scale=sm[:, 3:4], r=[cst, sm], w=[sm])
    DT = P.sb('DT', [128, 2, 128], F32)
    qdec = P.sb('qdec', [128, 2, 128], F32)
    for d in range(2):
        P.act(DT[:, d, :], cst[:, 256 * d:256 * d + 128], AF.Exp, scale=sm[:, 2 + d:3 + d], r=[cst, sm], w=[DT])
        P.tt(DT[:, d, :], DT[:, d, :], cst[:, 256 * d + 128:256 * d + 256], ALU.mult, r=[DT, cst], w=[DT])
        P.act(qdec[:, d, :], cst[:, 512 + 128 * d:640 + 128 * d], AF.Exp, scale=sm[:, 2 + d:3 + d], r=[cst, sm], w=[qdec])
    if stop == 2:
        return P.emit()
    S = P.sb('S', [32, 64], F32)
    Sb = P.sb('Sb', [32, 64], BF16)
    qd = [P.sb('qd%d' % i, [32, 128], BF16) for i in range(2)]
    pT = [P.sb('pT%d' % i, [128, 128], BF16) for i in range(2)]
    kd = [P.sb('kd%d' % i, [128, 32], BF16) for i in range(2)]
    o_ps = [P.ps('o%d' % i, [128, 64], F32) for i in range(2)]
    k_ps = P.ps('kps', [128, 32], BF16)
    u_ps = P.ps('ups', [32, 64], F32)
    fs = P.sb('fs', [128, 8], F32)
    tot = P.sb('tot', [128, 64], F32)
    xc = P.sb('xc', [128, 64], F32)
    junk = P.sb('junk', [128, 64], F32)
    yt = [P.sb('yt%d' % i, [128, 64], BF16) for i in range(2)]
    it = 0
    for d in range(2):
        order = list(range(NTL)) if d == 0 else [1, 0] + list(range(NTL - 1, 1, -1))
        P.memset(S[:], 0.0)
        P.memset(Sb[:], 0.0)
        for g in order:
            sl = slice(g * 128, (g + 1) * 128)
            q_ = qd[it % 2]; p_ = pT[it % 2]; k_ = kd[it % 2]; o_ = o_ps[it % 2]; s_ = pa[it % 2]
            y_ = yt[it % 2]
            it += 1
            P.tt(q_[:], qT[:, sl], qdec[0:32, d, :], ALU.mult, r=[qT, qdec], w=[q_])
            P.mm(o_[:], q_[:], Sb[:], start=True, stop=False, r=[q_, Sb], w=[o_])
            P.mm(s_[:, 0:128], kT[:, sl], qT[:, sl], r=[kT, qT], w=[s_])
            P.tt(p_[:], s_[:, 0:128], DT[:, d, :], ALU.mult, r=[s_, DT], w=[p_])
            P.mm(o_[:], p_[:], V[:, g, :], start=False, stop=True, r=[p_, V], w=[o_])
            P.op('tensor', lambda e, sl=sl: e.transpose(k_ps[:], kT[:, sl], identb[0:32, 0:32]), [kT, identb], [k_ps])
            P.ts(k_[:], k_ps[:], sm[:, 6 + d:7 + d], None, ALU.mult, r=[k_ps, sm], w=[k_])
            P.mm(u_ps[:], k_[:], V[:, g, :], r=[k_, V], w=[u_ps])
            P.stt(S[:], S[:], sm[0:32, 4 + d:5 + d], u_ps[:], ALU.mult, ALU.add, r=[S, sm, u_ps], w=[S])
            P.copy(Sb[:], S[:], eng='scalar')
            if d == 0:
                P.copy(of[:, g, :], o_[:], eng='scalar', r=[o_], w=[(of, g)])
            else:
                P.tt(tot[:], o_[:], of[:, g, :], ALU.add, r=[o_, (of, g)], w=[tot])
                P.reduce(fs[:, 0:1], tot[:], r=[tot], w=[(fs, 0)])
                P.ts(fs[:, 1:2], fs[:, 0:1], -1.0 / 64, None, ALU.mult, r=[(fs, 0)], w=[(fs, 1)])
                P.ts(xc[:], tot[:], fs[:, 1:2], None, ALU.add, r=[tot, (fs, 1)], w=[xc])
                P.act(junk[:], xc[:], AF.Square, accum=fs[:, 2:3], r=[xc], w=[junk, (fs, 2)])
                P.act(fs[:, 3:4], fs[:, 2:3], AF.Sqrt, scale=1.0 / 64, bias=1e-5, r=[(fs, 2)], w=[(fs, 3)])
                P.recip(fs[:, 4:5], fs[:, 3:4], r=[(fs, 3)], w=[(fs, 4)])
                P.stt(xc[:], xc[:], fs[:, 4:5], aux[:, 64:128], ALU.mult, ALU.mult, r=[xc, (fs, 4), aux], w=[xc])
                P.tt(y_[:], xc[:], G[:, g, :], ALU.mult, r=[xc, (G, g)], w=[y_])
                P.dma(y_d[sl, :], y_[:], q='sync', out_dram=True, w=['y_d'])
    return P.emit()


def build_rwkv(L, same_sync=True, stop=99):
    P = Prog()
    NTL = L // 128
    hT_ds = [P.dram('hTf', [D, L], BF16, 'ExternalInput'), P.dram('hTb', [D, L], BF16, 'ExternalInput')]
    hP_ds = [P.dram('hPf', [D, L], BF16, 'ExternalInput'), P.dram('hPb', [D, L], BF16, 'ExternalInput')]
    hN_ds = [P.dram('hNf', [D, L], BF16, 'ExternalInput'), P.dram('hNb', [D, L], BF16, 'ExternalInput')]
    w_d = P.dram('w', [D, 576], F32, 'ExternalInput')
    mu_d = P.dram('mu', [128, 576], F32, 'ExternalInput')
    aux_d = P.dram('aux', [128, 576], F32, 'ExternalInput')
    wa_d = P.dram('w2a2', [128, 2, 64], F32, 'ExternalInput')
    g2_d = P.dram('g2h', [128, 64], F32, 'ExternalInput')
    cst_d = P.dram('cst', [128, 384], F32, 'ExternalInput')
    y_d = P.dram('y', [L, 64], BF16, 'ExternalOutput')
    rows_s = [P.dram('rows%d' % d, [L, 384], F32, 'ExternalOutput') for d in range(2)]
    vT_s = [P.dram('vTs%d' % d, [64, L], F32, 'ExternalOutput') for d in range(2)]
    vg_s = P.dram('vgs', [L, 128], F32, 'ExternalOutput')

    cst = P.sb('cst', [128, 384], F32); P.dma(cst[:], cst_d)
    ident = cst[:, 0:128]; J = cst[:, 128:256]; E = cst[0:2, 256:384]
    aux = P.sb('aux', [128, 576], F32); P.dma(aux[:], aux_d)
    mu = P.sb('mu', [128, 576], F32); P.dma(mu[:], mu_d)
    Wf = P.sb('Wf', [128, KC, 576], F32)
    wsrc = w_d.rearrange('(k p) n -> p k n', p=128)
    for k in range(KC):
        P.dma(Wf[:, k, :], wsrc[:, k, :], q='sync', w=[(Wf, k)])
    W1 = P.sb('W1', [128, KC, 576], BF16)
    W2 = P.sb('W2', [128, KC, 576], BF16)
    wtmp = P.sb('wtmp', [128, 576], F32)
    for k in range(KC):
        P.tt(wtmp[:], Wf[:, k, :], mu[:], ALU.mult, r=[(Wf, k), mu], w=[wtmp])
        P.tt(W1[:, k, :], Wf[:, k, :], wtmp[:], ALU.subtract, r=[(Wf, k), wtmp], w=[(W1, k)])
        P.ts(W2[:, k, :], wtmp[:], 0.5, None, ALU.mult, r=[wtmp], w=[(W2, k)])
    waf = P.sb('waf', [128, 2, 64], F32); P.dma(waf[:], wa_d)
    wab = P.sb('wab', [128, 2, 64], BF16); P.copy(wab[:], waf[:])
    waf2 = P.sb('waf2', [128, 2, 64], F32); P.dma(waf2[0:64], wa_d[64:128])
    wab2 = P.sb('wab2', [128, 2, 64], BF16); P.copy(wab2[0:64], waf2[0:64])
    g2f = P.sb('g2f', [128, 64], F32); P.dma(g2f[:], g2_d)
    g2b = P.sb('g2b', [128, 64], BF16); P.copy(g2b[:], g2f[:])
    omka = P.sb('omka', [128, 64], F32)
    P.ts(omka[:], aux[:, 320:384], -1.0, 1.0, ALU.mult, ALU.add, r=[aux], w=[omka])

    hbs = [P.sb('hbx%d' % i, [128, KC, 512], BF16) for i in range(2)]
    hps = [P.sb('hpx%d' % i, [128, KC, 512], BF16) for i in range(1)]
    hns = [P.sb('hnx%d' % i, [128, KC, 512], BF16) for i in range(1)]
    hs = P.sb('hs', [128, KC, 512], BF16)
    pa = [P.ps('pa%d' % i, [128, 512], F32) for i in range(2)]
    pt = [P.ps('pt%d' % i, [128, 128], F32) for i in range(2)]
    pw = [P.ps('pw%d' % i, [128, 128], F32) for i in range(2)]
    lo = P.sb('lo', [128, 512], BF16)
    lo2 = P.sb('lo2', [128, 512], BF16)
    stg = P.sb('stg', [128, 512], F32)
    sg = P.sb('sg', [128, 512], BF16)
    vTt = P.sb('vTt', [64, 512], F32)
    rows = [P.sb('rw%d' % i, [128, 6, 64], F32) for i in range(2)]
    for rw_ in rows:
        P.memset(rw_[:], 0.0)
    ea = P.sb('ea', [128, 64], F32)
    eb = P.sb('eb', [128, 64], F32)
    ec = P.sb('ec', [128, 64], F32)
    ed = P.sb('ed', [128, 64], F32)
    fs = P.sb('fs', [128, 8], F32)
    vg = [P.sb('vg%d' % i, [128, 128], F32) for i in range(2)]
    ci_all = 0
    for d in range(2):
        hv = hT_ds[d].rearrange('(k p) t -> p k t', p=128)
        hpv = hP_ds[d].rearrange('(k p) t -> p k t', p=128)
        hnv = hN_ds[d].rearrange('(k p) t -> p k t', p=128)
        for ci, (t0, n) in enumerate(chunks_of(L)):
            hb = hbs[ci_all % 2]; hp = hps[0]; hn = hns[0]
            ci_all += 1
            P.dma(hb[:, :, :n], hv[:, :, t0:t0 + n], q='sync', w=[hb])
            P.dma(hp[:, :, :n], hpv[:, :, t0:t0 + n], q='sync', w=[hp])
            P.dma(hn[:, :, :n], hnv[:, :, t0:t0 + n], q='sync', w=[hn])
            P.tt(hs[:, :, :n], hp[:, :, :n], hn[:, :, :n], ALU.add, r=[hp, hn], w=[hs])

            def proj_fm(ps, c0, c1):
                for k in range(KC):
                    P.mm(ps[0:c1 - c0, :n], W1[:, k, c0:c1], hb[:, k, :n], start=(k == 0), stop=False, r=[W1, hb], w=[ps])
                for k in range(KC):
                    P.mm(ps[0:c1 - c0, :n], W2[:, k, c0:c1], hs[:, k, :n], start=False, stop=(k == KC - 1), r=[W2, hs], w=[ps])
            proj_fm(pa[0], 128, 192)
            P.copy(vTt[:, :n], pa[0][0:64, :n], eng='scalar')
            P.dma(vT_s[d][:, t0:t0 + n], vTt[:, :n], q='sync', w=[('vT_s', d)])
            proj_fm(pa[1], 192 + 128 * d, 256 + 128 * d)
            P.copy(stg[0:64, :n], pa[1][0:64, :n], eng='vector', r=[pa[1]], w=[stg])
            P.act(lo[0:64, :n], stg[0:64, :n], AF.Tanh, r=[stg], w=[lo])
            proj_fm(pa[1], 256 + 128 * d, 320 + 128 * d)
            P.copy(lo2[0:64, :n], pa[1][0:64, :n], eng='vector', r=[pa[1]], w=[lo2])
            if d == 0:
                proj_fm(pa[0], 448, 576)
                P.copy(stg[:, :n], pa[0][:, :n], eng='vector', r=[pa[0]], w=[stg])
                P.act(sg[:, :n], stg[:, :n], AF.Sigmoid, r=[stg], w=[sg])
            for j in range(n // 128):
                g = t0 // 128 + j
                tsl = slice(j * 128, (j + 1) * 128)
                p1 = pt[g % 2]; p2 = pw[g % 2]; rw = rows[g % 2]
                ncols = 192 if d == 0 else 128
                for k in range(KC):
                    P.mm(p1[:, 0:128], hb[:, k, tsl], W1[:, k, 0:128], start=(k == 0), stop=False, r=[W1, hb], w=[p1])
                for k in range(KC):
                    P.mm(p1[:, 0:128], hs[:, k, tsl], W2[:, k, 0:128], start=False, stop=(k == KC - 1), r=[W2, hs], w=[p1])
                P.mm(p2[:, 0:64], lo[0:64, tsl], wab[0:64, d, :], r=[lo, wab], w=[p2])
                P.mm(p2[:, 64:128], lo2[0:64, tsl], wab2[0:64, d, :], r=[lo2, wab2], w=[p2])
                r_ = rw[:, 4, :]
                P.copy(r_, p1[:, 0:64], eng='scalar', r=[p1], w=[rw])
                kk_ = ea
                P.copy(kk_[:], p1[:, 64:128], eng='scalar', r=[p1], w=[ea])
                P.tt(eb[:], p2[:, 0:64], aux[:, 64 * d:64 * d + 64], ALU.add, r=[p2, aux], w=[eb])
                P.act(eb[:], eb[:], AF.Sigmoid, r=[eb], w=[eb])
                P.act(rw[:, 1, :], eb[:], AF.Exp, scale=-0.6065306597126334, r=[eb], w=[rw])
                P.tt(ec[:], p2[:, 64:128], aux[:, 128 + 64 * d:192 + 64 * d], ALU.add, r=[p2, aux], w=[ec])
                P.act(ec[:], ec[:], AF.Sigmoid, r=[ec], w=[ec])
                P.tt(ed[:], ea[:], aux[:, 256:320], ALU.mult, r=[ea, aux], w=[ed])
                P.act(eb[:], ed[:], AF.Square, accum=fs[:, 0:1], r=[ed], w=[eb, fs])
                P.ts(fs[:, 0:1], fs[:, 0:1], 1e-12, None, ALU.max, r=[fs], w=[fs])
                P.act(fs[:, 1:2], fs[:, 0:1], AF.Sqrt, r=[fs], w=[fs])
                P.recip(fs[:, 2:3], fs[:, 1:2], r=[fs], w=[fs])
                P.ts(rw[:, 0, :], ed[:], fs[:, 2:3], -1.0, ALU.mult, ALU.mult, r=[ed, fs], w=[rw])
                P.stt(rw[:, 2, :], rw[:, 0, :], -1.0, ec[:], ALU.mult, ALU.mult, r=[rw, ec], w=[rw])
                P.tt(ed[:], ec[:], aux[:, 320:384], ALU.mult, r=[ec, aux], w=[ed])
                P.tt(ed[:], ed[:], omka[:], ALU.add, r=[ed, omka], w=[ed])
                P.tt(rw[:, 3, :], ea[:], ed[:], ALU.mult, r=[ea, ed], w=[rw])
                P.tt(ed[:], rw[:, 4, :], rw[:, 3, :], ALU.mult, r=[rw], w=[ed])
                P.ttr(ed[:], ed[:], aux[:, 384:448], ALU.mult, ALU.add, rw[:, 5, 0:1], r=[ed, aux], w=[ed, rw])
                P.dma(rows_s[d][g * 128:(g + 1) * 128, :], rw[:].rearrange('p a b -> p (a b)'), q='sync', r=[rw], w=[('rows_s', d)])
                if d == 0:
                    vg_ = vg[g % 2]
                    for k in range(KC):
                        P.mm(p2[:, 0:64], hb[:, k, tsl], W1[:, k, 128:192], start=(k == 0), stop=False, r=[W1, hb], w=[p2])
                    for k in range(KC):
                        P.mm(p2[:, 0:64], hs[:, k, tsl], W2[:, k, 128:192], start=False, stop=(k == KC - 1), r=[W2, hs], w=[p2])
                    P.mm(p2[:, 64:128], sg[:, tsl], g2b[:], r=[sg, g2b], w=[p2])
                    P.copy(vg_[:], p2[:], eng='vector', r=[p2], w=[vg_])
                    P.dma(vg_s[g * 128:(g + 1) * 128, :], vg_[:], q='sync', r=[vg_], w=['vg_s'])
    if stop == 1:
        return P.emit()
    NB = 8
    S = P.sb('S', [128, 64], F32)
    P.memset(S[:], 0.0)
    tmp = P.sb('tmp', [128, 64], F32)
    sa = P.sb('sa', [128, 1], F32)
    ysc = P.sb('ysc', [128, L], F32)
    rb = [P.sb('rb%d' % i, [2, NB * 320], F32) for i in range(2)]
    vc = [P.sb('vc%d' % i, [128, NB], F32) for i in range(2)]
    bc = pa + [P.ps('bc%d' % i, [128, 512], F32) for i in range(2)]
    P.same_engine_sync_default = same_sync
    for blk in range(L // NB):
        s0 = blk * NB
        rb_ = rb[blk % 2]; vc_ = vc[blk % 2]
        for d in range(2):
            P.dma(rb_[d:d + 1, :].rearrange('o (s c) -> o s c', c=320), rows_s[d][s0:s0 + NB, 0:320], q='sync', r=[('rows_s', d)], w=[rb_])
            P.dma(vc_[64 * d:64 * d + 64, :], vT_s[d][:, s0:s0 + NB], q='sync', r=[('vT_s', d)], w=[vc_])
        for j in range(NB):
            s = s0 + j
            b_ = bc[s % 4]
            P.mm(b_[:, 0:320], E, rb_[0:2, j * 320:(j + 1) * 320], r=[cst, rb_], w=[b_])
            P.tt(tmp[:], S[:], b_[:, 0:64], ALU.mult, r=[S, b_], w=[tmp])
            P.reduce(sa[:], tmp[:], r=[tmp], w=[sa])
            P.tt(S[:], S[:], b_[:, 64:128], ALU.mult, r=[S, b_], w=[S])
            P.stt(S[:], b_[:, 128:192], sa[:, 0:1], S[:], ALU.mult, ALU.add, r=[b_, sa, S], w=[S])
            P.stt(S[:], b_[:, 192:256], vc_[:, j:j + 1], S[:], ALU.mult, ALU.add, r=[b_, vc_, S], w=[S])
            P.tt(tmp[:], S[:], b_[:, 256:320], ALU.mult, r=[S, b_], w=[tmp])
            P.reduce(ysc[:, s:s + 1], tmp[:], r=[tmp], w=[(ysc, s // 128)])
    if stop == 2:
        return P.emit()
    yfs = P.sb('yfs', [128, 64], F32)
    ybs = P.sb('ybs', [128, 128], F32)
    bf_ = P.sb('bf_', [128, 64], F32)
    tot = P.sb('tot', [128, 64], F32)
    xc = P.sb('xc', [128, 64], F32)
    yt = [P.sb('yt%d' % i, [128, 64], BF16) for i in range(2)]
    for g in range(NTL):
        gm = (1 - g) if g < 2 else (NTL + 1 - g)
        p1 = pt[g % 2]; p2 = pw[g % 2]; vg_ = vg[g % 2]; y_ = yt[g % 2]
        P.mm(p1[:, 0:64], ysc[:, g * 128:(g + 1) * 128], ident[:, 0:64], r=[(ysc, g), cst], w=[p1])
        P.mm(p2[:, 0:64], ysc[:, gm * 128:(gm + 1) * 128], ident[:, 64:128], r=[(ysc, gm), cst], w=[p2])
        P.copy(yfs[:], p1[:, 0:64], eng='scalar', r=[p1], w=[yfs])
        P.copy(ybs[:, 0:64], p2[:, 0:64], eng='vector', r=[p2], w=[ybs])
        P.dma(ybs[:, 64:128], rows_s[1][gm * 128:(gm + 1) * 128, 320:384], q='sync', r=[('rows_s', 1)], w=[ybs])
        P.dma(bf_[:], rows_s[0][g * 128:(g + 1) * 128, 320:384], q='sync', r=[('rows_s', 0)], w=[bf_])
        P.dma(vg_[:], vg_s[g * 128:(g + 1) * 128, :], q='sync', r=['vg_s'], w=[vg_])
        P.mm(p2[:, 0:65], J, ybs[:, 0:65], r=[cst, ybs], w=[p2])
        P.tt(tot[:], yfs[:], p2[:, 0:64], ALU.add, r=[yfs, p2], w=[tot])
        P.tt(bf_[:, 0:1], bf_[:, 0:1], p2[:, 64:65], ALU.add, r=[bf_, p2], w=[bf_])
        P.reduce(fs[:, 0:1], tot[:], r=[tot], w=[fs])
        P.ts(fs[:, 1:2], fs[:, 0:1], -1.0 / 64, None, ALU.mult, r=[fs], w=[fs])
        P.ts(xc[:], tot[:], fs[:, 1:2], None, ALU.add, r=[tot, fs], w=[xc])
        P.act(tot[:], xc[:], AF.Square, accum=fs[:, 2:3], r=[xc], w=[tot, fs])
        P.act(fs[:, 3:4], fs[:, 2:3], AF.Sqrt, scale=1.0 / 64, bias=64e-5, r=[fs], w=[fs])
        P.recip(fs[:, 4:5], fs[:, 3:4], r=[fs], w=[fs])
        P.stt(xc[:], xc[:], fs[:, 4:5], aux[:, 448:512], ALU.mult, ALU.mult, r=[xc, fs, aux], w=[xc])
        P.tt(xc[:], xc[:], aux[:, 512:576], ALU.add, r=[xc, aux], w=[xc])
        P.stt(xc[:], vg_[:, 0:64], bf_[:, 0:1], xc[:], ALU.mult, ALU.add, r=[vg_, bf_, xc], w=[xc])
        P.tt(y_[:], xc[:], vg_[:, 64:128], ALU.mult, r=[xc, vg_], w=[y_])
        P.dma(y_d[g * 128:(g + 1) * 128, :], y_[:], q='sync', out_dram=True, w=['y_d'])
    return P.emit()


import numpy as np
import ml_dtypes
BF = ml_dtypes.bfloat16

A_OFF = 0; B_OFF = 768; C_OFF = 1920; D_OFF = 2688; G_OFF = 3200


def rope_tab(pos, nfreq):
    inv = (np.float32(10000.0) ** (-(np.arange(nfreq, dtype=np.float32) / np.float32(nfreq)))).astype(np.float32)
    ang = (pos.astype(np.float32)[:, None] * inv[None, :]).astype(np.float32)
    c = np.cos(ang).astype(np.float32); s = np.sin(ang).astype(np.float32)
    return np.concatenate([c, c], -1), np.concatenate([-s, s], -1)


def axial_tab(S, dim):
    rows = S // 64
    row = np.repeat(np.arange(rows), 64); col = np.tile(np.arange(64), rows)
    cr, sr = rope_tab(row, dim // 4); cc, sc = rope_tab(col, dim // 4)
    return np.concatenate([cr, cc], -1), np.concatenate([sr, sc], -1)


def half_perm(n):
    return np.concatenate([np.arange(n // 2, n), np.arange(0, n // 2)])


def with_ctx(tab, fill):
    S, d = tab.shape
    out = np.full((d, 256 + S), fill, np.float32)
    out[:, 256:] = tab.T
    return np.ascontiguousarray(out)


def attn_inputs(kind, S, h, w_in_l, p, l):
    if kind == 'A':
        qc = A_OFF + h * 64 + np.arange(64)
        kc = A_OFF + 256 + h * 64 + np.arange(64)
        vc = A_OFF + 512 + h * 64 + np.arange(64)
        perm = np.concatenate([g * 16 + half_perm(16) for g in range(4)])
        c1, s1 = axial_tab(S, 32)
        cosT = with_ctx(np.concatenate([c1, c1], -1), 1.0); sinT = with_ctx(np.concatenate([s1, s1], -1), 0.0)
    else:
        qc = D_OFF + h * 64 + np.arange(64)
        kc = D_OFF + 256 + (h // 2) * 64 + np.arange(64)
        vc = D_OFF + 384 + (h // 2) * 64 + np.arange(64)
        perm = np.concatenate([g * 32 + half_perm(32) for g in range(2)])
        c1, s1 = axial_tab(S, 64)
        cosT = with_ctx(c1, 1.0); sinT = with_ctx(s1, 0.0)
    w = np.concatenate([w_in_l[:, qc], w_in_l[:, qc[perm]], w_in_l[:, kc], w_in_l[:, kc[perm]], w_in_l[:, vc]], 1)
    aux = np.zeros((128, 256), np.float32)
    if kind == 'A':
        aux[:, 0:64] = p['diff_lam_q'].reshape(1, 64)
        aux[:, 64:128] = p['diff_lam_k'].reshape(1, 64)
        aux[:, 128:192] = p['diff_subln'][h * 64:(h + 1) * 64][None, :]
        lam_init = 0.8 - 0.6 * np.exp(-0.3 * l)
        aux[:, 192] = lam_init
        aux[:, 193] = 1.0 - lam_init
    else:
        aux[:, 194] = p['win_sink'][h]
    d = dict(w=np.ascontiguousarray(w), cosT=cosT, sinT=sinT, ident=np.eye(128, dtype=np.float32), aux=aux)
    if kind == 'D':
        j = np.arange(128)[:, None]; i = np.arange(128)[None, :]
        d['masks'] = np.ascontiguousarray(np.stack([(i <= j), (j <= i)], 1).astype(np.float32))
    return d


def ret_inputs(S, h, w_in_l, p):
    qc = C_OFF + h * 32 + np.arange(32)
    kc = C_OFF + 128 + h * 32 + np.arange(32)
    vc = C_OFF + 256 + h * 64 + np.arange(64)
    gc = C_OFF + 512 + h * 64 + np.arange(64)
    perm = half_perm(32)
    w = np.concatenate([w_in_l[:, qc], w_in_l[:, qc[perm]], w_in_l[:, kc], w_in_l[:, kc[perm]], w_in_l[:, vc], w_in_l[:, gc]], 1)
    c1, s1 = rope_tab(np.arange(S), 16)
    sc = np.float32(32 ** -0.5)
    tabs = np.stack([with_ctx(c1, 1.0), with_ctx(s1, 0.0), with_ctx(c1 * sc, sc), with_ctx(s1 * sc, 0.0)], 0)
    j = np.arange(128)[:, None].astype(np.float32); i = np.arange(128)[None, :].astype(np.float32)
    cst = np.zeros((128, 770), np.float32)
    cst[:, 0:128] = np.maximum(i - j, 0); cst[:, 128:256] = (i >= j)
    cst[:, 256:384] = np.maximum(j - i, 0); cst[:, 384:512] = (j >= i)
    cst[:, 512:640] = i + 1; cst[:, 640:768] = 128 - i
    cst[:, 768] = 127 - j[:, 0]; cst[:, 769] = j[:, 0]
    aux = np.zeros((128, 128), np.float32)
    aux[:, 0] = p['ret_decay'][0, h]; aux[:, 1] = p['ret_decay'][1, h]
    aux[:, 64:128] = p['ret_gn'][h * 64:(h + 1) * 64][None, :]
    return dict(w=np.ascontiguousarray(w), tabs=np.ascontiguousarray(tabs), identb=np.eye(128).astype(BF), cst=cst, aux=aux)


def rwkv_inputs(h, w_in_l, p):
    hs = slice(h * 64, (h + 1) * 64)
    cols = np.concatenate([B_OFF + h * 64 + np.arange(64), B_OFF + 256 + h * 64 + np.arange(64), B_OFF + 512 + h * 64 + np.arange(64),
                           B_OFF + 768 + np.arange(64), B_OFF + 896 + np.arange(64),
                           B_OFF + 832 + np.arange(64), B_OFF + 960 + np.arange(64),
                           B_OFF + 1024 + np.arange(128)])
    w = np.ascontiguousarray(w_in_l[:, cols])
    mu = np.ascontiguousarray(np.broadcast_to(p['rwkv_mu'][cols - B_OFF][None, :], (128, 576))).astype(np.float32)
    aux = np.zeros((128, 576), np.float32)
    for i, v in enumerate([p['rwkv_w0'][0][hs], p['rwkv_w0'][1][hs], p['rwkv_a0'][0][hs], p['rwkv_a0'][1][hs], p['rwkv_kk'][hs],
                           p['rwkv_ka'][hs], p['rwkv_rk'][h], p['rwkv_lnx_g'][hs], p['rwkv_lnx_b'][hs]]):
        aux[:, i * 64:(i + 1) * 64] = v[None, :]
    wa = np.zeros((128, 2, 64), np.float32)
    for d in range(2):
        wa[0:64, d, :] = p['rwkv_w2'][d][:, hs]
        wa[64:128, d, :] = p['rwkv_a2'][d][:, hs]
    cst = np.zeros((128, 384), np.float32)
    cst[:, 0:128] = np.eye(128); cst[:, 128:256] = np.eye(128)[::-1]
    cst[0, 256:320] = 1.0; cst[1, 320:384] = 1.0
    return dict(w=w, mu=mu, aux=aux, w2a2=wa, g2h=np.ascontiguousarray(p['rwkv_g2'][:, hs]), cst=cst)


def flip_seq(hT):
    return np.ascontiguousarray(np.concatenate([hT[:, :256][:, ::-1], hT[:, 256:][:, ::-1]], 1))


def shift_pn(hT):
    P_ = np.zeros_like(hT); N_ = np.zeros_like(hT)
    for (a, b) in ((0, 256), (256, hT.shape[1])):
        P_[:, a + 1:b] = hT[:, a:b - 1]
        N_[:, a:b - 1] = hT[:, a + 1:b]
    return P_, N_


_PROGS = {}


def _prog(name, fn, *a):
    key = (name,) + a
    if key not in _PROGS:
        _PROGS[key] = fn(*a)
    return _PROGS[key]


def _run(nc, maps):
    res = run_bass_kernel_spmd(nc, maps, core_ids=list(range(8)))
    return res.results


def _colT(v, n):
    return np.ascontiguousarray(np.asarray(v, np.float32).reshape(n, 128).T)


def kernel(**inp):
    inp = {k: np.asarray(v) for k, v in inp.items()}
    x = inp['x'].astype(np.float32).copy()
    cx = inp['ctx'].astype(np.float32).copy()
    B, S, _ = x.shape
    L = 256 + S
    NTl = S // 512
    NT = NTl + 1
    c, c_ctx = inp['c'], inp['c_ctx']
    identb = np.eye(128).astype(BF)
    identf = np.eye(128, dtype=np.float32)
    cTs = [np.ascontiguousarray(np.stack([c[b], c_ctx], -1).reshape(8, 128, 2).transpose(1, 0, 2)).astype(np.float32) for b in range(B)]
    pnames = ['w_in', 'diff_lam_q', 'diff_lam_k', 'diff_subln', 'rwkv_mu', 'rwkv_w0', 'rwkv_w2', 'rwkv_a0', 'rwkv_a2', 'rwkv_g2',
              'rwkv_kk', 'rwkv_ka', 'rwkv_rk', 'rwkv_lnx_g', 'rwkv_lnx_b', 'ret_decay', 'ret_gn', 'win_sink', 'w_branch', 'w_out']
    depth = inp['w_in'].shape[0]

    def tok_of(core):
        b, j = core // 4, core % 4
        return b, j, slice(j * NTl * 128, (j + 1) * NTl * 128), slice((j % 2) * 128, (j % 2 + 1) * 128)

    def x_tok():
        out = []
        for core in range(8):
            b, j, ls, cs_ = tok_of(core)
            out.append(np.ascontiguousarray(np.concatenate([x[b, ls], cx[b, cs_]], 0)))
        return out

    def scatter_x(res, key):
        for core in range(8):
            b, j, ls, cs_ = tok_of(core)
            xo = np.asarray(res[core][key])
            x[b, ls] = xo[:NTl * 128]
            if j < 2:
                cx[b, cs_] = xo[NTl * 128:]

    for l in range(depth):
        p = {n: inp[n][l] for n in pnames}
        aw, ab = inp['ada_w'][l], inp['ada_b'][l]
        nc = _prog('T0', build_T0, NT)
        xt = x_tok()
        maps = [dict(x=xt[core], adaw=np.ascontiguousarray(aw[:, 0:2048]), adab=_colT(ab[0:2048], 16), cT=cTs[core // 4],
                     gT=_colT(inp['norm_pre_mix'][l], 8), ident=identb) for core in range(8)]
        res = _run(nc, maps)
        hT_own = [np.asarray(res[core]['hT']) for core in range(8)]
        hT_b = []
        for b in range(B):
            parts = [hT_own[b * 4 + 0][:, NTl * 128:], hT_own[b * 4 + 1][:, NTl * 128:]] + [hT_own[b * 4 + j][:, :NTl * 128] for j in range(4)]
            hT_b.append(np.ascontiguousarray(np.concatenate(parts, 1)))
        Y = [np.zeros((L, 4, 4, 64), BF) for _ in range(B)]
        for n, kind in enumerate(['A', 'B', 'C', 'D']):
            maps = []
            for core in range(8):
                b, h = core // 4, core % 4
                if kind in ('A', 'D'):
                    d = attn_inputs(kind, S, h, p['w_in'], p, l)
                    d['hT'] = hT_b[b]
                elif kind == 'C':
                    d = ret_inputs(S, h, p['w_in'], p)
                    d['hT'] = hT_b[b]
                else:
                    d = rwkv_inputs(h, p['w_in'], p)
                    d['hTf'] = hT_b[b]
                    d['hTb'] = flip_seq(hT_b[b])
                    for nm in ('f', 'b'):
                        d['hP' + nm], d['hN' + nm] = shift_pn(d['hT' + nm])
                maps.append(d)
            if kind in ('A', 'D'):
                nc = _prog('attn' + kind, build_attn, L, kind)
            elif kind == 'C':
                nc = _prog('ret', build_ret, L)
            else:
                nc = _prog('rwkv', build_rwkv, L)
            res = _run(nc, maps)
            for core in range(8):
                b, h = core // 4, core % 4
                Y[b][:, n, h, :] = np.asarray(res[core]['y'])
        nc = _prog('T1', build_T1, NT)
        xt = x_tok()
        maps = []
        for core in range(8):
            b, j, ls, cs_ = tok_of(core)
            Yb = Y[b].reshape(L, 1024)
            yT = np.ascontiguousarray(np.concatenate([Yb[256:][ls], Yb[:256][cs_]], 0).T)
            maps.append(dict(x=xt[core], hT=hT_own[core], yT=yT, wg=np.ascontiguousarray(p['w_in'][:, G_OFF:]),
                             wb=np.ascontiguousarray(p['w_branch'].reshape(1024, 1024)), wo=p['w_out'],
                             adaw=np.ascontiguousarray(aw[:, 2048:3072]), adab=_colT(ab[2048:3072], 8), cT=cTs[b],
                             gbc=np.ascontiguousarray(np.broadcast_to(inp['norm_post_mix'][l][None, :], (128, 1024))).astype(np.float32),
                             identf=identf))
        res = _run(nc, maps)
        scatter_x(res, 'xo')
        nc = _prog('T2', build_T2, NT)
        xt = x_tok()
        maps = []
        for core in range(8):
            b = core // 4
            maps.append(dict(x=xt[core], wu=inp['w_up'][l], wd=inp['w_down'][l], adaw=np.ascontiguousarray(aw[:, 3072:6144]),
                             adab=_colT(ab[3072:6144], 24), cT=cTs[b], gT=_colT(inp['norm_pre_mlp'][l], 8),
                             gbc=np.ascontiguousarray(np.broadcast_to(inp['norm_post_mlp'][l][None, :], (128, 1024))).astype(np.float32),
                             identf=identf, ident=identb))
        res = _run(nc, maps)
        scatter_x(res, 'xo')
    return x.astype(np.float32)
```

```python
import numpy as np
from contextlib import ExitStack
import concourse.bass as bass
import concourse.mybir as mybir
from concourse.bass_utils import run_bass_kernel_spmd

F32 = mybir.dt.float32
BF16 = mybir.dt.bfloat16
AF = mybir.ActivationFunctionType
ALU = mybir.AluOpType
AX = mybir.AxisListType


class Prog:
    ENGS = ['tensor', 'vector', 'scalar', 'gpsimd', 'sync']
    NDMA = 6

    def __init__(self, same_engine_sync=True):
        self.nc = bass.Bass("TRN2", target_bir_lowering=False)
        self.stack = ExitStack()
        self.ops = []
        self.last_w = {}
        self.readers = {}
        self.same_engine_sync = same_engine_sync
        self.n_un = 0

    def dram(self, name, shape, dt, kind):
        return self.nc.dram_tensor(name, list(shape), dt, kind=kind).ap()

    def sb(self, name, shape, dt=F32):
        return self.stack.enter_context(self.nc.sbuf_tensor('sb_' + name, list(shape), dt))

    def ps(self, name, shape, dt=F32):
        return self.stack.enter_context(self.nc.psum_tensor('ps_' + name, list(shape), dt))

    @staticmethod
    def _key(x):
        if isinstance(x, str):
            return (x, None)
        if isinstance(x, tuple):
            return (Prog._key(x[0])[0], str(x[1]))
        return (x.tensor.name if hasattr(x, 'tensor') else x.name, None)

    @staticmethod
    def _conf(table, base, sub):
        d = table.get(base)
        if not d:
            return []
        if sub is None:
            return list(d.values())
        return [v for s_, v in d.items() if s_ is None or s_ == sub]

    def op(self, eng, fn, r=(), w=(), dma=False, out_dram=False):
        idx = len(self.ops)
        deps = set()
        rk = [self._key(k) for k in r]
        wk = [self._key(k) for k in w]
        for (b, s_) in rk:
            deps.update(self._conf(self.last_w, b, s_))
        for (b, s_) in wk:
            deps.update(self._conf(self.last_w, b, s_))
            for lst in self._conf(self.readers, b, s_):
                deps.update(lst)
        for (b, s_) in rk:
            self.readers.setdefault(b, {}).setdefault(s_, []).append(idx)
        for (b, s_) in wk:
            if s_ is None:
                self.last_w[b] = {None: idx}
                self.readers[b] = {}
            else:
                self.last_w.setdefault(b, {})[s_] = idx
                rd = self.readers.setdefault(b, {})
                rd[s_] = []
        deps.discard(idx)
        self.ops.append(dict(eng=eng, fn=fn, deps=deps, dma=dma, signal=False, out_dram=out_dram))
        return idx

    def mm(self, out, lhsT, rhs, start=True, stop=True, r=None, w=None):
        self.op('tensor', lambda e: e.matmul(out, lhsT, rhs, start=start, stop=stop),
                r if r is not None else [lhsT, rhs], w if w is not None else [out])

    def act(self, out, in_, func, scale=1.0, bias=0.0, accum=None, r=None, w=None, extra_r=()):
        rr = list(r) if r is not None else [in_]
        for x in (scale, bias):
            if not isinstance(x, (int, float)):
                rr.append(x)
        rr += list(extra_r)
        ww = list(w) if w is not None else [out]
        if accum is not None:
            ww.append(accum)
        self.op('scalar', lambda e: e.activation(out, in_, func, bias=bias, scale=scale, accum_out=accum), rr, ww)

    def tt(self, out, in0, in1, op, eng='vector', r=None, w=None):
        self.op(eng, lambda e: e.tensor_tensor(out, in0, in1, op),
                r if r is not None else [in0, in1], w if w is not None else [out])

    def ts(self, out, in0, s1, s2, op0, op1=None, eng='vector', accum=None, r=None, w=None):
        rr = list(r) if r is not None else [in0]
        for x in (s1, s2):
            if x is not None and not isinstance(x, (int, float)):
                rr.append(x)
        ww = list(w) if w is not None else [out]
        if accum is not None:
            ww.append(accum)
        if op1 is None:
            self.op(eng, lambda e: e.tensor_scalar(out, in0, s1, None, op0), rr, ww)
        else:
            self.op(eng, lambda e: e.tensor_scalar(out, in0, s1, s2, op0, op1, accum_out=accum), rr, ww)

    def stt(self, out, in0, scalar, in1, op0, op1, accum=None, r=None, w=None):
        rr = list(r) if r is not None else [in0, in1]
        if not isinstance(scalar, (int, float)):
            rr.append(scalar)
        ww = list(w) if w is not None else [out]
        if accum is not None:
            ww.append(accum)
        self.op('vector', lambda e: e.scalar_tensor_tensor(out, in0, scalar, in1, op0, op1, accum_out=accum), rr, ww)

    def ttr(self, out, in0, in1, op0, op1, accum, r=None, w=None):
        rr = r if r is not None else [in0, in1]
        ww = w if w is not None else [out, accum]
        self.op('vector', lambda e: e.tensor_tensor(out, in0, in1, op0), rr, ww)
        self.op('vector', lambda e: e.tensor_reduce(accum, out, AX.X, op1), list(ww), list(ww))

    def copy(self, out, in_, eng='vector', r=None, w=None):
        if eng == 'scalar':
            self.op('scalar', lambda e: e.copy(out, in_), r if r is not None else [in_], w if w is not None else [out])
        else:
            self.op(eng, lambda e: e.tensor_copy(out, in_), r if r is not None else [in_], w if w is not None else [out])

    def memset(self, ap, val, eng='vector', w=None):
        self.op(eng, lambda e: e.memset(ap, val), [], w if w is not None else [ap])

    def recip(self, out, in_, r=None, w=None):
        self.op('vector', lambda e: e.reciprocal(out, in_), r if r is not None else [in_], w if w is not None else [out])

    def reduce(self, out, in_, op=None, r=None, w=None):
        op = op or ALU.add
        self.op('vector', lambda e: e.tensor_reduce(out, in_, AX.X, op), r if r is not None else [in_],
                w if w is not None else [out])

    def dma(self, out, in_, q='sync', r=None, w=None, out_dram=False, **kw):
        self.op(q, lambda e: e.dma_start(out=out, in_=in_, **kw), r if r is not None else [in_],
                w if w is not None else [out], dma=True, out_dram=out_dram)

    def barrier(self, scratch):
        idx = len(self.ops)
        deps = set(range(idx))
        self.ops.append(dict(eng='vector', fn=lambda e: e.memset(scratch, 0.0), deps=deps, dma=False,
                             signal=False, out_dram=False, barrier=True))
        for b in set(list(self.last_w.keys()) + list(self.readers.keys())):
            self.last_w[b] = {None: idx}
            self.readers[b] = {}

    def emit(self):
        nc = self.nc
        ops = self.ops
        for i, o in enumerate(ops):
            for j in o['deps']:
                pj = ops[j]
                if pj['dma']:
                    continue
                if pj['eng'] == o['eng'] and (o['eng'] == 'tensor' or not self.same_engine_sync):
                    continue
                pj['signal'] = True
        cnt = {e: 0 for e in self.ENGS}
        for o in ops:
            if o['dma']:
                continue
            if o['signal']:
                cnt[o['eng']] += 1
            o['cnt'] = cnt[o['eng']]
        dma_seq = {e: 0 for e in self.ENGS}
        dma_tot = {}
        prev_on_sem = {}
        for i, o in enumerate(ops):
            if not o['dma']:
                continue
            q = o['eng']
            s = (q, dma_seq[q] % self.NDMA)
            dma_seq[q] += 1
            dma_tot[s] = dma_tot.get(s, 0) + 16
            o['dsem'] = s
            o['dval'] = dma_tot[s]
            o['dprev'] = prev_on_sem.get(s)
            prev_on_sem[s] = i
        with ExitStack() as st:
            esem = {e: st.enter_context(nc.semaphore('s_' + e)) for e in self.ENGS}
            dsem = {}
            for q in self.ENGS:
                if dma_seq[q] > 0:
                    for k in range(self.NDMA):
                        dsem[(q, k)] = st.enter_context(nc.semaphore('d_%s_%d' % (q, k)))
            block = st.enter_context(nc.Block())

            def run_engine(ename, e):
                waited = {}
                for i, o in enumerate(ops):
                    if o['eng'] != ename:
                        continue
                    need = {}
                    for j in o['deps']:
                        pj = ops[j]
                        if pj['dma']:
                            key = ('d', pj['dsem'])
                            val = pj['dval']
                        else:
                            if pj['eng'] == ename and (ename == 'tensor' or not self.same_engine_sync):
                                continue
                            key = ('e', pj['eng'])
                            val = pj['cnt']
                        if val > need.get(key, 0):
                            need[key] = val
                    if o['dma'] and o['dprev'] is not None:
                        key = ('d', o['dsem'])
                        val = ops[o['dprev']]['dval']
                        if val > need.get(key, 0):
                            need[key] = val
                    for key, val in need.items():
                        if waited.get(key, 0) >= val:
                            continue
                        waited[key] = val
                        sem = dsem[key[1]] if key[0] == 'd' else esem[key[1]]
                        e.wait_ge(sem, val)
                    ins = o['fn'](e)
                    if o['dma']:
                        ins.then_inc(dsem[o['dsem']], 16)
                    elif o['signal']:
                        ins.then_inc(esem[ename], 1)
                for s, tot in dma_tot.items():
                    if s[0] == ename and waited.get(('d', s), 0) < tot:
                        e.wait_ge(dsem[s], tot)

            @block.tensor
            def _(e):
                run_engine('tensor', e)

            @block.vector
            def _(e):
                run_engine('vector', e)

            @block.scalar
            def _(e):
                run_engine('scalar', e)

            @block.gpsimd
            def _(e):
                run_engine('gpsimd', e)

            @block.sync
            def _(e):
                run_engine('sync', e)
        self.stack.close()
        return nc


D = 1024
KC = 8


def load_w_bf16(P, name, dram_ap, K, N, q='gpsimd'):
    kc = K // 128
    t = P.sb(name, [128, kc, N], BF16)
    src = dram_ap.rearrange('(k p) n -> p k n', p=128)
    for k in range(kc):
        for n0 in range(0, N, 2048):
            n1 = min(N, n0 + 2048)
            P.dma(t[:, k, n0:n1], src[:, k, n0:n1], q='gpsimd', w=[(t, k)])
    return t


def mod_setup(P, adaw_d, adab_d, cT_d, nch, pfx):
    aw = P.sb(pfx + 'aw', [128, KC, nch * D], F32)
    src = adaw_d.rearrange('(k p) n -> p k n', p=128)
    for k in range(KC):
        P.dma(aw[:, k, :], src[:, k, :], q='sync', w=[(aw, k)])
    ab = P.sb(pfx + 'ab', [128, nch * 8], F32)
    P.dma(ab[:], adab_d, q='sync')
    cT = P.sb(pfx + 'cT', [128, KC, 2], F32)
    P.dma(cT[:], cT_d, q='sync')
    sc = P.sb(pfx + 'sc', [128, KC, 2], F32)
    P.act(sc[:], cT[:], AF.Silu)
    modT = P.sb(pfx + 'modT', [128, nch * 8, 2], F32)
    pm = P.ps(pfx + 'pm', [128, 2], F32)
    for j in range(nch * 8):
        for k in range(KC):
            P.mm(pm[:], aw[:, k, j * 128:(j + 1) * 128], sc[:, k, :], start=(k == 0), stop=(k == KC - 1),
                 r=[(aw, k), sc], w=[pm])
        P.ts(modT[:, j, :], pm[:], ab[:, j:j + 1], None, ALU.add, r=[pm, ab], w=[(modT, j)])
    return modT


def norm_to_hT(P, xt, hT_out, scaleT, shiftT, col, ident, bufs, i):
    junk, ss, xn, pT = bufs
    P.act(junk[:], xt, AF.Square, accum=ss[:, 0:1], w=[junk])
    P.act(ss[:, 1:2], ss[:, 0:1], AF.Sqrt, scale=1.0 / D, bias=1e-6, r=[ss], w=[ss])
    P.recip(ss[:, 2:3], ss[:, 1:2], r=[ss], w=[ss])
    P.ts(xn[:], xt, ss[:, 2:3], None, ALU.mult, r=[xt, ss], w=[xn])
    for k in range(KC):
        pt = pT[k % 2]
        P.op('tensor', lambda e, pt=pt, k=k: e.transpose(pt[:], xn[:, k * 128:(k + 1) * 128], ident[:]),
             [xn, ident], [pt])
        P.ts(hT_out[:, k, :], pt[:], scaleT[:, k, col:col + 1], shiftT[:, k, col:col + 1], ALU.mult, ALU.add,
             r=[pt, scaleT, shiftT], w=[hT_out])


def build_T0(NT):
    P = Prog()
    x_d = P.dram('x', [NT * 128, D], F32, 'ExternalInput')
    adaw_d = P.dram('adaw', [D, 2 * D], F32, 'ExternalInput')
    adab_d = P.dram('adab', [128, 16], F32, 'ExternalInput')
    cT_d = P.dram('cT', [128, KC, 2], F32, 'ExternalInput')
    g_d = P.dram('gT', [128, KC], F32, 'ExternalInput')
    id_d = P.dram('ident', [128, 128], BF16, 'ExternalInput')
    hT_d = P.dram('hT', [D, NT * 128], BF16, 'ExternalOutput')
    emit_T0(P, x_d, adaw_d, adab_d, cT_d, g_d, id_d, hT_d, NT)
    return P.emit()


def emit_T0(P, x_d, adaw_d, adab_d, cT_d, g_d, id_d, hT_d, NT, pfx='t0', x_sb_tiles=None):
    ident = P.sb(pfx + 'ident', [128, 128], BF16)
    P.dma(ident[:], id_d, q='sync')
    gT = P.sb(pfx + 'gT', [128, KC], F32)
    P.dma(gT[:], g_d, q='sync')
    modT = mod_setup(P, adaw_d, adab_d, cT_d, 2, pfx)
    scaleT = P.sb(pfx + 'scaleT', [128, KC, 2], F32)
    shiftT = P.sb(pfx + 'shiftT', [128, KC, 2], F32)
    for c in range(2):
        P.stt(scaleT[:, :, c], modT[:, 8:16, c], 1.0, gT[:], ALU.add, ALU.mult, r=[modT, gT], w=[scaleT])
        P.copy(shiftT[:, :, c], modT[:, 0:8, c], r=[modT], w=[shiftT])
    bufs = (P.sb(pfx + 'junk', [128, D], F32), P.sb(pfx + 'ss', [128, 4], F32), P.sb(pfx + 'xn', [128, D], BF16),
            [P.ps(pfx + 'pT%d' % i, [128, 128], BF16) for i in range(2)])
    xts = [P.sb(pfx + 'x%d' % i, [128, D], F32) for i in range(2)]
    hts = [P.sb(pfx + 'h%d' % i, [128, KC, 128], BF16) for i in range(2)]
    hT_v = hT_d.rearrange('(k p) t -> p k t', p=128)
    for i in range(NT):
        if x_sb_tiles is None:
            xt = xts[i % 2]
            P.dma(xt[:], x_d[i * 128:(i + 1) * 128, :], q='sync')
            xa = xt[:]
        else:
            xa = x_sb_tiles(i)
        ht = hts[i % 2]
        norm_to_hT(P, xa, ht, scaleT, shiftT, 1 if i == NT - 1 else 0, ident, bufs, i)
        P.dma(hT_v[:, :, i * 128:(i + 1) * 128], ht[:], q='sync', out_dram=True, w=['hT_d'])


def mod_stream(P, adaw_d, adab_d, cT_d, nch, pfx):
    awb = [P.sb(pfx + 'awb%d' % i, [128, KC, 512], F32) for i in range(1)]
    src = adaw_d.rearrange('(k p) n -> p k n', p=128)
    ab = P.sb(pfx + 'ab', [128, nch * 8], F32)
    P.dma(ab[:], adab_d, q='sync')
    cT = P.sb(pfx + 'cT', [128, KC, 2], F32)
    P.dma(cT[:], cT_d, q='sync')
    sc = P.sb(pfx + 'sc', [128, KC, 2], F32)
    P.act(sc[:], cT[:], AF.Silu)
    modT = P.sb(pfx + 'modT', [128, nch * 8, 2], F32)
    pm = P.ps(pfx + 'pm', [128, 2], F32)
    for hc in range(nch * 2):
        aw = awb[0]
        P.dma(aw[:], src[:, :, hc * 512:(hc + 1) * 512], q='sync', w=[aw])
        for jj in range(4):
            j = hc * 4 + jj
            for k in range(KC):
                P.mm(pm[:], aw[:, k, jj * 128:(jj + 1) * 128], sc[:, k, :], start=(k == 0), stop=(k == KC - 1),
                     r=[aw, sc], w=[pm])
            P.ts(modT[:, j, :], pm[:], ab[:, j:j + 1], None, ALU.add, r=[pm, ab], w=[(modT, j)])
    return modT


def bcast_rows(P, modT, j0, gbc_d, identf, pfx):
    gb = P.sb(pfx + 'gb', [128, D], F32)
    P.dma(gb[:], gbc_d, q='sync')
    ones = P.sb(pfx + 'ones', [128, 128], F32)
    P.memset(ones[:], 1.0)
    mk = P.sb(pfx + 'mk', [128, 128], F32)
    pb = P.ps(pfx + 'pb', [128, 128], F32)
    G = [P.sb(pfx + 'G%d' % c, [128, D], F32) for c in range(2)]
    for c in range(2):
        for k in range(KC):
            P.ts(mk[:], ones[:], modT[:, j0 + k, c:c + 1], None, ALU.mult, r=[ones, modT], w=[mk])
            P.mm(pb[:], mk[:], identf[:], r=[mk, identf], w=[pb])
            P.tt(G[c][:, k * 128:(k + 1) * 128], pb[:], gb[:, k * 128:(k + 1) * 128], ALU.mult, r=[pb, gb], w=[G[c]])
    return G


def resid_epilogue(P, ps_halves, xt, G, out_d_rows, bufs, q='sync'):
    mo, junk, ss, tmp, xo = bufs
    for hf in range(2):
        P.copy(mo[:, hf * 512:(hf + 1) * 512], ps_halves[hf][:], eng='vector' if hf == 0 else 'scalar', r=[ps_halves[hf]], w=[mo])
    P.act(junk[:], mo[:], AF.Square, accum=ss[:, 0:1], r=[mo], w=[junk, ss])
    P.act(ss[:, 1:2], ss[:, 0:1], AF.Sqrt, scale=1.0 / D, bias=1e-6, r=[ss], w=[ss])
    P.recip(ss[:, 2:3], ss[:, 1:2], r=[ss], w=[ss])
    P.stt(tmp[:], mo[:], ss[:, 2:3], G[:], ALU.mult, ALU.mult, r=[mo, ss, G], w=[tmp])
    P.tt(xo[:], xt[:], tmp[:], ALU.add, r=[xt, tmp], w=[xo])
    P.dma(out_d_rows, xo[:], q=q, out_dram=True, w=['xout'])


def groups_of(NT, gs):
    g = []
    t = 0
    while t < NT - 1:
        n = min(gs, NT - 1 - t)
        g.append((t, n, 0))
        t += n
    g.append((NT - 1, 1, 1))
    return g


def build_T1(NT):
    P = Prog()
    TT = NT * 128
    x_d = P.dram('x', [TT, D], F32, 'ExternalInput')
    hT_d = P.dram('hT', [D, TT], BF16, 'ExternalInput')
    yT_d = P.dram('yT', [D, TT], BF16, 'ExternalInput')
    wg_d = P.dram('wg', [D, 4096], F32, 'ExternalInput')
    wb_d = P.dram('wb', [D, D], F32, 'ExternalInput')
    wo_d = P.dram('wo', [D, D], F32, 'ExternalInput')
    adaw_d = P.dram('adaw', [D, D], F32, 'ExternalInput')
    adab_d = P.dram('adab', [128, 8], F32, 'ExternalInput')
    cT_d = P.dram('cT', [128, KC, 2], F32, 'ExternalInput')
    gbc_d = P.dram('gbc', [128, D], F32, 'ExternalInput')
    idf_d = P.dram('identf', [128, 128], F32, 'ExternalInput')
    xo_d = P.dram('xo', [TT, D], F32, 'ExternalOutput')
    identf = P.sb('identf', [128, 128], F32)
    P.dma(identf[:], idf_d)
    Wg = load_w_bf16(P, 'Wg', wg_d, D, 4096)
    Wb = load_w_bf16(P, 'Wb', wb_d, D, D)
    Wo = load_w_bf16(P, 'Wo', wo_d, D, D)
    modT = mod_stream(P, adaw_d, adab_d, cT_d, 1, 't1')
    G = bcast_rows(P, modT, 0, gbc_d, identf, 't1')
    hv = hT_d.rearrange('(k p) t -> p k t', p=128)
    yv = yT_d.rearrange('(k p) t -> p k t', p=128)
    hbs = [P.sb('hb%d' % i, [128, KC, 512], BF16) for i in range(1)]
    ybs = [P.sb('yb%d' % i, [128, KC, 512], BF16) for i in range(1)]
    mT = P.sb('mT', [128, KC, 512], BF16)
    macc = P.sb('macc', [128, 512], F32)
    sig = P.sb('sig', [128, 512], F32)
    tmpm = P.sb('tmpm', [128, 512], F32)
    pg = [P.ps('pg%d' % i, [128, 512], F32) for i in range(2)]
    pl = [P.ps('pl%d' % i, [128, 512], F32) for i in range(2)]
    po = [P.ps('po%d' % i, [128, 512], F32) for i in range(2)]
    xts = [P.sb('xt%d' % i, [128, D], F32) for i in range(2)]
    bufs = (P.sb('mo', [128, D], F32), P.sb('junk', [128, D], F32), P.sb('ss', [128, 4], F32), P.sb('tmp', [128, D], F32),
            P.sb('xo', [128, D], F32))
    it = 0
    for gi, (t0, ntile, c) in enumerate(groups_of(NT, 4)):
        n = ntile * 128
        hb = hbs[0]; yb = ybs[0]
        P.dma(hb[:, :, :n], hv[:, :, t0 * 128:t0 * 128 + n], q='sync', w=[hb])
        P.dma(yb[:, :, :n], yv[:, :, t0 * 128:t0 * 128 + n], q='sync', w=[yb])
        for dc in range(KC):
            for nb in range(4):
                g_ = pg[it % 2]; l_ = pl[it % 2]
                it += 1
                col = nb * 1024 + dc * 128
                for k in range(KC):
                    P.mm(g_[:, :n], Wg[:, k, col:col + 128], hb[:, k, :n], start=(k == 0), stop=(k == KC - 1), r=[Wg, hb], w=[g_])
                for cc in range(2):
                    P.mm(l_[:, :n], Wb[:, nb * 2 + cc, dc * 128:(dc + 1) * 128], yb[:, nb * 2 + cc, :n], start=(cc == 0), stop=(cc == 1),
                         r=[Wb, yb], w=[l_])
                P.copy(sig[:, :n], g_[:, :n], eng='vector', r=[g_], w=[sig])
                P.act(sig[:, :n], sig[:, :n], AF.Sigmoid, r=[sig], w=[sig])
                if nb == 0:
                    P.tt(macc[:, :n], sig[:, :n], l_[:, :n], ALU.mult, r=[sig, l_], w=[macc])
                else:
                    P.tt(tmpm[:, :n], sig[:, :n], l_[:, :n], ALU.mult, r=[sig, l_], w=[tmpm])
                    if nb < 3:
                        P.tt(macc[:, :n], macc[:, :n], tmpm[:, :n], ALU.add, r=[macc, tmpm], w=[macc])
                    else:
                        P.tt(mT[:, dc, :n], macc[:, :n], tmpm[:, :n], ALU.add, r=[macc, tmpm], w=[mT])
        for j in range(ntile):
            t = t0 + j
            xt = xts[t % 2]
            P.dma(xt[:], x_d[t * 128:(t + 1) * 128, :], q='sync')
            for hf in range(2):
                for dc in range(KC):
                    P.mm(po[hf][:], mT[:, dc, j * 128:(j + 1) * 128], Wo[:, dc, hf * 512:(hf + 1) * 512], start=(dc == 0), stop=(dc == KC - 1),
                         r=[mT, Wo], w=[po[hf]])
            resid_epilogue(P, po, xt, G[c], xo_d[t * 128:(t + 1) * 128, :], bufs)
    return P.emit()


def build_T2(NT):
    P = Prog()
    TT = NT * 128
    x_d = P.dram('x', [TT, D], F32, 'ExternalInput')
    wu_d = P.dram('wu', [D, 4096], F32, 'ExternalInput')
    wd_d = P.dram('wd', [4096, D], F32, 'ExternalInput')
    adaw_d = P.dram('adaw', [D, 3 * D], F32, 'ExternalInput')
    adab_d = P.dram('adab', [128, 24], F32, 'ExternalInput')
    cT_d = P.dram('cT', [128, KC, 2], F32, 'ExternalInput')
    g_d = P.dram('gT', [128, KC], F32, 'ExternalInput')
    gbc_d = P.dram('gbc', [128, D], F32, 'ExternalInput')
    idf_d = P.dram('identf', [128, 128], F32, 'ExternalInput')
    id_d = P.dram('ident', [128, 128], BF16, 'ExternalInput')
    xo_d = P.dram('xo', [TT, D], F32, 'ExternalOutput')
    identf = P.sb('identf', [128, 128], F32)
    P.dma(identf[:], idf_d)
    ident = P.sb('identb', [128, 128], BF16)
    P.dma(ident[:], id_d)
    gT = P.sb('gT', [128, KC], F32)
    P.dma(gT[:], g_d)
    Wu = load_w_bf16(P, 'Wu', wu_d, D, 4096)
    Wd = load_w_bf16(P, 'Wd', wd_d, 4096, D)
    modT = mod_stream(P, adaw_d, adab_d, cT_d, 3, 't2')
    G = bcast_rows(P, modT, 16, gbc_d, identf, 't2')
    scaleT = P.sb('scaleT', [128, KC, 2], F32)
    shiftT = P.sb('shiftT', [128, KC, 2], F32)
    for c in range(2):
        P.stt(scaleT[:, :, c], modT[:, 8:16, c], 1.0, gT[:], ALU.add, ALU.mult, r=[modT, gT], w=[scaleT])
        P.copy(shiftT[:, :, c], modT[:, 0:8, c], r=[modT], w=[shiftT])
    nbufs = (P.sb('njunk', [128, D], F32), P.sb('nss', [128, 4], F32), P.sb('nxn', [128, D], BF16),
             [P.ps('npT%d' % i, [128, 128], BF16) for i in range(2)])
    h2 = P.sb('h2', [128, KC, 256], BF16)
    aT = P.sb('aT', [128, 32, 256], BF16)
    rl = P.sb('rl', [128, 256], F32)
    pu = [P.ps('pu%d' % i, [128, 256], F32) for i in range(2)]
    po = [P.ps('po%d' % i, [128, 512], F32) for i in range(2)]
    xts = [P.sb('xt%d' % i, [128, D], F32) for i in range(2)]
    tmp_ = P.sb('tmp', [128, D], F32)
    bufs = (P.sb('mo', [128, D], F32), nbufs[0], P.sb('ss', [128, 4], F32), tmp_, tmp_)
    for gi, (t0, ntile, c) in enumerate(groups_of(NT, 2)):
        n = ntile * 128
        for j in range(ntile):
            t = t0 + j
            xt = xts[j]
            P.dma(xt[:], x_d[t * 128:(t + 1) * 128, :], q='sync')
            norm_to_hT(P, xt[:], h2[:, :, j * 128:(j + 1) * 128], scaleT, shiftT, c, ident, nbufs, t)
        for ff in range(32):
            p_ = pu[ff % 2]
            for k in range(KC):
                P.mm(p_[:, :n], Wu[:, k, ff * 128:(ff + 1) * 128], h2[:, k, :n], start=(k == 0), stop=(k == KC - 1), r=[Wu, h2], w=[p_])
            P.ts(rl[:, :n], p_[:, :n], 0.0, None, ALU.max, r=[p_], w=[rl])
            P.tt(aT[:, ff, :n], rl[:, :n], rl[:, :n], ALU.mult, r=[rl], w=[aT])
        for j in range(ntile):
            t = t0 + j
            for hf in range(2):
                for ff in range(32):
                    P.mm(po[hf][:], aT[:, ff, j * 128:(j + 1) * 128], Wd[:, ff, hf * 512:(hf + 1) * 512], start=(ff == 0), stop=(ff == 31),
                         r=[aT, Wd], w=[po[hf]])
            resid_epilogue(P, po, xts[j], G[c], xo_d[t * 128:(t + 1) * 128, :], bufs)
    return P.emit()


D = 1024
KC = 8


def chunks_of(L):
    ch = [(0, 256)]
    t = 256
    while t < L:
        ch.append((t, 512))
        t += 512
    return ch


class HLoader:
    def __init__(self, P, hT_d, pfx='hl'):
        self.P = P
        self.v = hT_d.rearrange('(k p) t -> p k t', p=128)
        self.bufs = [P.sb(pfx + 'hb%d' % i, [128, KC, 512], BF16) for i in range(2)]
        self.i = 0

    def load(self, t0, n):
        b = self.bufs[self.i % 2]
        self.i += 1
        self.P.dma(b[:, :, :n], self.v[:, :, t0:t0 + n], q='sync', w=[b])
        return b


def build_attn(L, kind):
    P = Prog()
    NTL = L // 128
    hT_d = P.dram('hT', [D, L], BF16, 'ExternalInput')
    w_d = P.dram('w', [D, 320], F32, 'ExternalInput')
    cos_d = P.dram('cosT', [64, L], F32, 'ExternalInput')
    sin_d = P.dram('sinT', [64, L], F32, 'ExternalInput')
    id_d = P.dram('ident', [128, 128], F32, 'ExternalInput')
    aux_d = P.dram('aux', [128, 256], F32, 'ExternalInput')
    y_d = P.dram('y', [L, 64], BF16, 'ExternalOutput')
    if kind == 'D':
        mk_d = P.dram('masks', [128, 2, 128], F32, 'ExternalInput')

    W = load_w_bf16(P, 'W', w_d, D, 320)
    ident = P.sb('ident', [128, 128], F32)
    P.dma(ident[:], id_d)
    aux = P.sb('aux', [128, 256], F32)
    P.dma(aux[:], aux_d)
    QT = P.sb('QT', [64, L], BF16)
    KT = P.sb('KT', [64, L], BF16)
    Va = P.sb('Va', [128, NTL, 65], BF16)
    P.memset(Va[:, :, 64:65], 1.0)
    pa = [P.ps('pa%d' % i, [128, 512], F32) for i in range(2)]
    acc = [P.ps('acc%d' % i, [128, 512], F32) for i in range(2)]
    pv = [P.ps('pv%d' % i, [128, 64], F32) for i in range(2)]
    ptr = [P.ps('ptr%d' % i, [128, 65], F32) for i in range(2)]
    cs = [P.sb('cs%d' % i, [64, 512], F32) for i in range(2)]
    sn = [P.sb('sn%d' % i, [64, 512], F32) for i in range(2)]
    t1 = P.sb('t1', [64, 512], F32)
    t2 = P.sb('t2', [64, 512], F32)
    hl = HLoader(P, hT_d)
    for ci, (t0, n) in enumerate(chunks_of(L)):
        hb = hl.load(t0, n)
        c_ = cs[ci % 2]
        s_ = sn[ci % 2]
        P.dma(c_[:, :n], cos_d[:, t0:t0 + n], q='sync')
        P.dma(s_[:, :n], sin_d[:, t0:t0 + n], q='sync')
        for which, dst in ((0, QT), (2, KT)):
            for k in range(KC):
                P.mm(pa[0][0:64, :n], W[:, k, which * 64:(which + 1) * 64], hb[:, k, :n], start=(k == 0), stop=(k == KC - 1),
                     r=[W, hb], w=[pa[0]])
            for k in range(KC):
                P.mm(pa[1][0:64, :n], W[:, k, (which + 1) * 64:(which + 2) * 64], hb[:, k, :n], start=(k == 0), stop=(k == KC - 1),
                     r=[W, hb], w=[pa[1]])
            P.tt(t1[:, :n], pa[0][0:64, :n], c_[:, :n], ALU.mult)
            P.tt(t2[:, :n], pa[1][0:64, :n], s_[:, :n], ALU.mult)
            P.tt(dst[:, t0:t0 + n], t1[:, :n], t2[:, :n], ALU.add, w=[(dst, ci)])
        for j in range(n // 128):
            p_ = pv[j % 2]
            for k in range(KC):
                P.mm(p_[:], hb[:, k, j * 128:(j + 1) * 128], W[:, k, 256:320], start=(k == 0), stop=(k == KC - 1),
                     r=[W, hb], w=[p_])
            P.copy(Va[:, t0 // 128 + j, 0:64], p_[:], eng='scalar', w=[(Va, t0 // 128 + j)])

    sm = P.sb('sm', [128, 16], F32)
    junk = P.sb('junk', [128, 64], F32)
    subA = P.sb('subA', [128, 64], F32)
    if kind == 'A':
        P.ttr(junk[:, 0:32], aux[:, 0:32], aux[:, 64:96], ALU.mult, ALU.add, sm[:, 0:1], r=[aux], w=[junk, (sm, 0)])
        P.ttr(junk[:, 32:64], aux[:, 32:64], aux[:, 96:128], ALU.mult, ALU.add, sm[:, 1:2], r=[aux, (sm, 0)], w=[junk, (sm, 1)])
        P.act(sm[:, 2:4], sm[:, 0:2], AF.Exp, r=[(sm, 1)], w=[(sm, 2)])
        P.tt(sm[:, 4:5], sm[:, 2:3], sm[:, 3:4], ALU.subtract, r=[(sm, 2)], w=[(sm, 4)])
        P.ts(sm[:, 5:6], sm[:, 4:5], aux[:, 192:193], -1.0, ALU.add, ALU.mult, r=[(sm, 4), aux], w=[(sm, 5)])
        P.ts(subA[:], aux[:, 128:192], aux[:, 193:194], None, ALU.mult, r=[aux], w=[subA])
        comps = [(0, 32), (32, 64)]
        scale = 32 ** -0.5
        QW = 512
    else:
        P.act(sm[:, 6:7], aux[:, 194:195], AF.Exp, r=[aux], w=[(sm, 6)])
        mk = P.sb('mk', [128, 2, 128], F32)
        P.dma(mk[:], mk_d)
        comps = [(0, 64)]
        scale = 64 ** -0.5
        QW = 128
    neglam = sm[:, 5:6]
    esink = sm[:, 6:7]

    pT = [P.sb('pT%d' % i, [128, 512], BF16) for i in range(3)]
    accS = [P.sb('accS%d' % i, [65, 512], F32) for i in range(2)]
    o1 = P.sb('o1', [128, 64], F32)
    dd = P.sb('dd', [128, 64], F32)
    fs = P.sb('fs', [128, 8], F32)
    yt = [P.sb('yt%d' % i, [128, 64], BF16) for i in range(2)]
    if kind == 'A':
        qch = [(0, 256, [0, 1])] + [(t0, 512, list(range(NTL))) for (t0, n) in chunks_of(L)[1:]]
    else:
        qch = []
        for g in range(NTL):
            if g < 2:
                qch.append((g * 128, 128, [(0, None), (1, None)]))
            else:
                ks = [(0, None), (1, None)]
                if g > 2:
                    ks.append((g - 1, 0))
                ks.append((g, None))
                if g < NTL - 1:
                    ks.append((g + 1, 1))
                qch.append((g * 128, 128, ks))
    it = 0
    yi = 0
    for (q0, nq, ks) in qch:
        for c, (c0, c1) in enumerate(comps):
            ksl = [(kk if kind == 'D' else (kk, None)) for kk in ks]
            bufs_ = []
            for ti in range(len(ksl)):
                bufs_.append((pa[it % 2], pT[it % 3]))
                it += 1

            def mm1(ti):
                kt = ksl[ti][0]
                s_ = bufs_[ti][0]
                P.mm(s_[:, :nq], KT[c0:c1, kt * 128:(kt + 1) * 128], QT[c0:c1, q0:q0 + nq], r=[KT, QT], w=[s_])
            mm1(0)
            for ti, (kt, mi) in enumerate(ksl):
                s_, p_ = bufs_[ti]
                P.act(p_[:, :nq], s_[:, :nq], AF.Exp, scale=scale)
                if mi is not None:
                    P.tt(p_[:, :nq], p_[:, :nq], mk[:, mi, :], ALU.mult, r=[p_, mk], w=[p_])
                if ti + 1 < len(ksl):
                    mm1(ti + 1)
                P.mm(acc[c][0:65, :nq], Va[:, kt, :], p_[:, :nq], start=(ti == 0), stop=(ti == len(ksl) - 1),
                     r=[Va, p_], w=[acc[c]])
            P.copy(accS[c][:, :nq], acc[c][0:65, :nq], eng='vector')
        for j in range(nq // 128):
            for c in range(len(comps)):
                P.mm(ptr[c][:], accS[c][:, j * 128:(j + 1) * 128], ident[0:65, 0:65], r=[accS[c], ident], w=[ptr[c]])
            y_ = yt[yi % 2]
            yi += 1
            if kind == 'A':
                P.recip(fs[:, 0:1], ptr[0][:, 64:65], r=[ptr[0]], w=[(fs, 0)])
                P.ts(o1[:], ptr[0][:, 0:64], fs[:, 0:1], None, ALU.mult, r=[ptr[0], (fs, 0)], w=[o1])
                P.recip(fs[:, 1:2], ptr[1][:, 64:65], r=[ptr[1]], w=[(fs, 1)])
                P.tt(fs[:, 2:3], fs[:, 1:2], neglam, ALU.mult, r=[(fs, 1), (sm, 5)], w=[(fs, 2)])
                P.stt(dd[:], ptr[1][:, 0:64], fs[:, 2:3], o1[:], ALU.mult, ALU.add, r=[ptr[1], (fs, 2), o1], w=[dd])
                P.ttr(junk[:], dd[:], dd[:], ALU.mult, ALU.add, fs[:, 3:4], r=[dd], w=[junk, (fs, 3)])
                P.act(fs[:, 4:5], fs[:, 3:4], AF.Sqrt, scale=1.0 / 64, bias=1e-5, r=[(fs, 3)], w=[(fs, 4)])
                P.recip(fs[:, 5:6], fs[:, 4:5], r=[(fs, 4)], w=[(fs, 5)])
                P.stt(y_[:], dd[:], fs[:, 5:6], subA[:], ALU.mult, ALU.mult, r=[dd, (fs, 5), subA], w=[y_])
            else:
                P.tt(fs[:, 0:1], ptr[0][:, 64:65], esink, ALU.add, r=[ptr[0], (sm, 6)], w=[(fs, 0)])
                P.recip(fs[:, 1:2], fs[:, 0:1], r=[(fs, 0)], w=[(fs, 1)])
                P.ts(y_[:], ptr[0][:, 0:64], fs[:, 1:2], None, ALU.mult, r=[ptr[0], (fs, 1)], w=[y_])
            P.dma(y_d[q0 + j * 128:q0 + (j + 1) * 128, :], y_[:], q='sync', out_dram=True, w=['y_d'])
    return P.emit()


def build_ret(L, stop=99):
    P = Prog()
    NTL = L // 128
    hT_d = P.dram('hT', [D, L], BF16, 'ExternalInput')
    w_d = P.dram('w', [D, 256], F32, 'ExternalInput')
    tab_d = P.dram('tabs', [4, 32, L], F32, 'ExternalInput')
    idb_d = P.dram('identb', [128, 128], BF16, 'ExternalInput')
    cst_d = P.dram('cst', [128, 770], F32, 'ExternalInput')
    aux_d = P.dram('aux', [128, 128], F32, 'ExternalInput')
    y_d = P.dram('y', [L, 64], BF16, 'ExternalOutput')
    W = load_w_bf16(P, 'W', w_d, D, 256)
    identb = P.sb('identb', [128, 128], BF16)
    P.dma(identb[:], idb_d)
    cst = P.sb('cst', [128, 770], F32)
    P.dma(cst[:], cst_d)
    aux = P.sb('aux', [128, 128], F32)
    P.dma(aux[:], aux_d)
    qT = P.sb('qT', [32, L], BF16)
    kT = P.sb('kT', [32, L], BF16)
    V = P.sb('V', [128, NTL, 64], BF16)
    G = P.sb('G', [128, NTL, 64], F32)
    of = P.sb('of', [128, NTL, 64], F32)
    pa = [P.ps('pa%d' % i, [128, 512], F32) for i in range(2)]
    pv = [P.ps('pv%d' % i, [128, 128], F32) for i in range(2)]
    tb = [P.sb('tb%d' % i, [32, 4, 512], F32) for i in range(2)]
    t1 = P.sb('t1', [32, 512], F32)
    t2 = P.sb('t2', [32, 512], F32)
    hl = HLoader(P, hT_d)
    for ci, (t0, n) in enumerate(chunks_of(L)):
        hb = hl.load(t0, n)
        tb_ = tb[ci % 2]
        for q4 in range(4):
            P.dma(tb_[:, q4, :n], tab_d[q4, :, t0:t0 + n], q='sync', w=[tb_])
        for which, dst, tc in ((0, qT, 0), (2, kT, 2)):
            for k in range(KC):
                P.mm(pa[0][0:32, :n], W[:, k, which * 32:(which + 1) * 32], hb[:, k, :n], start=(k == 0), stop=(k == KC - 1),
                     r=[W, hb], w=[pa[0]])
            for k in range(KC):
                P.mm(pa[1][0:32, :n], W[:, k, (which + 1) * 32:(which + 2) * 32], hb[:, k, :n], start=(k == 0), stop=(k == KC - 1),
                     r=[W, hb], w=[pa[1]])
            P.tt(t1[:, :n], pa[0][0:32, :n], tb_[:, tc, :n], ALU.mult)
            P.tt(t2[:, :n], pa[1][0:32, :n], tb_[:, tc + 1, :n], ALU.mult)
            P.tt(dst[:, t0:t0 + n], t1[:, :n], t2[:, :n], ALU.add, w=[(dst, ci)])
        for j in range(n // 128):
            p_ = pv[j % 2]
            g = t0 // 128 + j
            for k in range(KC):
                P.mm(p_[:], hb[:, k, j * 128:(j + 1) * 128], W[:, k, 128:256], start=(k == 0), stop=(k == KC - 1),
                     r=[W, hb], w=[p_])
            P.copy(V[:, g, :], p_[:, 0:64], eng='vector', r=[p_], w=[(V, g)])
            P.copy(G[:, g, :], p_[:, 64:128], eng='vector', r=[p_], w=[(G, g)])
            P.act(G[:, g, :], G[:, g, :], AF.Silu, r=[(G, g)], w=[(G, g)])
    if stop == 1:
        return P.emit()
    sm = P.sb('sm', [128, 8], F32)
    P.act(sm[:, 0:2], aux[:, 0:2], AF.Sigmoid, r=[aux], w=[sm])
    P.act(sm[:, 2:4], sm[:, 0:2], AF.Ln, r=[sm], w=[sm])
    P.act(sm[:, 4:6], sm[:, 2:4], AF.Exp, scale=128.0, r=[sm], w=[sm])
    P.act(sm[:, 6:7], cst[:, 768:769], AF.Exp, scale=sm[:, 2:3], r=[cst, sm], w=[sm])
    P.act(sm[:, 7:8], cst[:, 769:770], AF.Exp, scale=sm[:, 3:4], r=[cst, sm], w=[sm])
    DT = P.sb('DT', [128, 2, 128], F32)
    qdec = P.sb('qdec', [128, 2, 128], F32)
    for d in range(2):
        P.act(DT[:, d, :], cst[:, 256 * d:256 * d + 128], AF.Exp, scale=sm[:, 2 + d:3 + d], r=[cst, sm], w=[DT])
        P.tt(DT[:, d, :], DT[:, d, :], cst[:, 256 * d + 128:256 * d + 256], ALU.mult, r=[DT, cst], w=[DT])
        P.act(qdec[:, d, :], cst[:, 512 + 128 * d:640 + 128 * d], AF.Exp, scale=sm[:, 2 + d:3 + d], r=[cst, sm], w=[qdec])
    if stop == 2:
        return P.emit()
    S = P.sb('S', [32, 64], F32)
    Sb = P.sb('Sb', [32, 64], BF16)
    qd = [P.sb('qd%d' % i, [32, 128], BF16) for i in range(2)]
    pT = [P.sb('pT%d' % i, [128, 128], BF16) for i in range(2)]
    kd = [P.sb('kd%d' % i, [128, 32], BF16) for i in range(2)]
    o_ps = [P.ps('o%d' % i, [128, 64], F32) for i in range(2)]
    k_ps = P.ps('kps', [128, 32], BF16)
    u_ps = P.ps('ups', [32, 64], F32)
    fs = P.sb('fs', [128, 8], F32)
    tot = P.sb('tot', [128, 64], F32)
    xc = P.sb('xc', [128, 64], F32)
    junk = P.sb('junk', [128, 64], F32)
    yt = [P.sb('yt%d' % i, [128, 64], BF16) for i in range(2)]
    it = 0
    for d in range(2):
        order = list(range(NTL)) if d == 0 else [1, 0] + list(range(NTL - 1, 1, -1))
        P.memset(S[:], 0.0)
        P.memset(Sb[:], 0.0)
        for g in order:
            sl = slice(g * 128, (g + 1) * 128)
            q_ = qd[it % 2]; p_ = pT[it % 2]; k_ = kd[it % 2]; o_ = o_ps[it % 2]; s_ = pa[it % 2]
            y_ = yt[it % 2]
            it += 1
            P.tt(q_[:], qT[:, sl], qdec[0:32, d, :], ALU.mult, r=[qT, qdec], w=[q_])
            P.mm(o_[:], q_[:], Sb[:], start=True, stop=False, r=[q_, Sb], w=[o_])
            P.mm(s_[:, 0:128], kT[:, sl], qT[:, sl], r=[kT, qT], w=[s_])
            P.tt(p_[:], s_[:, 0:128], DT[:, d, :], ALU.mult, r=[s_, DT], w=[p_])
            P.mm(o_[:], p_[:], V[:, g, :], start=False, stop=True, r=[p_, V], w=[o_])
            P.op('tensor', lambda e, sl=sl: e.transpose(k_ps[:], kT[:, sl], identb[0:32, 0:32]), [kT, identb], [k_ps])
            P.ts(k_[:], k_ps[:], sm[:, 6 + d:7 + d], None, ALU.mult, r=[k_ps, sm], w=[k_])
            P.mm(u_ps[:], k_[:], V[:, g, :], r=[k_, V], w=[u_ps])
            P.stt(S[:], S[:], sm[0:32, 4 + d:5 + d], u_ps[:], ALU.mult, ALU.add, r=[S, sm, u_ps], w=[S])
            P.copy(Sb[:], S[:], eng='scalar')
            if d == 0:
                P.copy(of[:, g, :], o_[:], eng='scalar', r=[o_], w=[(of, g)])
            else:
                P.tt(tot[:], o_[:], of[:, g, :], ALU.add, r=[o_, (of, g)], w=[tot])
                P.reduce(fs[:, 0:1], tot[:], r=[tot], w=[(fs, 0)])
                P.ts(fs[:, 1:2], fs[:, 0:1], -1.0 / 64, None, ALU.mult, r=[(fs, 0)], w=[(fs, 1)])
                P.ts(xc[:], tot[:], fs[:, 1:2], None, ALU.add, r=[tot, (fs, 1)], w=[xc])
                P.act(junk[:], xc[:], AF.Square, accum=fs[:, 2:3], r=[xc], w=[junk, (fs, 2)])
                P.act(fs[:, 3:4], fs[:, 2:3], AF.Sqrt, scale=1.0 / 64, bias=1e-5, r=[(fs, 2)], w=[(fs, 3)])
                P.recip(fs[:, 4:5], fs[:, 3:4], r=[(fs, 3)], w=[(fs, 4)])
                P.stt(xc[:], xc[:], fs[:, 4:5], aux[:, 64:128], ALU.mult, ALU.mult, r=[xc, (fs, 4), aux], w=[xc])
                P.tt(y_[:], xc[:], G[:, g, :], ALU.mult, r=[xc, (G, g)], w=[y_])
                P.dma(y_d[sl, :], y_[:], q='sync', out_dram=True, w=['y_d'])
    return P.emit()


def build_rwkv(L, same_sync=True, stop=99):
    P = Prog(same_engine_sync=same_sync)
    NTL = L // 128
    hT_ds = [P.dram('hTf', [D, L], BF16, 'ExternalInput'), P.dram('hTb', [D, L], BF16, 'ExternalInput')]
    hP_ds = [P.dram('hPf', [D, L], BF16, 'ExternalInput'), P.dram('hPb', [D, L], BF16, 'ExternalInput')]
    hN_ds = [P.dram('hNf', [D, L], BF16, 'ExternalInput'), P.dram('hNb', [D, L], BF16, 'ExternalInput')]
    w_d = P.dram('w', [D, 576], F32, 'ExternalInput')
    mu_d = P.dram('mu', [128, 576], F32, 'ExternalInput')
    aux_d = P.dram('aux', [128, 576], F32, 'ExternalInput')
    wa_d = P.dram('w2a2', [128, 2, 64], F32, 'ExternalInput')
    g2_d = P.dram('g2h', [128, 64], F32, 'ExternalInput')
    cst_d = P.dram('cst', [128, 384], F32, 'ExternalInput')
    y_d = P.dram('y', [L, 64], BF16, 'ExternalOutput')
    rows_s = [P.dram('rows%d' % d, [L, 384], F32, 'ExternalOutput') for d in range(2)]
    vT_s = [P.dram('vTs%d' % d, [64, L], F32, 'ExternalOutput') for d in range(2)]
    vg_s = P.dram('vgs', [L, 128], F32, 'ExternalOutput')

    cst = P.sb('cst', [128, 384], F32); P.dma(cst[:], cst_d)
    ident = cst[:, 0:128]; J = cst[:, 128:256]; E = cst[0:2, 256:384]
    aux = P.sb('aux', [128, 576], F32); P.dma(aux[:], aux_d)
    mu = P.sb('mu', [128, 576], F32); P.dma(mu[:], mu_d)
    Wf = P.sb('Wf', [128, KC, 576], F32)
    wsrc = w_d.rearrange('(k p) n -> p k n', p=128)
    for k in range(KC):
        P.dma(Wf[:, k, :], wsrc[:, k, :], q='sync', w=[(Wf, k)])
    W1 = P.sb('W1', [128, KC, 576], BF16)
    W2 = P.sb('W2', [128, KC, 576], BF16)
    wtmp = P.sb('wtmp', [128, 576], F32)
    for k in range(KC):
        P.tt(wtmp[:], Wf[:, k, :], mu[:], ALU.mult, r=[(Wf, k), mu], w=[wtmp])
        P.tt(W1[:, k, :], Wf[:, k, :], wtmp[:], ALU.subtract, r=[(Wf, k), wtmp], w=[(W1, k)])
        P.ts(W2[:, k, :], wtmp[:], 0.5, None, ALU.mult, r=[wtmp], w=[(W2, k)])
    waf = P.sb('waf', [128, 2, 64], F32); P.dma(waf[:], wa_d)
    wab = P.sb('wab', [128, 2, 64], BF16); P.copy(wab[:], waf[:])
    waf2 = P.sb('waf2', [128, 2, 64], F32); P.dma(waf2[0:64], wa_d[64:128])
    wab2 = P.sb('wab2', [128, 2, 64], BF16); P.copy(wab2[0:64], waf2[0:64])
    g2f = P.sb('g2f', [128, 64], F32); P.dma(g2f[:], g2_d)
    g2b = P.sb('g2b', [128, 64], BF16); P.copy(g2b[:], g2f[:])
    omka = P.sb('omka', [128, 64], F32)
    P.ts(omka[:], aux[:, 320:384], -1.0, 1.0, ALU.mult, ALU.add, r=[aux], w=[omka])

    hbs = [P.sb('hbx%d' % i, [128, KC, 512], BF16) for i in range(2)]
    hps = [P.sb('hpx%d' % i, [128, KC, 512], BF16) for i in range(1)]
    hns = [P.sb('hnx%d' % i, [128, KC, 512], BF16) for i in range(1)]
    hs = P.sb('hs', [128, KC, 512], BF16)
    pa = [P.ps('pa%d' % i, [128, 512], F32) for i in range(2)]
    pt = [P.ps('pt%d' % i, [128, 128], F32) for i in range(2)]
    pw = [P.ps('pw%d' % i, [128, 128], F32) for i in range(2)]
    lo = P.sb('lo', [128, 512], BF16)
    lo2 = P.sb('lo2', [128, 512], BF16)
    stg = P.sb('stg', [128, 512], F32)
    sg = P.sb('sg', [128, 512], BF16)
    vTt = P.sb('vTt', [64, 512], F32)
    rows = [P.sb('rw%d' % i, [128, 6, 64], F32) for i in range(2)]
    for rw_ in rows:
        P.memset(rw_[:], 0.0)
    ea = P.sb('ea', [128, 64], F32)
    eb = P.sb('eb', [128, 64], F32)
    ec = P.sb('ec', [128, 64], F32)
    ed = P.sb('ed', [128, 64], F32)
    fs = P.sb('fs', [128, 8], F32)
    vg = [P.sb('vg%d' % i, [128, 128], F32) for i in range(2)]
    ci_all = 0
    for d in range(2):
        hv = hT_ds[d].rearrange('(k p) t -> p k t', p=128)
        hpv = hP_ds[d].rearrange('(k p) t -> p k t', p=128)
        hnv = hN_ds[d].rearrange('(k p) t -> p k t', p=128)
        for ci, (t0, n) in enumerate(chunks_of(L)):
            hb = hbs[ci_all % 2]; hp = hps[0]; hn = hns[0]
            ci_all += 1
            P.dma(hb[:, :, :n], hv[:, :, t0:t0 + n], q='sync', w=[hb])
            P.dma(hp[:, :, :n], hpv[:, :, t0:t0 + n], q='sync', w=[hp])
            P.dma(hn[:, :, :n], hnv[:, :, t0:t0 + n], q='sync', w=[hn])
            P.tt(hs[:, :, :n], hp[:, :, :n], hn[:, :, :n], ALU.add, r=[hp, hn], w=[hs])

            def proj_fm(ps, c0, c1):
                for k in range(KC):
                    P.mm(ps[0:c1 - c0, :n], W1[:, k, c0:c1], hb[:, k, :n], start=(k == 0), stop=False, r=[W1, hb], w=[ps])
                for k in range(KC):
                    P.mm(ps[0:c1 - c0, :n], W2[:, k, c0:c1], hs[:, k, :n], start=False, stop=(k == KC - 1), r=[W2, hs], w=[ps])
            proj_fm(pa[0], 128, 192)
            P.copy(vTt[:, :n], pa[0][0:64, :n], eng='scalar')
            P.dma(vT_s[d][:, t0:t0 + n], vTt[:, :n], q='sync', w=[('vT_s', d)])
            proj_fm(pa[1], 192 + 128 * d, 256 + 128 * d)
            P.copy(stg[0:64, :n], pa[1][0:64, :n], eng='vector', r=[pa[1]], w=[stg])
            P.act(lo[0:64, :n], stg[0:64, :n], AF.Tanh, r=[stg], w=[lo])
            proj_fm(pa[1], 256 + 128 * d, 320 + 128 * d)
            P.copy(lo2[0:64, :n], pa[1][0:64, :n], eng='vector', r=[pa[1]], w=[lo2])
            if d == 0:
                proj_fm(pa[0], 448, 576)
                P.copy(stg[:, :n], pa[0][:, :n], eng='vector', r=[pa[0]], w=[stg])
                P.act(sg[:, :n], stg[:, :n], AF.Sigmoid, r=[stg], w=[sg])
            for j in range(n // 128):
                g = t0 // 128 + j
                tsl = slice(j * 128, (j + 1) * 128)
                p1 = pt[g % 2]; p2 = pw[g % 2]; rw = rows[g % 2]
                ncols = 192 if d == 0 else 128
                for k in range(KC):
                    P.mm(p1[:, 0:128], hb[:, k, tsl], W1[:, k, 0:128], start=(k == 0), stop=False, r=[W1, hb], w=[p1])
                for k in range(KC):
                    P.mm(p1[:, 0:128], hs[:, k, tsl], W2[:, k, 0:128], start=False, stop=(k == KC - 1), r=[W2, hs], w=[p1])
                P.mm(p2[:, 0:64], lo[0:64, tsl], wab[0:64, d, :], r=[lo, wab], w=[p2])
                P.mm(p2[:, 64:128], lo2[0:64, tsl], wab2[0:64, d, :], r=[lo2, wab2], w=[p2])
                r_ = rw[:, 4, :]
                P.copy(r_, p1[:, 0:64], eng='scalar', r=[p1], w=[rw])
                kk_ = ea
                P.copy(kk_[:], p1[:, 64:128], eng='scalar', r=[p1], w=[ea])
                P.tt(eb[:], p2[:, 0:64], aux[:, 64 * d:64 * d + 64], ALU.add, r=[p2, aux], w=[eb])
                P.act(eb[:], eb[:], AF.Sigmoid, r=[eb], w=[eb])
                P.act(rw[:, 1, :], eb[:], AF.Exp, scale=-0.6065306597126334, r=[eb], w=[rw])
                P.tt(ec[:], p2[:, 64:128], aux[:, 128 + 64 * d:192 + 64 * d], ALU.add, r=[p2, aux], w=[ec])
                P.act(ec[:], ec[:], AF.Sigmoid, r=[ec], w=[ec])
                P.tt(ed[:], ea[:], aux[:, 256:320], ALU.mult, r=[ea, aux], w=[ed])
                P.act(eb[:], ed[:], AF.Square, accum=fs[:, 0:1], r=[ed], w=[eb, fs])
                P.ts(fs[:, 0:1], fs[:, 0:1], 1e-12, None, ALU.max, r=[fs], w=[fs])
                P.act(fs[:, 1:2], fs[:, 0:1], AF.Sqrt, r=[fs], w=[fs])
                P.recip(fs[:, 2:3], fs[:, 1:2], r=[fs], w=[fs])
                P.ts(rw[:, 0, :], ed[:], fs[:, 2:3], -1.0, ALU.mult, ALU.mult, r=[ed, fs], w=[rw])
                P.stt(rw[:, 2, :], rw[:, 0, :], -1.0, ec[:], ALU.mult, ALU.mult, r=[rw, ec], w=[rw])
                P.tt(ed[:], ec[:], aux[:, 320:384], ALU.mult, r=[ec, aux], w=[ed])
                P.tt(ed[:], ed[:], omka[:], ALU.add, r=[ed, omka], w=[ed])
                P.tt(rw[:, 3, :], ea[:], ed[:], ALU.mult, r=[ea, ed], w=[rw])
                P.tt(ed[:], rw[:, 4, :], rw[:, 3, :], ALU.mult, r=[rw], w=[ed])
                P.ttr(ed[:], ed[:], aux[:, 384:448], ALU.mult, ALU.add, rw[:, 5, 0:1], r=[ed, aux], w=[ed, rw])
                P.dma(rows_s[d][g * 128:(g + 1) * 128, :], rw[:].rearrange('p a b -> p (a b)'), q='sync', r=[rw], w=[('rows_s', d)])
                if d == 0:
                    vg_ = vg[g % 2]
                    for k in range(KC):
                        P.mm(p2[:, 0:64], hb[:, k, tsl], W1[:, k, 128:192], start=(k == 0), stop=False, r=[W1, hb], w=[p2])
                    for k in range(KC):
                        P.mm(p2[:, 0:64], hs[:, k, tsl], W2[:, k, 128:192], start=False, stop=(k == KC - 1), r=[W2, hs], w=[p2])
                    P.mm(p2[:, 64:128], sg[:, tsl], g2b[:], r=[sg, g2b], w=[p2])
                    P.copy(vg_[:], p2[:], eng='vector', r=[p2], w=[vg_])
                    P.dma(vg_s[g * 128:(g + 1) * 128, :], vg_[:], q='sync', r=[vg_], w=['vg_s'])
    if stop == 1:
        return P.emit()
    NB = 8
    S = P.sb('S', [128, 64], F32)
    P.memset(S[:], 0.0)
    tmp = P.sb('tmp', [128, 64], F32)
    sa = P.sb('sa', [128, 1], F32)
    ysc = P.sb('ysc', [128, L], F32)
    rb = [P.sb('rb%d' % i, [2, NB * 320], F32) for i in range(2)]
    vc = [P.sb('vc%d' % i, [128, NB], F32) for i in range(2)]
    bc = pa + [P.ps('bc%d' % i, [128, 512], F32) for i in range(2)]
    P.same_engine_sync_default = same_sync
    for blk in range(L // NB):
        s0 = blk * NB
        rb_ = rb[blk % 2]; vc_ = vc[blk % 2]
        for d in range(2):
            P.dma(rb_[d:d + 1, :].rearrange('o (s c) -> o s c', c=320), rows_s[d][s0:s0 + NB, 0:320], q='sync', r=[('rows_s', d)], w=[rb_])
            P.dma(vc_[64 * d:64 * d + 64, :], vT_s[d][:, s0:s0 + NB], q='sync', r=[('vT_s', d)], w=[vc_])
        for j in range(NB):
            s = s0 + j
            b_ = bc[s % 4]
            P.mm(b_[:, 0:320], E, rb_[0:2, j * 320:(j + 1) * 320], r=[cst, rb_], w=[b_])
            P.tt(tmp[:], S[:], b_[:, 0:64], ALU.mult, r=[S, b_], w=[tmp])
            P.reduce(sa[:], tmp[:], r=[tmp], w=[sa])
            P.tt(S[:], S[:], b_[:, 64:128], ALU.mult, r=[S, b_], w=[S])
            P.stt(S[:], b_[:, 128:192], sa[:, 0:1], S[:], ALU.mult, ALU.add, r=[b_, sa, S], w=[S])
            P.stt(S[:], b_[:, 192:256], vc_[:, j:j + 1], S[:], ALU.mult, ALU.add, r=[b_, vc_, S], w=[S])
            P.tt(tmp[:], S[:], b_[:, 256:320], ALU.mult, r=[S, b_], w=[tmp])
            P.reduce(ysc[:, s:s + 1], tmp[:], r=[tmp], w=[(ysc, s // 128)])
    if stop == 2:
        return P.emit()
    yfs = P.sb('yfs', [128, 64], F32)
    ybs = P.sb('ybs', [128, 128], F32)
    bf_ = P.sb('bf_', [128, 64], F32)
    tot = P.sb('tot', [128, 64], F32)
    xc = P.sb('xc', [128, 64], F32)
    yt = [P.sb('yt%d' % i, [128, 64], BF16) for i in range(2)]
    for g in range(NTL):
        gm = (1 - g) if g < 2 else (NTL + 1 - g)
        p1 = pt[g % 2]; p2 = pw[g % 2]; vg_ = vg[g % 2]; y_ = yt[g % 2]
        P.mm(p1[:, 0:64], ysc[:, g * 128:(g + 1) * 128], ident[:, 0:64], r=[(ysc, g), cst], w=[p1])
        P.mm(p2[:, 0:64], ysc[:, gm * 128:(gm + 1) * 128], ident[:, 64:128], r=[(ysc, gm), cst], w=[p2])
        P.copy(yfs[:], p1[:, 0:64], eng='scalar', r=[p1], w=[yfs])
        P.copy(ybs[:, 0:64], p2[:, 0:64], eng='vector', r=[p2], w=[ybs])
        P.dma(ybs[:, 64:128], rows_s[1][gm * 128:(gm + 1) * 128, 320:384], q='sync', r=[('rows_s', 1)], w=[ybs])
        P.dma(bf_[:], rows_s[0][g * 128:(g + 1) * 128, 320:384], q='sync', r=[('rows_s', 0)], w=[bf_])
        P.dma(vg_[:], vg_s[g * 128:(g + 1) * 128, :], q='sync', r=['vg_s'], w=[vg_])
        P.mm(p2[:, 0:65], J, ybs[:, 0:65], r=[cst, ybs], w=[p2])
        P.tt(tot[:], yfs[:], p2[:, 0:64], ALU.add, r=[yfs, p2], w=[tot])
        P.tt(bf_[:, 0:1], bf_[:, 0:1], p2[:, 64:65], ALU.add, r=[bf_, p2], w=[bf_])
        P.reduce(fs[:, 0:1], tot[:], r=[tot], w=[fs])
        P.ts(fs[:, 1:2], fs[:, 0:1], -1.0 / 64, None, ALU.mult, r=[fs], w=[fs])
        P.ts(xc[:], tot[:], fs[:, 1:2], None, ALU.add, r=[tot, fs], w=[xc])
        P.act(tot[:], xc[:], AF.Square, accum=fs[:, 2:3], r=[xc], w=[tot, fs])
        P.act(fs[:, 3:4], fs[:, 2:3], AF.Sqrt, scale=1.0 / 64, bias=64e-5, r=[fs], w=[fs])
        P.recip(fs[:, 4:5], fs[:, 3:4], r=[fs], w=[fs])
        P.stt(xc[:], xc[:], fs[:, 4:5], aux[:, 448:512], ALU.mult, ALU.mult, r=[xc, fs, aux], w=[xc])
        P.tt(xc[:], xc[:], aux[:, 512:576], ALU.add, r=[xc, aux], w=[xc])
        P.stt(xc[:], vg_[:, 0:64], bf_[:, 0:1], xc[:], ALU.mult, ALU.add, r=[vg_, bf_, xc], w=[xc])
        P.tt(y_[:], xc[:], vg_[:, 64:128], ALU.mult, r=[xc, vg_], w=[y_])
        P.dma(y_d[g * 128:(g + 1) * 128, :], y_[:], q='sync', out_dram=True, w=['y_d'])
    return P.emit()


def build_rwkv2(L, stop=99):
    P = Prog()
    NTL = L // 128
    hT_ds = [P.dram('hTf', [D, L + 4], BF16, 'ExternalInput'), P.dram('hTb', [D, L + 4], BF16, 'ExternalInput')]
    w_d = P.dram('w', [D, 576], F32, 'ExternalInput')
    mu_d = P.dram('mu', [128, 576], F32, 'ExternalInput')
    aux_d = P.dram('aux', [128, 576], F32, 'ExternalInput')
    wa_d = P.dram('w2a2', [128, 2, 64], F32, 'ExternalInput')
    g2_d = P.dram('g2h', [128, 64], F32, 'ExternalInput')
    cst_d = P.dram('cst2', [128, 2688], F32, 'ExternalInput')
    y_d = P.dram('y', [L, 64], BF16, 'ExternalOutput')
    vg_s = P.dram('vgs', [L, 128], F32, 'ExternalOutput')

    cst = P.sb('cst', [128, 2688], F32); P.dma(cst[:], cst_d)
    ident = cst[:, 0:128]; J = cst[:, 128:256]; triU = cst[:, 256:384]
    mS = cst[:, 384:512]; mI = cst[:, 512:640]; mSL = cst[:, 640:768]; ones = cst[:, 768:896]
    aux = P.sb('aux', [128, 576], F32); P.dma(aux[:], aux_d)
    mu = P.sb('mu', [128, 576], F32); P.dma(mu[:], mu_d)
    Wfs = [P.sb('Wf%d' % i, [128, 576], F32) for i in range(2)]
    wsrc = w_d.rearrange('(k p) n -> p k n', p=128)
    W1 = P.sb('W1', [128, KC, 576], BF16)
    W2 = P.sb('W2', [128, KC, 576], BF16)
    wtmp = P.sb('wtmp', [128, 576], F32)
    for k in range(KC):
        Wf_ = Wfs[k % 2]
        P.dma(Wf_[:], wsrc[:, k, :], q='sync', w=[Wf_])
        P.tt(wtmp[:], Wf_[:], mu[:], ALU.mult, r=[Wf_, mu], w=[wtmp])
        P.tt(W1[:, k, :], Wf_[:], wtmp[:], ALU.subtract, r=[Wf_, wtmp], w=[(W1, k)])
        P.ts(W2[:, k, :], wtmp[:], 0.5, None, ALU.mult, r=[wtmp], w=[(W2, k)])
    waf = P.sb('waf', [128, 2, 64], F32); P.dma(waf[:], wa_d)
    wab = P.sb('wab', [128, 2, 64], BF16); P.copy(wab[:], waf[:])
    waf2 = P.sb('waf2', [128, 2, 64], F32); P.dma(waf2[0:64], wa_d[64:128])
    wab2 = P.sb('wab2', [128, 2, 64], BF16); P.copy(wab2[0:64], waf2[0:64])
    g2f = P.sb('g2f', [128, 64], F32); P.dma(g2f[:], g2_d)
    g2b = P.sb('g2b', [128, 64], BF16); P.copy(g2b[:], g2f[:])
    omka = P.sb('omka', [128, 64], F32)
    P.ts(omka[:], aux[:, 320:384], -1.0, 1.0, ALU.mult, ALU.add, r=[aux], w=[omka])

    LN = range(2)
    hb = [P.sb('hbx%d' % d, [128, KC, 512], BF16) for d in LN]
    hp = [P.sb('hpx%d' % d, [128, KC, 512], BF16) for d in LN]
    hn = [P.sb('hnx%d' % d, [128, KC, 512], BF16) for d in LN]
    hs = hp
    lo = [P.sb('lo%d' % d, [64, 512], BF16) for d in LN]
    lo2 = [P.sb('lob%d' % d, [64, 512], BF16) for d in LN]
    stg = P.sb('stg', [128, 512], F32)
    sg = P.sb('sg', [128, 512], BF16)
    pa = [P.ps('pa%d' % d, [128, 512], F32) for d in LN]
    pb = [P.ps('pb%d' % d, [128, 512], F32) for d in LN]
    pc = [P.ps('pc%d' % d, [128, 512], F32) for d in LN]
    pd = [P.ps('pd%d' % d, [128, 512], F32) for d in LN]
    TM = [P.sb('TM%d' % d, [128, 8, 64], F32) for d in LN]
    E4 = [P.sb('E4%d' % d, [128, 4, 64], F32) for d in LN]
    X6 = [P.sb('X6%d' % d, [128, 6, 64], F32) for d in LN]
    AX_ = [P.sb('AX%d' % d, [128, 128], F32) for d in LN]
    FM = [P.sb('FM%d' % d, [64, 4, 128], F32) for d in LN]
    MB = [P.sb('MB%d' % d, [128, 256], F32) for d in LN]
    MK = [P.sb('MK%d' % d, [128, 256], F32) for d in LN]
    NTt = [[P.sb('NT%d_%d' % (d, i), [128, 128], F32) for i in range(1)] for d in LN]
    NNt = [[P.sb('NN%d_%d' % (d, i), [128, 128], F32) for i in range(1)] for d in LN]
    Tt = [[P.sb('Tt%d_%d' % (d, i), [128, 128], F32) for i in range(2)] for d in LN]
    TtT = [[P.sb('TtT%d_%d' % (d, i), [128, 128], F32) for i in range(2)] for d in LN]
    Us = [P.sb('Us%d' % d, [128, 128], F32) for d in LN]
    UsT = [P.sb('UsT%d' % d, [128, 128], F32) for d in LN]
    Ys = [P.sb('Ys%d' % d, [128, 256], F32) for d in LN]
    AW = [P.sb('AW%d' % d, [128, 128], F32) for d in LN]
    Pm = [P.sb('Pm%d' % d, [64, 64], F32) for d in LN]
    QT = [P.sb('QT%d' % d, [64, 64], F32) for d in LN]
    Rp = [P.sb('Rp%d' % d, [64, 128], F32) for d in LN]
    yl = [P.sb('yl%d' % d, [128, 64], F32) for d in LN]
    gC = [P.sb('gC%d' % d, [64, 2], F32) for d in LN]
    ST = [P.sb('ST%d' % d, [64, 64], F32) for d in LN]
    Yd = [P.sb('Yd%d' % d, [128, NTL, 65], F32) for d in LN]
    ea = [P.sb('ea%d' % d, [128, 64], F32) for d in LN]
    cs_ = [P.sb('csb%d' % d, [128, 130], F32) for d in LN]
    eb = [P.sb('eb%d' % d, [128, 64], F32) for d in LN]
    ec = [P.sb('ec%d' % d, [128, 64], F32) for d in LN]
    fs = [P.sb('fs%d' % d, [128, 8], F32) for d in LN]
    vg = [P.sb('vg%d' % i, [128, 128], F32) for i in range(2)]
    for d in LN:
        P.memset(ST[d][:], 0.0)

    for ci, (t0, n) in enumerate(chunks_of(L)):
        for d in LN:
            hv = hT_ds[d].rearrange('(k p) t -> p k t', p=128)
            o_ = t0 + 1 if t0 < 256 else t0 + 3
            P.dma(hb[d][:, :, :n], hv[:, :, o_:o_ + n], q='sync', w=[hb[d]])
            P.dma(hp[d][:, :, :n], hv[:, :, o_ - 1:o_ - 1 + n], q='sync', w=[hp[d]])
            P.dma(hn[d][:, :, :n], hv[:, :, o_ + 1:o_ + 1 + n], q='sync', w=[hn[d]])
            P.tt(hs[d][:, :, :n], hp[d][:, :, :n], hn[d][:, :, :n], ALU.add, r=[hp[d], hn[d]], w=[hs[d]])

            def proj_fm(ps, c0, c1, d=d):
                for k in range(KC):
                    P.mm(ps[0:c1 - c0, :n], W1[:, k, c0:c1], hb[d][:, k, :n], start=(k == 0), stop=False, r=[W1, hb[d]], w=[ps])
                for k in range(KC):
                    P.mm(ps[0:c1 - c0, :n], W2[:, k, c0:c1], hs[d][:, k, :n], start=False, stop=(k == KC - 1), r=[W2, hs[d]], w=[ps])
            proj_fm(pa[d], 192 + 128 * d, 256 + 128 * d)
            P.copy(stg[0:64, :n], pa[d][0:64, :n], eng='vector', r=[pa[d]], w=[stg])
            P.act(lo[d][:, :n], stg[0:64, :n], AF.Tanh, r=[stg], w=[lo[d]])
            proj_fm(pb[d], 256 + 128 * d, 320 + 128 * d)
            P.copy(lo2[d][:, :n], pb[d][0:64, :n], eng='vector', r=[pb[d]], w=[lo2[d]])
            if d == 0:
                proj_fm(pa[0], 448, 576)
                P.copy(stg[:, :n], pa[0][:, :n], eng='vector', r=[pa[0]], w=[stg])
                P.act(sg[:, :n], stg[:, :n], AF.Sigmoid, r=[stg], w=[sg])
        for j in range(n // 128):
            g = t0 // 128 + j
            tsl = slice(j * 128, (j + 1) * 128)
            for d in LN:
                p1 = pa[d]; p2 = pb[d]; T_ = TM[d]
                for k in range(KC):
                    P.mm(p1[:, 0:192], hb[d][:, k, tsl], W1[:, k, 0:192], start=(k == 0), stop=False, r=[W1, hb[d]], w=[p1])
                for k in range(KC):
                    P.mm(p1[:, 0:192], hs[d][:, k, tsl], W2[:, k, 0:192], start=False, stop=(k == KC - 1), r=[W2, hs[d]], w=[p1])
                P.mm(p2[:, 0:64], lo[d][:, tsl], wab[0:64, d, :], r=[lo[d], wab], w=[p2])
                P.mm(p2[:, 64:128], lo2[d][:, tsl], wab2[0:64, d, :], r=[lo2[d], wab2], w=[p2])
                P.copy(T_[:, 4, :], p1[:, 0:64], eng='scalar', r=[p1], w=[T_])
                P.copy(ea[d][:], p1[:, 64:128], eng='scalar', r=[p1], w=[ea[d]])
                P.copy(T_[:, 5, :], p1[:, 128:192], eng='scalar', r=[p1], w=[T_])
                P.tt(eb[d][:], p2[:, 0:64], aux[:, 64 * d:64 * d + 64], ALU.add, r=[p2, aux], w=[eb[d]])
                P.act(eb[d][:], eb[d][:], AF.Sigmoid, r=[eb[d]], w=[eb[d]])
                P.ts(T_[:, 0, :], eb[d][:], -0.6065306597126334, None, ALU.mult, r=[eb[d]], w=[T_])
                P.tt(ec[d][:], p2[:, 64:128], aux[:, 128 + 64 * d:192 + 64 * d], ALU.add, r=[p2, aux], w=[ec[d]])
                P.act(T_[:, 2, :], ec[d][:], AF.Sigmoid, r=[ec[d]], w=[T_])
                P.tt(ec[d][:], ea[d][:], aux[:, 256:320], ALU.mult, r=[ea[d], aux], w=[ec[d]])
                P.act(eb[d][:], ec[d][:], AF.Square, accum=fs[d][:, 0:1], r=[ec[d]], w=[eb[d], fs[d]])
                P.ts(fs[d][:, 0:1], fs[d][:, 0:1], 1e-12, None, ALU.max, r=[fs[d]], w=[fs[d]])
                P.act(fs[d][:, 1:2], fs[d][:, 0:1], AF.Sqrt, r=[fs[d]], w=[fs[d]])
                P.recip(fs[d][:, 2:3], fs[d][:, 1:2], r=[fs[d]], w=[fs[d]])
                P.ts(T_[:, 1, :], ec[d][:], fs[d][:, 2:3], -1.0, ALU.mult, ALU.mult, r=[ec[d], fs[d]], w=[T_])
                P.stt(T_[:, 6, :], T_[:, 1, :], -1.0, T_[:, 2, :], ALU.mult, ALU.mult, r=[T_], w=[T_])
                P.tt(ec[d][:], T_[:, 2, :], aux[:, 320:384], ALU.mult, r=[T_, aux], w=[ec[d]])
                P.tt(ec[d][:], ec[d][:], omka[:], ALU.add, r=[ec[d], omka], w=[ec[d]])
                P.tt(T_[:, 3, :], ea[d][:], ec[d][:], ALU.mult, r=[ea[d], ec[d]], w=[T_])
                P.tt(ec[d][:], T_[:, 4, :], T_[:, 3, :], ALU.mult, r=[T_], w=[ec[d]])
                P.ttr(ec[d][:], ec[d][:], aux[:, 384:448], ALU.mult, ALU.add, Yd[d][:, g, 64:65], r=[ec[d], aux], w=[ec[d], (Yd[d], g)])
                if d == 0:
                    vg_ = vg[g % 2]
                    P.mm(p2[:, 128:192], sg[:, tsl], g2b[:], r=[sg, g2b], w=[p2])
                    P.copy(vg_[:, 0:64], T_[:, 5, :], eng='vector', r=[T_], w=[vg_])
                    P.copy(vg_[:, 64:128], p2[:, 128:192], eng='vector', r=[p2], w=[vg_])
                    P.dma(vg_s[g * 128:(g + 1) * 128, :], vg_[:], q='sync', r=[vg_], w=['vg_s'])
            if stop < 2:
                continue
            for d in LN:
                T_ = TM[d]
                P.mm(pc[d][:, 0:64], triU, T_[:, 0, :], r=[cst, T_], w=[pc[d]])
                P.mm(pc[d][:, 64:128], ones, T_[:, 0, :], r=[cst, T_], w=[pc[d]])
                P.mm(pc[d][0:64, 128:130], T_[:, 0, :], ones[:, 0:2], r=[cst, T_], w=[pc[d]])
            for d in LN:
                T_ = TM[d]; E_ = E4[d]
                P.copy(cs_[d][:], pc[d][:, 0:130], eng='vector', r=[pc[d]], w=[cs_[d]])
                P.act(E_[:, 0, :], cs_[d][:, 0:64], AF.Exp, r=[cs_[d]], w=[E_])
                P.act(E_[:, 1, :], cs_[d][:, 0:64], AF.Exp, scale=-1.0, r=[cs_[d]], w=[E_])
                P.tt(ea[d][:], cs_[d][:, 0:64], T_[:, 0, :], ALU.subtract, r=[cs_[d], T_], w=[ea[d]])
                P.act(E_[:, 2, :], ea[d][:], AF.Exp, r=[ea[d]], w=[E_])
                P.tt(eb[d][:], cs_[d][:, 64:128], cs_[d][:, 0:64], ALU.subtract, r=[cs_[d]], w=[eb[d]])
                P.act(E_[:, 3, :], eb[d][:], AF.Exp, r=[eb[d]], w=[E_])
                P.act(gC[d][:, 0:2], cs_[d][0:64, 128:130], AF.Exp, r=[cs_[d]], w=[gC[d]])
            for d in LN:
                T_ = TM[d]; E_ = E4[d]; X_ = X6[d]
                P.tt(AX_[d][:, 0:64], T_[:, 1, :], E_[:, 2, :], ALU.mult, r=[T_, E_], w=[AX_[d]])
                P.tt(X_[:, 1, :], T_[:, 6, :], E_[:, 1, :], ALU.mult, r=[T_, E_], w=[X_])
                P.tt(X_[:, 2, :], T_[:, 3, :], E_[:, 1, :], ALU.mult, r=[T_, E_], w=[X_])
                P.tt(X_[:, 3, :], T_[:, 4, :], E_[:, 0, :], ALU.mult, r=[T_, E_], w=[X_])
                P.tt(X_[:, 4, :], T_[:, 6, :], E_[:, 3, :], ALU.mult, r=[T_, E_], w=[X_])
                P.tt(X_[:, 5, :], T_[:, 3, :], E_[:, 3, :], ALU.mult, r=[T_, E_], w=[X_])
            if stop < 3:
                continue
            for d in LN:
                X_ = X6[d]
                P.mm(pd[d][0:64, 0:128], AX_[d][:, 0:64], ident, r=[AX_[d], cst], w=[pd[d]])
                for q_ in range(1, 4):
                    P.mm(pd[d][0:64, q_ * 128:(q_ + 1) * 128], X_[:, q_, :], ident, r=[X_, cst], w=[pd[d]])
            for d in LN:
                P.copy(FM[d][:].rearrange('p a b -> p (a b)'), pd[d][0:64, :], eng='vector', r=[pd[d]], w=[FM[d]])
            if stop < 4:
                continue
            for d in LN:
                F_ = FM[d]
                P.mm(pa[d][:, 0:128], F_[:, 1, :], F_[:, 0, :], r=[F_], w=[pa[d]])
                P.mm(pa[d][:, 128:256], F_[:, 1, :], F_[:, 3, :], r=[F_], w=[pa[d]])
                P.mm(pa[d][:, 256:384], F_[:, 0, :], F_[:, 1, :], r=[F_], w=[pa[d]])
                P.mm(pb[d][:, 0:128], F_[:, 2, :], F_[:, 0, :], r=[F_], w=[pb[d]])
                P.mm(pb[d][:, 128:256], F_[:, 2, :], F_[:, 3, :], r=[F_], w=[pb[d]])
            for d in LN:
                P.tt(MB[d][:, 0:128], pa[d][:, 0:128], mS, ALU.mult, r=[pa[d], cst], w=[MB[d]])
                P.tt(MB[d][:, 128:256], pa[d][:, 128:256], mI, ALU.mult, r=[pa[d], cst], w=[MB[d]])
                P.tt(NTt[d][0][:], pa[d][:, 256:384], mSL, ALU.mult, r=[pa[d], cst], w=[NTt[d][0]])
                P.tt(MK[d][:, 0:128], pb[d][:, 0:128], mS, ALU.mult, r=[pb[d], cst], w=[MK[d]])
                P.tt(MK[d][:, 128:256], pb[d][:, 128:256], mI, ALU.mult, r=[pb[d], cst], w=[MK[d]])
                P.copy(NNt[d][0][:], MB[d][:, 0:128], eng='scalar', r=[MB[d]], w=[NNt[d][0]])
            if stop < 5:
                continue
            for d in LN:
                P.mm(pc[d][:, 256:320], MK[d][:, 0:128], TM[d][:, 5, :], r=[MK[d], TM[d]], w=[pc[d]])
            for d in LN:
                P.copy(AX_[d][:, 64:128], pc[d][:, 256:320], eng='vector', r=[pc[d]], w=[AX_[d]])
            for d in LN:
                P.copy(Tt[d][0][:], ident, eng='vector', r=[cst], w=[Tt[d][0]])
                P.copy(TtT[d][0][:], ident, eng='vector', r=[cst], w=[TtT[d][0]])
            for lv in range(7):
                cur, nxt = lv % 2, (lv + 1) % 2
                mk_ = cst[:, 896 + 256 * lv:896 + 256 * lv + 128]
                mkT_ = cst[:, 896 + 256 * lv + 128:896 + 256 * lv + 256]
                for d in LN:
                    P.tt(Us[d][:], NNt[d][0][:], mk_, ALU.mult, r=[NNt[d][0], cst], w=[Us[d]])
                    P.tt(UsT[d][:], NTt[d][0][:], mkT_, ALU.mult, r=[NTt[d][0], cst], w=[UsT[d]])
                for d in LN:
                    P.mm(pa[d][:, 0:128], UsT[d][:], Tt[d][cur][:], r=[UsT[d], Tt[d][cur]], w=[pa[d]])
                    if lv < 6:
                        P.mm(pa[d][:, 128:256], Us[d][:], TtT[d][cur][:], r=[Us[d], TtT[d][cur]], w=[pa[d]])
                for d in LN:
                    P.copy(Ys[d][:, 0:(256 if lv < 6 else 128)], pa[d][:, 0:(256 if lv < 6 else 128)], eng='vector', r=[pa[d]], w=[Ys[d]])
                for d in LN:
                    P.mm(pb[d][:, 0:128], TtT[d][cur][:], Ys[d][:, 0:128], r=[TtT[d][cur], Ys[d]], w=[pb[d]])
                    if lv < 6:
                        P.mm(pb[d][:, 128:256], Tt[d][cur][:], Ys[d][:, 128:256], r=[Tt[d][cur], Ys[d]], w=[pb[d]])
                for d in LN:
                    P.tt(Tt[d][nxt][:], pb[d][:, 0:128], Tt[d][cur][:], ALU.add, r=[pb[d], Tt[d][cur]], w=[Tt[d][nxt]])
                    if lv < 6:
                        P.tt(TtT[d][nxt][:], pb[d][:, 128:256], TtT[d][cur][:], ALU.add, r=[pb[d], TtT[d][cur]], w=[TtT[d][nxt]])
            TF = [Tt[d][1] for d in LN]
            for d in LN:
                P.mm(pc[d][:, 0:128], TF[d][:], AX_[d][:], r=[TF[d], AX_[d]], w=[pc[d]])
            for d in LN:
                P.copy(AW[d][:], pc[d][:, 0:128], eng='vector', r=[pc[d]], w=[AW[d]])
            for d in LN:
                X_ = X6[d]
                P.mm(pd[d][0:64, 0:64], AW[d][:, 0:64], X_[:, 4, :], r=[AW[d], X_], w=[pd[d]])
                P.mm(pd[d][0:64, 64:128], X_[:, 4, :], AW[d][:, 64:128], start=True, stop=False, r=[AW[d], X_], w=[pd[d]])
                P.mm(pd[d][0:64, 64:128], X_[:, 5, :], TM[d][:, 5, :], start=False, stop=True, r=[X_, TM[d]], w=[pd[d]])
                P.mm(pd[d][0:64, 128:256], AW[d][:, 0:64], MB[d][:, 128:256], r=[AW[d], MB[d]], w=[pd[d]])
                P.mm(pb[d][:, 256:320], MB[d][:, 128:256], AW[d][:, 64:128], start=True, stop=False, r=[MB[d], AW[d]], w=[pb[d]])
                P.mm(pb[d][:, 256:320], MK[d][:, 128:256], TM[d][:, 5, :], start=False, stop=True, r=[MK[d], TM[d]], w=[pb[d]])
            for d in LN:
                P.stt(Pm[d][:], ident[0:64, 0:64], gC[d][:, 0:1], pd[d][0:64, 0:64], ALU.mult, ALU.add, r=[cst, gC[d], pd[d]], w=[Pm[d]])
                P.copy(QT[d][:], pd[d][0:64, 64:128], eng='vector', r=[pd[d]], w=[QT[d]])
                P.tt(Rp[d][:], pd[d][0:64, 128:256], FM[d][:, 3, :], ALU.add, r=[pd[d], FM[d]], w=[Rp[d]])
                P.copy(yl[d][:], pb[d][:, 256:320], eng='vector', r=[pb[d]], w=[yl[d]])
            if stop < 7:
                continue
            for d in LN:
                P.mm(pc[d][:, 320:384], Rp[d][:], ST[d][:], r=[Rp[d], ST[d]], w=[pc[d]])
                P.mm(pc[d][0:64, 384:448], Pm[d][:], ST[d][:], r=[Pm[d], ST[d]], w=[pc[d]])
            for d in LN:
                P.tt(Yd[d][:, g, 0:64], pc[d][:, 320:384], yl[d][:], ALU.add, r=[pc[d], yl[d]], w=[(Yd[d], g)])
                P.tt(ST[d][:], pc[d][0:64, 384:448], QT[d][:], ALU.add, r=[pc[d], QT[d]], w=[ST[d]])
    tot = P.sb('tot', [128, 64], F32)
    xc = P.sb('xc', [128, 64], F32)
    bf_ = P.sb('bf_', [128, 1], F32)
    yt = [P.sb('yt%d' % i, [128, 64], BF16) for i in range(2)]
    f_ = fs[0]
    for g in range(NTL):
        gm = (1 - g) if g < 2 else (NTL + 1 - g)
        p2 = pb[g % 2]; vg_ = vg[g % 2]; y_ = yt[g % 2]
        P.dma(vg_[:], vg_s[g * 128:(g + 1) * 128, :], q='sync', r=['vg_s'], w=[vg_])
        P.mm(p2[:, 0:65], J, Yd[1][:, gm, :], r=[cst, (Yd[1], gm)], w=[p2])
        P.tt(tot[:], Yd[0][:, g, 0:64], p2[:, 0:64], ALU.add, r=[(Yd[0], g), p2], w=[tot])
        P.tt(bf_[:], Yd[0][:, g, 64:65], p2[:, 64:65], ALU.add, r=[(Yd[0], g), p2], w=[bf_])
        P.reduce(f_[:, 0:1], tot[:], r=[tot], w=[f_])
        P.ts(f_[:, 1:2], f_[:, 0:1], -1.0 / 64, None, ALU.mult, r=[f_], w=[f_])
        P.ts(xc[:], tot[:], f_[:, 1:2], None, ALU.add, r=[tot, f_], w=[xc])
        P.act(tot[:], xc[:], AF.Square, accum=f_[:, 2:3], r=[xc], w=[tot, f_])
        P.act(f_[:, 3:4], f_[:, 2:3], AF.Sqrt, scale=1.0 / 64, bias=64e-5, r=[f_], w=[f_])
        P.recip(f_[:, 4:5], f_[:, 3:4], r=[f_], w=[f_])
        P.stt(xc[:], xc[:], f_[:, 4:5], aux[:, 448:512], ALU.mult, ALU.mult, r=[xc, f_, aux], w=[xc])
        P.tt(xc[:], xc[:], aux[:, 512:576], ALU.add, r=[xc, aux], w=[xc])
        P.stt(xc[:], vg_[:, 0:64], bf_[:, 0:1], xc[:], ALU.mult, ALU.add, r=[vg_, bf_, xc], w=[xc])
        P.tt(y_[:], xc[:], vg_[:, 64:128], ALU.mult, r=[xc, vg_], w=[y_])
        P.dma(y_d[g * 128:(g + 1) * 128, :], y_[:], q='sync', out_dram=True, w=['y_d'])
    return P.emit()


import numpy as np
import ml_dtypes
BF = ml_dtypes.bfloat16

A_OFF = 0; B_OFF = 768; C_OFF = 1920; D_OFF = 2688; G_OFF = 3200


def rope_tab(pos, nfreq):
    inv = (np.float32(10000.0) ** (-(np.arange(nfreq, dtype=np.float32) / np.float32(nfreq)))).astype(np.float32)
    ang = (pos.astype(np.float32)[:, None] * inv[None, :]).astype(np.float32)
    c = np.cos(ang).astype(np.float32); s = np.sin(ang).astype(np.float32)
    return np.concatenate([c, c], -1), np.concatenate([-s, s], -1)


def axial_tab(S, dim):
    rows = S // 64
    row = np.repeat(np.arange(rows), 64); col = np.tile(np.arange(64), rows)
    cr, sr = rope_tab(row, dim // 4); cc, sc = rope_tab(col, dim // 4)
    return np.concatenate([cr, cc], -1), np.concatenate([sr, sc], -1)


def half_perm(n):
    return np.concatenate([np.arange(n // 2, n), np.arange(0, n // 2)])


def with_ctx(tab, fill):
    S, d = tab.shape
    out = np.full((d, 256 + S), fill, np.float32)
    out[:, 256:] = tab.T
    return np.ascontiguousarray(out)


def attn_inputs(kind, S, h, w_in_l, p, l):
    if kind == 'A':
        qc = A_OFF + h * 64 + np.arange(64)
        kc = A_OFF + 256 + h * 64 + np.arange(64)
        vc = A_OFF + 512 + h * 64 + np.arange(64)
        perm = np.concatenate([g * 16 + half_perm(16) for g in range(4)])
        c1, s1 = axial_tab(S, 32)
        cosT = with_ctx(np.concatenate([c1, c1], -1), 1.0); sinT = with_ctx(np.concatenate([s1, s1], -1), 0.0)
    else:
        qc = D_OFF + h * 64 + np.arange(64)
        kc = D_OFF + 256 + (h // 2) * 64 + np.arange(64)
        vc = D_OFF + 384 + (h // 2) * 64 + np.arange(64)
        perm = np.concatenate([g * 32 + half_perm(32) for g in range(2)])
        c1, s1 = axial_tab(S, 64)
        cosT = with_ctx(c1, 1.0); sinT = with_ctx(s1, 0.0)
    w = np.concatenate([w_in_l[:, qc], w_in_l[:, qc[perm]], w_in_l[:, kc], w_in_l[:, kc[perm]], w_in_l[:, vc]], 1)
    aux = np.zeros((128, 256), np.float32)
    if kind == 'A':
        aux[:, 0:64] = p['diff_lam_q'].reshape(1, 64)
        aux[:, 64:128] = p['diff_lam_k'].reshape(1, 64)
        aux[:, 128:192] = p['diff_subln'][h * 64:(h + 1) * 64][None, :]
        lam_init = 0.8 - 0.6 * np.exp(-0.3 * l)
        aux[:, 192] = lam_init
        aux[:, 193] = 1.0 - lam_init
    else:
        aux[:, 194] = p['win_sink'][h]
    d = dict(w=np.ascontiguousarray(w), cosT=cosT, sinT=sinT, ident=np.eye(128, dtype=np.float32), aux=aux)
    if kind == 'D':
        j = np.arange(128)[:, None]; i = np.arange(128)[None, :]
        d['masks'] = np.ascontiguousarray(np.stack([(i <= j), (j <= i)], 1).astype(np.float32))
    return d


def ret_inputs(S, h, w_in_l, p):
    qc = C_OFF + h * 32 + np.arange(32)
    kc = C_OFF + 128 + h * 32 + np.arange(32)
    vc = C_OFF + 256 + h * 64 + np.arange(64)
    gc = C_OFF + 512 + h * 64 + np.arange(64)
    perm = half_perm(32)
    w = np.concatenate([w_in_l[:, qc], w_in_l[:, qc[perm]], w_in_l[:, kc], w_in_l[:, kc[perm]], w_in_l[:, vc], w_in_l[:, gc]], 1)
    c1, s1 = rope_tab(np.arange(S), 16)
    sc = np.float32(32 ** -0.5)
    tabs = np.stack([with_ctx(c1, 1.0), with_ctx(s1, 0.0), with_ctx(c1 * sc, sc), with_ctx(s1 * sc, 0.0)], 0)
    j = np.arange(128)[:, None].astype(np.float32); i = np.arange(128)[None, :].astype(np.float32)
    cst = np.zeros((128, 770), np.float32)
    cst[:, 0:128] = np.maximum(i - j, 0); cst[:, 128:256] = (i >= j)
    cst[:, 256:384] = np.maximum(j - i, 0); cst[:, 384:512] = (j >= i)
    cst[:, 512:640] = i + 1; cst[:, 640:768] = 128 - i
    cst[:, 768] = 127 - j[:, 0]; cst[:, 769] = j[:, 0]
    aux = np.zeros((128, 128), np.float32)
    aux[:, 0] = p['ret_decay'][0, h]; aux[:, 1] = p['ret_decay'][1, h]
    aux[:, 64:128] = p['ret_gn'][h * 64:(h + 1) * 64][None, :]
    return dict(w=np.ascontiguousarray(w), tabs=np.ascontiguousarray(tabs), identb=np.eye(128).astype(BF), cst=cst, aux=aux)


def rwkv_inputs(h, w_in_l, p):
    hs = slice(h * 64, (h + 1) * 64)
    cols = np.concatenate([B_OFF + h * 64 + np.arange(64), B_OFF + 256 + h * 64 + np.arange(64), B_OFF + 512 + h * 64 + np.arange(64),
                           B_OFF + 768 + np.arange(64), B_OFF + 896 + np.arange(64),
                           B_OFF + 832 + np.arange(64), B_OFF + 960 + np.arange(64),
                           B_OFF + 1024 + np.arange(128)])
    w = np.ascontiguousarray(w_in_l[:, cols])
    mu = np.ascontiguousarray(np.broadcast_to(p['rwkv_mu'][cols - B_OFF][None, :], (128, 576))).astype(np.float32)
    aux = np.zeros((128, 576), np.float32)
    for i, v in enumerate([p['rwkv_w0'][0][hs], p['rwkv_w0'][1][hs], p['rwkv_a0'][0][hs], p['rwkv_a0'][1][hs], p['rwkv_kk'][hs],
                           p['rwkv_ka'][hs], p['rwkv_rk'][h], p['rwkv_lnx_g'][hs], p['rwkv_lnx_b'][hs]]):
        aux[:, i * 64:(i + 1) * 64] = v[None, :]
    wa = np.zeros((128, 2, 64), np.float32)
    for d in range(2):
        wa[0:64, d, :] = p['rwkv_w2'][d][:, hs]
        wa[64:128, d, :] = p['rwkv_a2'][d][:, hs]
    cst = np.zeros((128, 384), np.float32)
    cst[:, 0:128] = np.eye(128); cst[:, 128:256] = np.eye(128)[::-1]
    cst[0, 256:320] = 1.0; cst[1, 320:384] = 1.0
    return dict(w=w, mu=mu, aux=aux, w2a2=wa, g2h=np.ascontiguousarray(p['rwkv_g2'][:, hs]), cst=cst)


def flip_seq(hT):
    return np.ascontiguousarray(np.concatenate([hT[:, :256][:, ::-1], hT[:, 256:][:, ::-1]], 1))


def shift_pn(hT):
    P_ = np.zeros_like(hT); N_ = np.zeros_like(hT)
    for (a, b) in ((0, 256), (256, hT.shape[1])):
        P_[:, a + 1:b] = hT[:, a:b - 1]
        N_[:, a:b - 1] = hT[:, a + 1:b]
    return P_, N_


def rwkv2_cst():
    c = np.zeros((128, 896), np.float32)
    i = np.arange(128)[:, None]; t = np.arange(128)[None, :]
    c[:, 0:128] = np.eye(128); c[:, 128:256] = np.eye(128)[::-1]
    c[:, 256:384] = (i <= t); c[:, 384:512] = (i < t); c[:, 512:640] = (i <= t); c[:, 640:768] = (i > t); c[:, 768:896] = 1.0
    full = np.zeros((128, 2688), np.float32)
    full[:, :896] = c
    for lv in range(7):
        sz = 1 << lv
        m = ((i // (2 * sz)) == (t // (2 * sz))) & ((i // sz) % 2 == 0) & ((t // sz) % 2 == 1)
        full[:, 896 + 256 * lv:896 + 256 * lv + 128] = m
        full[:, 896 + 256 * lv + 128:896 + 256 * lv + 256] = m.T
    return full


def pad_seq(hT):
    L = hT.shape[1]
    out = np.zeros((hT.shape[0], L + 4), hT.dtype)
    out[:, 1:257] = hT[:, :256]
    out[:, 259:259 + L - 256] = hT[:, 256:]
    return out


_PROGS = {}
_DEBUG_B = False


def _prog(name, fn, *a):
    key = (name,) + a
    if key not in _PROGS:
        _PROGS[key] = fn(*a)
    return _PROGS[key]


def _run(nc, maps):
    res = run_bass_kernel_spmd(nc, maps, core_ids=list(range(8)))
    return res.results


def _colT(v, n):
    return np.ascontiguousarray(np.asarray(v, np.float32).reshape(n, 128).T)


def kernel(**inp):
    inp = {k: np.asarray(v) for k, v in inp.items()}
    x = inp['x'].astype(np.float32).copy()
    cx = inp['ctx'].astype(np.float32).copy()
    B, S, _ = x.shape
    L = 256 + S
    NTl = S // 512
    NT = NTl + 1
    c, c_ctx = inp['c'], inp['c_ctx']
    identb = np.eye(128).astype(BF)
    identf = np.eye(128, dtype=np.float32)
    cTs = [np.ascontiguousarray(np.stack([c[b], c_ctx], -1).reshape(8, 128, 2).transpose(1, 0, 2)).astype(np.float32) for b in range(B)]
    pnames = ['w_in', 'diff_lam_q', 'diff_lam_k', 'diff_subln', 'rwkv_mu', 'rwkv_w0', 'rwkv_w2', 'rwkv_a0', 'rwkv_a2', 'rwkv_g2',
              'rwkv_kk', 'rwkv_ka', 'rwkv_rk', 'rwkv_lnx_g', 'rwkv_lnx_b', 'ret_decay', 'ret_gn', 'win_sink', 'w_branch', 'w_out']
    depth = inp['w_in'].shape[0]

    def tok_of(core):
        b, j = core // 4, core % 4
        return b, j, slice(j * NTl * 128, (j + 1) * NTl * 128), slice((j % 2) * 128, (j % 2 + 1) * 128)

    def x_tok():
        out = []
        for core in range(8):
            b, j, ls, cs_ = tok_of(core)
            out.append(np.ascontiguousarray(np.concatenate([x[b, ls], cx[b, cs_]], 0)))
        return out

    def scatter_x(res, key):
        for core in range(8):
            b, j, ls, cs_ = tok_of(core)
            xo = np.asarray(res[core][key])
            x[b, ls] = xo[:NTl * 128]
            if j < 2:
                cx[b, cs_] = xo[NTl * 128:]

    for l in range(depth):
        p = {n: inp[n][l] for n in pnames}
        aw, ab = inp['ada_w'][l], inp['ada_b'][l]
        nc = _prog('T0', build_T0, NT)
        xt = x_tok()
        maps = [dict(x=xt[core], adaw=np.ascontiguousarray(aw[:, 0:2048]), adab=_colT(ab[0:2048], 16), cT=cTs[core // 4],
                     gT=_colT(inp['norm_pre_mix'][l], 8), ident=identb) for core in range(8)]
        res = _run(nc, maps)
        hT_own = [np.asarray(res[core]['hT']) for core in range(8)]
        hT_b = []
        for b in range(B):
            parts = [hT_own[b * 4 + 0][:, NTl * 128:], hT_own[b * 4 + 1][:, NTl * 128:]] + [hT_own[b * 4 + j][:, :NTl * 128] for j in range(4)]
            hT_b.append(np.ascontiguousarray(np.concatenate(parts, 1)))
        Y = [np.zeros((L, 4, 4, 64), BF) for _ in range(B)]
        for n, kind in enumerate(['A', 'B', 'C', 'D']):
            maps = []
            for core in range(8):
                b, h = core // 4, core % 4
                if kind in ('A', 'D'):
                    d = attn_inputs(kind, S, h, p['w_in'], p, l)
                    d['hT'] = hT_b[b]
                elif kind == 'C':
                    d = ret_inputs(S, h, p['w_in'], p)
                    d['hT'] = hT_b[b]
                else:
                    d = rwkv_inputs(h, p['w_in'], p)
                    d.pop('cst'); d['cst2'] = rwkv2_cst()
                    d['hTf'] = pad_seq(hT_b[b])
                    d['hTb'] = pad_seq(flip_seq(hT_b[b]))
                maps.append(d)
            if kind in ('A', 'D'):
                nc = _prog('attn' + kind, build_attn, L, kind)
            elif kind == 'C':
                nc = _prog('ret', build_ret, L)
            else:
                nc = _prog('rwkv2', build_rwkv2, L)
            res = _run(nc, maps)
            if kind == 'B' and _DEBUG_B:
                maps2 = []
                for core in range(8):
                    b, h = core // 4, core % 4
                    d = rwkv_inputs(h, p['w_in'], p)
                    d['hTf'] = hT_b[b]; d['hTb'] = flip_seq(hT_b[b])
                    for nm in ('f', 'b'):
                        d['hP' + nm], d['hN' + nm] = shift_pn(d['hT' + nm])
                    maps2.append(d)
                res2 = _run(_prog('rwkv', build_rwkv, L), maps2)
                for core in range(8):
                    a_ = np.asarray(res[core]['y']).astype(np.float32); b_ = np.asarray(res2[core]['y']).astype(np.float32)
                    print('DBG layer', l, 'core', core, 'rel', np.linalg.norm(a_ - b_) / np.linalg.norm(b_), 'nan', np.isnan(a_).sum(), flush=True)
            for core in range(8):
                b, h = core // 4, core % 4
                Y[b][:, n, h, :] = np.asarray(res[core]['y'])
        nc = _prog('T1', build_T1, NT)
        xt = x_tok()
        maps = []
        for core in range(8):
            b, j, ls, cs_ = tok_of(core)
            Yb = Y[b].reshape(L, 1024)
            yT = np.ascontiguousarray(np.concatenate([Yb[256:][ls], Yb[:256][cs_]], 0).T)
            maps.append(dict(x=xt[core], hT=hT_own[core], yT=yT, wg=np.ascontiguousarray(p['w_in'][:, G_OFF:]),
                             wb=np.ascontiguousarray(p['w_branch'].reshape(1024, 1024)), wo=p['w_out'],
                             adaw=np.ascontiguousarray(aw[:, 2048:3072]), adab=_colT(ab[2048:3072], 8), cT=cTs[b],
                             gbc=np.ascontiguousarray(np.broadcast_to(inp['norm_post_mix'][l][None, :], (128, 1024))).astype(np.float32),
                             identf=identf))
        res = _run(nc, maps)
        scatter_x(res, 'xo')
        nc = _prog('T2', build_T2, NT)
        xt = x_tok()
        maps = []
        for core in range(8):
            b = core // 4
            maps.append(dict(x=xt[core], wu=inp['w_up'][l], wd=inp['w_down'][l], adaw=np.ascontiguousarray(aw[:, 3072:6144]),
                             adab=_colT(ab[3072:6144], 24), cT=cTs[b], gT=_colT(inp['norm_pre_mlp'][l], 8),
                             gbc=np.ascontiguousarray(np.broadcast_to(inp['norm_post_mlp'][l][None, :], (128, 1024))).astype(np.float32),
                             identf=identf, ident=identb))
        res = _run(nc, maps)
        scatter_x(res, 'xo')
    return x.astype(np.float32)
```
